# Optimizing a Trainium2 kernel written in Bass

```python
import jax, jax.numpy as jnp
from jax import lax
import numpy as np

D_MODEL = 2048
BATCH = 4
SEQ = 2048
DEPTH = 1
DEC_BATCH = 128
DEC_SEQ = 1
PAST_LEN = 16384
PAGE_SIZE = 128

GLA_HEADS = 4
GLA_DK = D_MODEL // 2
GLA_DV = D_MODEL
GLA_HK = GLA_DK // GLA_HEADS
GLA_HV = GLA_DV // GLA_HEADS
GLA_RANK = 16
GLA_GATE_NORM = 16.0
GLA_CHUNK = 64
CONV_DIM = D_MODEL // 2
CONV_K = 3
N_MEM = 256
MEM_HEADS = 4
MEM_DIM = D_MODEL // 2
MEM_HD = MEM_DIM // MEM_HEADS
D_FF = ((8 * D_MODEL // 3) + 127) // 128 * 128
FFN_CONV_K = 3
DN_ALPHA = (2 * DEPTH) ** 0.25
DN_BETA = (8 * DEPTH) ** -0.25
LN_EPS = 1e-5
RMS_EPS = 1e-6
SPLITS = (GLA_DK, GLA_DK, GLA_DV, GLA_DV, GLA_RANK, CONV_DIM, CONV_DIM, CONV_DIM, MEM_DIM, D_MODEL, D_MODEL, D_MODEL)
IN_COLS = sum(SPLITS)
SPLIT_POINTS = tuple(int(s) for s in np.cumsum(SPLITS)[:-1])

kernel_name = 'hybrid_gla_shortconv_memxattn_deepnorm_step'


def _layer_norm(x, g, b):
    xf = x.astype(jnp.float32)
    mu = jnp.mean(xf, -1, keepdims=True)
    var = jnp.mean(jnp.square(xf - mu), -1, keepdims=True)
    y = (xf - mu) * lax.rsqrt(var + LN_EPS)
    return (y * g.astype(jnp.float32) + b.astype(jnp.float32)).astype(x.dtype)


def _causal_dwconv(u, buf, w, bias=None):
    K = w.shape[0]
    L = u.shape[1]
    z = jnp.concatenate([buf.astype(u.dtype), u], axis=1)
    y = z[:, 0:L] * w[0]
    for j in range(1, K):
        y = y + z[:, j:j + L] * w[j]
    if bias is not None:
        y = y + bias
    return y, z[:, L:]


def _gla_recurrence(q, k, v, log_a, s0):
    B, L, H, _ = q.shape
    c = min(GLA_CHUNK, L)
    n = -(-L // c)
    pad = n * c - L
    f32 = jnp.float32

    def chunks(t):
        t = jnp.pad(t.astype(f32), ((0, 0), (0, pad), (0, 0), (0, 0)))
        return t.reshape(B, n, c, H, t.shape[-1]).transpose(1, 0, 3, 2, 4)

    tri = jnp.tril(jnp.ones((c, c), dtype=bool))[:, :, None]

    def step(S, inp):
        qc, kc, vc, ac = inp
        bc = jnp.cumsum(ac, axis=2)
        diff = bc[:, :, :, None, :] - bc[:, :, None, :, :]
        decay = jnp.exp(jnp.where(tri, diff, -jnp.inf))
        scores = jnp.einsum('bhtk,bhsk,bhtsk->bhts', qc, kc, decay)
        o = (jnp.einsum('bhts,bhsv->bhtv', scores, vc)
             + jnp.einsum('bhtk,bhkv->bhtv', qc * jnp.exp(bc), S))
        b_last = bc[:, :, -1]
        S = (jnp.exp(b_last)[..., None] * S
             + jnp.einsum('bhsk,bhsv->bhkv', kc * jnp.exp(b_last[:, :, None] - bc), vc))
        return S, o

    s_fin, o = lax.scan(step, s0.astype(f32), (chunks(q), chunks(k), chunks(v), chunks(log_a)))
    o = o.transpose(1, 0, 3, 2, 4).reshape(B, n * c, H, -1)[:, :L]
    return o, s_fin


def _token_mixing(x, mem_k, mem_v, s_gla, s_conv, w_in, w_gla_a2, b_gla_a2, g_gla_norm, w_gla_out,
                  w_conv, w_conv_out, w_mem_out, w_o):
    B, L, _ = x.shape
    f32 = jnp.float32
    proj = jnp.einsum('bld,de->ble', x, w_in)
    q, k, v, g, a_lr, cb, cc, ch, mq, z_a, z_b, z_m = jnp.split(proj, SPLIT_POINTS, axis=-1)
    log_a = jax.nn.log_sigmoid((a_lr @ w_gla_a2 + b_gla_a2).astype(f32)) / GLA_GATE_NORM
    qh = q.reshape(B, L, GLA_HEADS, GLA_HK) * (GLA_HK ** -0.5)
    kh = k.reshape(B, L, GLA_HEADS, GLA_HK)
    vh = v.reshape(B, L, GLA_HEADS, GLA_HV)
    o, s_gla_new = _gla_recurrence(qh, kh, vh, log_a.reshape(B, L, GLA_HEADS, GLA_HK), s_gla)
    o = o * lax.rsqrt(jnp.mean(jnp.square(o), -1, keepdims=True) + RMS_EPS) * g_gla_norm.astype(f32)
    o = o.reshape(B, L, GLA_DV).astype(x.dtype) * jax.nn.silu(g)
    y_a = o @ w_gla_out
    u, s_conv_new = _causal_dwconv(cc * ch, s_conv, w_conv)
    y_b = (cb * u) @ w_conv_out
    qm = mq.reshape(B, L, MEM_HEADS, MEM_HD)
    logits = jnp.einsum('blhd,bmhd->bhlm', qm, mem_k).astype(f32) * (MEM_HD ** -0.5)
    p = jax.nn.softmax(logits, axis=-1).astype(x.dtype)
    om = jnp.einsum('bhlm,bmhd->blhd', p, mem_v).reshape(B, L, MEM_DIM).astype(x.dtype)
    y_m = om @ w_mem_out
    merged = jax.nn.sigmoid(z_a) * y_a + jax.nn.sigmoid(z_b) * y_b + jax.nn.sigmoid(z_m) * y_m
    return merged @ w_o, s_gla_new.astype(s_gla.dtype), s_conv_new


def _conv_ffn(x, s_ffn, w_ffn_gate, w_ffn_up, w_ffn_conv, b_ffn_conv, w_ffn_down):
    hg = x @ w_ffn_gate
    hu = x @ w_ffn_up
    hc, s_new = _causal_dwconv(hg, s_ffn, w_ffn_conv, b_ffn_conv)
    return (jax.nn.silu(hc) * hu) @ w_ffn_down, s_new


def _layer(x, mem_k, mem_v, s_gla, s_conv, s_ffn, lw):
    (w_in, w_gla_a2, b_gla_a2, g_gla_norm, w_gla_out, w_conv, w_conv_out, w_mem_out, w_o,
     ln1_g, ln1_b, w_ffn_gate, w_ffn_up, w_ffn_conv, b_ffn_conv, w_ffn_down, ln2_g, ln2_b) = lw
    m, s_gla_new, s_conv_new = _token_mixing(x, mem_k, mem_v, s_gla, s_conv, w_in, w_gla_a2, b_gla_a2,
                                             g_gla_norm, w_gla_out, w_conv, w_conv_out, w_mem_out, w_o)
    x = _layer_norm(DN_ALPHA * x + m, ln1_g, ln1_b)
    f, s_ffn_new = _conv_ffn(x, s_ffn, w_ffn_gate, w_ffn_up, w_ffn_conv, b_ffn_conv, w_ffn_down)
    x = _layer_norm(DN_ALPHA * x + f, ln2_g, ln2_b)
    return x, s_gla_new, s_conv_new, s_ffn_new


def setup_inputs(seed: int = 0) -> dict:
    key = jax.random.key(seed)
    ks = jax.random.split(key, 32)
    f32 = jnp.float32

    def nrm(k, shape, scale):
        return jax.random.normal(k, shape, f32) * scale

    D = D_MODEL
    return {
        'x_prompt': nrm(ks[0], (BATCH, SEQ, D), 1.0),
        'x_sample': nrm(ks[1], (DEC_BATCH, DEC_SEQ, D), 1.0),
        'mem_prompt': nrm(ks[2], (BATCH, N_MEM, D), 1.0),
        'cache_mem_k': nrm(ks[3], (DEPTH, DEC_BATCH, N_MEM, MEM_HEADS, MEM_HD), 1.0),
        'cache_mem_v': nrm(ks[4], (DEPTH, DEC_BATCH, N_MEM, MEM_HEADS, MEM_HD), 1.0),
        'state_gla': nrm(ks[5], (DEPTH, DEC_BATCH, GLA_HEADS, GLA_HK, GLA_HV), 0.5),
        'state_conv': nrm(ks[6], (DEPTH, DEC_BATCH, CONV_K - 1, CONV_DIM), 1.0),
        'state_ffn_conv': nrm(ks[7], (DEPTH, DEC_BATCH, FFN_CONV_K - 1, D_FF), 1.0),
        'w_in': nrm(ks[8], (DEPTH, D, IN_COLS), D ** -0.5),
        'w_gla_a2': nrm(ks[9], (DEPTH, GLA_RANK, GLA_DK), GLA_RANK ** -0.5),
        'b_gla_a2': nrm(ks[10], (DEPTH, GLA_DK), 0.1),
        'g_gla_norm': 1.0 + nrm(ks[11], (DEPTH, GLA_HV), 0.02),
        'w_gla_out': nrm(ks[12], (DEPTH, GLA_DV, D), GLA_DV ** -0.5),
        'w_conv': nrm(ks[13], (DEPTH, CONV_K, CONV_DIM), CONV_K ** -0.5),
        'w_conv_out': nrm(ks[14], (DEPTH, CONV_DIM, D), CONV_DIM ** -0.5),
        'w_mem_k': nrm(ks[15], (DEPTH, D, MEM_DIM), D ** -0.5),
        'w_mem_v': nrm(ks[16], (DEPTH, D, MEM_DIM), D ** -0.5),
        'w_mem_out': nrm(ks[17], (DEPTH, MEM_DIM, D), MEM_DIM ** -0.5),
        'w_o': nrm(ks[18], (DEPTH, D, D), DN_BETA * D ** -0.5),
        'ln1_g': 1.0 + nrm(ks[19], (DEPTH, D), 0.02),
        'ln1_b': nrm(ks[20], (DEPTH, D), 0.02),
        'w_ffn_gate': nrm(ks[21], (DEPTH, D, D_FF), D ** -0.5),
        'w_ffn_up': nrm(ks[22], (DEPTH, D, D_FF), D ** -0.5),
        'w_ffn_conv': nrm(ks[23], (DEPTH, FFN_CONV_K, D_FF), FFN_CONV_K ** -0.5),
        'b_ffn_conv': nrm(ks[24], (DEPTH, D_FF), 0.02),
        'w_ffn_down': nrm(ks[25], (DEPTH, D_FF, D), DN_BETA * D_FF ** -0.5),
        'ln2_g': 1.0 + nrm(ks[26], (DEPTH, D), 0.02),
        'ln2_b': nrm(ks[27], (DEPTH, D), 0.02),
    }


def reference(x_prompt, x_sample, mem_prompt, cache_mem_k, cache_mem_v, state_gla, state_conv, state_ffn_conv,
              w_in, w_gla_a2, b_gla_a2, g_gla_norm, w_gla_out, w_conv, w_conv_out, w_mem_k, w_mem_v, w_mem_out,
              w_o, ln1_g, ln1_b, w_ffn_gate, w_ffn_up, w_ffn_conv, b_ffn_conv, w_ffn_down, ln2_g, ln2_b):
    yp = x_prompt
    ys = x_sample
    bp = x_prompt.shape[0]
    dt = x_prompt.dtype
    p_mk, p_mv, p_gla, p_conv, p_ffn = [], [], [], [], []
    s_gla, s_conv, s_ffn = [], [], []
    for l in range(DEPTH):
        lw = (w_in[l], w_gla_a2[l], b_gla_a2[l], g_gla_norm[l], w_gla_out[l], w_conv[l], w_conv_out[l],
              w_mem_out[l], w_o[l], ln1_g[l], ln1_b[l], w_ffn_gate[l], w_ffn_up[l], w_ffn_conv[l],
              b_ffn_conv[l], w_ffn_down[l], ln2_g[l], ln2_b[l])
        mk = jnp.einsum('bmd,de->bme', mem_prompt, w_mem_k[l]).reshape(bp, N_MEM, MEM_HEADS, MEM_HD)
        mv = jnp.einsum('bmd,de->bme', mem_prompt, w_mem_v[l]).reshape(bp, N_MEM, MEM_HEADS, MEM_HD)
        yp, g_new, c_new, f_new = _layer(
            yp, mk, mv,
            jnp.zeros((bp, GLA_HEADS, GLA_HK, GLA_HV), dt),
            jnp.zeros((bp, CONV_K - 1, CONV_DIM), dt),
            jnp.zeros((bp, FFN_CONV_K - 1, D_FF), dt), lw)
        p_mk.append(mk)
        p_mv.append(mv)
        p_gla.append(g_new)
        p_conv.append(c_new)
        p_ffn.append(f_new)
        ys, g2, c2, f2 = _layer(ys, cache_mem_k[l], cache_mem_v[l], state_gla[l], state_conv[l],
                                state_ffn_conv[l], lw)
        s_gla.append(g2)
        s_conv.append(c2)
        s_ffn.append(f2)
    return (yp, ys, jnp.stack(p_mk), jnp.stack(p_mv), jnp.stack(p_gla), jnp.stack(p_conv), jnp.stack(p_ffn),
            jnp.stack(s_gla), jnp.stack(s_conv), jnp.stack(s_ffn))
```

```python
import numpy as np
import concourse.bass as bass
import concourse.mybir as mybir
from concourse.bass_utils import run_bass_kernel_spmd
from contextlib import ExitStack

F32 = mybir.dt.float32
BF16 = mybir.dt.bfloat16
AF = mybir.ActivationFunctionType
ALU = mybir.AluOpType
AX = mybir.AxisListType

D = 2048; KC = 16; NT = 1044; NS = 16; H0 = 16; M0 = 20; NP = 1024
DFF = 5504; FC = 43
GROUPS = [(0, 348), (348, 696), (696, 1044)]
PGROUPS = [(0, 512), (512, 1024)]
ALPHA = float(2.0 ** 0.25)
OFF = dict(q=0, k=1024, v=2048, g=4096, a=6144, cb=6160, cc=7184, ch=8208, mq=9232, za=10256, zb=12304, zm=14352)
TILES = [(0, 20)] + [(M0 + 128 * i, 128) for i in range(8)]


class StopBuild(Exception):
    pass


class Prog:
    def __init__(self):
        self.ops = {e: [] for e in ('pe', 'act', 'dve', 'pool', 'sp')}
        self.cnt = {}
        self.res_w = {}
        self.res_r = {}
        self.known = {e: {} for e in self.ops}
        self.rr = {'sp': 0, 'pool': 0, 'act': 0}
        self.nd = {'sp': 8, 'pool': 4, 'act': 4}

    def _deps(self, reads, writes):
        d = {}

        def add(sv):
            if sv is None:
                return
            s, v = sv
            if d.get(s, -1) < v:
                d[s] = v
        for k in reads:
            add(self.res_w.get(k))
        for k in writes:
            add(self.res_w.get(k))
            for s, v in self.res_r.get(k, {}).items():
                add((s, v))
        return d

    def _record(self, stream, val, reads, writes):
        for k in reads:
            self.res_r.setdefault(k, {})[stream] = val
        for k in writes:
            self.res_w[k] = (stream, val)
            self.res_r[k] = {}

    def _waits(self, eng, d):
        waits = []
        for s, v in d.items():
            if s == eng and eng == 'pe':
                continue
            if self.known[eng].get(s, -1) >= v:
                continue
            self.known[eng][s] = v
            waits.append((s, v))
        return waits

    def op(self, eng, fn, reads=(), writes=()):
        d = self._deps(reads, writes)
        waits = self._waits(eng, d)
        val = self.cnt.get(eng, 0) + 1
        self.cnt[eng] = val
        self.ops[eng].append((waits, fn, (eng, 1)))
        self._record(eng, val, reads, writes)

    def dma(self, q, fn, reads=(), writes=()):
        k = self.rr[q]
        self.rr[q] = (k + 1) % self.nd[q]
        stream = 'dma_%s%d' % (q, k)
        d = self._deps(reads, writes)
        prev = self.cnt.get(stream, 0)
        if prev > 0:
            d[stream] = max(d.get(stream, 0), prev)
        waits = self._waits(q, d)
        val = prev + 16
        self.cnt[stream] = val
        self.ops[q].append((waits, fn, (stream, 16)))
        self._record(stream, val, reads, writes)

    def barrier(self):
        snap = dict(self.cnt)
        for e in self.ops:
            waits = self._waits(e, dict(snap))
            if waits:
                self.ops[e].append((waits, None, None))

    def emit(self, nc, es):
        sems = {s: es.enter_context(nc.semaphore(s)) for s in self.cnt}
        block = es.enter_context(nc.Block())

        def run(name, eh):
            for waits, fn, sig in self.ops[name]:
                for s, v in waits:
                    eh.wait_ge(sems[s], v)
                if fn is not None:
                    ins = fn(eh)
                    ins.then_inc(sems[sig[0]], sig[1])

        @block.tensor
        def _(e):
            run('pe', e)

        @block.scalar
        def _(e):
            run('act', e)

        @block.vector
        def _(e):
            run('dve', e)

        @block.gpsimd
        def _(e):
            run('pool', e)

        @block.sync
        def _(e):
            run('sp', e)


def build_program(stop=None):
    holder = {}
    try:
        return _build(stop, holder)
    except StopBuild:
        return holder['finish']()


def _build(stop, holder):
    nc = bass.Bass("TRN2", target_bir_lowering=False)
    P = Prog()
    es = ExitStack()

    def din(name, shape):
        return nc.dram_tensor(name, list(shape), F32, kind="ExternalInput").ap()

    def dout(name, shape):
        return nc.dram_tensor(name, list(shape), F32, kind="ExternalOutput").ap()

    def finish():
        P.barrier()
        with nc.allow_non_contiguous_dma(reason="small strided state rows"):
            P.emit(nc, es)
        es.close()
        nc._prog_counts = dict(P.cnt)
        return nc
    holder['finish'] = finish

    def chk(tag):
        if stop == tag:
            raise StopBuild()

    xT_in = din("xT_in", [D, NT]); xpT_in = din("xpT_in", [D, NP]); x_tok = din("x_tok", [NT, D])
    flag_in = din("flag", [128, 1]); consts_in = din("consts", [128, 512]); ones_in = din("ones_row", [1, NT])
    memT_in = din("memT_in", [D, 256]); kcT_in = din("kcT_in", [NS, 1024, 256]); vc_in = din("vc_in", [NS, 256, 1024])
    sgla_in = din("sgla_in", [NS, 4, 256, 512]); scT_in = din("scT_in", [1024, 2, NS]); sfT_in = din("sfT_in", [DFF, 2, NS])
    w_in = din("w_in", [D, 16400]); wa2b_in = din("wa2b", [17, 1024]); gnorm_in = din("gnorm_b", [128, 512])
    w_gla_out = din("w_gla_out", [D, D]); wconv_in = din("wconv_p", [128, 24]); w_conv_out = din("w_conv_out", [1024, D])
    w_mem_k = din("w_mem_k", [D, 1024]); w_mem_v = din("w_mem_v", [D, 1024]); w_mem_out = din("w_mem_out", [1024, D])
    w_o = din("w_o", [D, D]); ln_in = din("ln_b", [4, 128, D])
    w_gate = din("w_ffn_gate", [D, DFF]); w_up = din("w_ffn_up", [D, DFF]); wfc_in = din("wfc_p", [128, FC * 4])
    w_down = din("w_ffn_down", [DFF, D])

    y_out = dout("y_out", [NT, D]); pmk_out = dout("pmk", [256, 1024]); pmv_out = dout("pmv", [256, 1024])
    pgla_out = dout("pgla", [4, 256, 512]); pconv_out = dout("pconvT", [1024, 2]); pffn_out = dout("pffnT", [DFF, 2])
    sgla_out = dout("sgla_out", [NS, 4, 256, 512]); sconv_out = dout("sconvT", [1024, 2, NS]); sffn_out = dout("sffnT", [DFF, 2, NS])
    x1_scr = nc.dram_tensor("x1_scr", [NT, D], F32, kind="Internal").ap()

    NW = 53200
    big = es.enter_context(nc.sbuf_tensor("big", [128, NW], F32))
    psum = es.enter_context(nc.psum_tensor("ps", [128, 4096], F32))

    class Arena:
        def __init__(self, base, limit):
            self.base = base; self.off = base; self.limit = limit

        def f32(self, n):
            o = self.off; self.off += n
            assert self.off <= self.limit, (self.off, self.limit)
            return big[:, o:o + n]

        def bf(self, n):
            w = (n + 1) // 2
            o = self.off; self.off += w
            assert self.off <= self.limit, (self.off, self.limit)
            return big[:, o:o + w].bitcast(BF16)

        def reset(self):
            self.off = self.base

    def r3(ap, a):
        return ap.rearrange("p (a b) -> p a b", a=a)

    def bank(b, n=512):
        return psum[:, b * 512:b * 512 + n]

    def bankbf(b, nb=1):
        return psum[:, b * 512:(b + nb) * 512].bitcast(BF16)

    PA = Arena(0, 1500)
    cst = PA.f32(512)
    ident = cst[:, 0:128]; maskf = cst[:, 128:256]; trineg = cst[:, 256:384]
    identb = PA.bf(128); maskb = PA.bf(128)
    flag = PA.f32(1); wconv = PA.f32(24); wfc = PA.f32(FC * 4); gnorm = PA.f32(512)
    A1 = Arena(1500, NW)
    xT = r3(A1.bf(KC * NT), KC)
    xpT_or_merged = A1.bf(KC * NT)
    xpT = r3(xpT_or_merged[:, 0:KC * NP], KC)
    mergedT = r3(xpT_or_merged, KC)
    WB_OFF = A1.off
    WB = [A1.bf(KC * 512) for _ in range(2)]
    OG_OFF = A1.off
    ogT = r3(A1.bf(KC * NT), KC)
    BU_OFF = A1.off
    buT = r3(A1.bf(8 * NT), 8)
    omT = r3(A1.bf(8 * NT), 8)
    SCR0 = A1.off
    SC = Arena(BU_OFF, NW)
    a17 = SC.f32(NT); a17p = SC.f32(NP)

    wb_i = [0]

    def load_w(src_list, nk, ncols_total):
        i = wb_i[0]; wb_i[0] ^= 1
        buf = WB[i][:, 0:nk * ncols_total].rearrange("p (k e) -> p k e", k=nk)
        for (src, co) in src_list:
            w = src.shape[1]
            sv = src.rearrange("(k p) e -> p k e", p=128)
            half = (nk + 1) // 2
            for (k0, k1) in ((0, half), (half, nk)):
                P.dma('pool', (lambda q, o=buf[:, k0:k1, co:co + w], s=sv[:, k0:k1, :]: q.dma_start(out=o, in_=s)),
                      writes=[('W', i)])
        return buf, ('W', i)

    fset = [0]

    def mmF(wbuf, wkey, e0, M, rhs_fn, rkeys, nk, groups):
        s = fset[0]; fset[0] ^= 1
        banks = [3 * s + gi for gi in range(len(groups))]
        keys = [('ps', b) for b in banks]

        def fn(pe):
            ins = None
            for kc in range(nk):
                for gi, (c0, c1) in enumerate(groups):
                    ins = pe.matmul(bank(banks[gi])[0:M, 0:c1 - c0], lhsT=wbuf[:, kc, e0:e0 + M], rhs=rhs_fn(kc, c0, c1),
                                    start=(kc == 0), stop=(kc == nk - 1))
            return ins
        P.op('pe', fn, reads=[wkey] + list(rkeys), writes=keys)
        return banks, keys

    def evac(eng, groups, banks, keys, M, out_fn, wkeys, func=None, scale=None, rkeys=()):
        for gi, (c0, c1) in enumerate(groups):
            src = bank(banks[gi])[0:M, 0:c1 - c0]
            dst = out_fn(c0, c1)
            if eng == 'act':
                kw = {}
                if scale is not None:
                    kw['scale'] = scale
                P.op('act', (lambda e, d=dst, s=src, kw=kw: e.activation(out=d, in_=s, func=(func or AF.Copy), **kw)),
                     reads=[keys[gi]] + list(rkeys), writes=wkeys)
            else:
                P.op('dve', (lambda e, d=dst, s=src: e.tensor_copy(out=d, in_=s)), reads=[keys[gi]] + list(rkeys), writes=wkeys)

    P.dma('sp', lambda q: q.dma_start(out=cst, in_=consts_in[:, :]), writes=['cst'])
    P.dma('sp', lambda q: q.dma_start(out=flag, in_=flag_in[:, :]), writes=['small'])
    P.dma('sp', lambda q: q.dma_start(out=wconv, in_=wconv_in[:, :]), writes=['small'])
    P.dma('sp', lambda q: q.dma_start(out=wfc, in_=wfc_in[:, :]), writes=['small'])
    P.dma('sp', lambda q: q.dma_start(out=gnorm, in_=gnorm_in[:, :]), writes=['small'])
    P.dma('sp', lambda q: q.dma_start(out=a17[16:17, :], in_=ones_in[0:1, :]), writes=['a17ones'])
    P.dma('sp', lambda q: q.dma_start(out=a17p[16:17, :], in_=ones_in[0:1, 0:NP]), writes=['a17ones'])
    P.op('dve', lambda e: e.tensor_copy(out=identb, in_=ident), reads=['cst'], writes=['cstb'])
    P.op('dve', lambda e: e.tensor_copy(out=maskb, in_=maskf), reads=['cst'], writes=['cstb'])
    xv = xT_in.rearrange("(k p) t -> p k t", p=128)
    xpv = xpT_in.rearrange("(k p) t -> p k t", p=128)
    for k0 in range(0, KC, 4):
        P.dma('pool', (lambda q, k0=k0: q.dma_start(out=xT[:, k0:k0 + 4, :], in_=xv[:, k0:k0 + 4, :])), writes=['xT'])
    for k0 in range(0, KC, 4):
        P.dma('pool', (lambda q, k0=k0: q.dma_start(out=xpT[:, k0:k0 + 4, :], in_=xpv[:, k0:k0 + 4, :])), writes=['xpT'])

    if stop == 'A0':
        return finish()
    rx = lambda kc, c0, c1: xT[:, kc, c0:c1]
    rxp = lambda kc, c0, c1: xpT[:, kc, c0:c1]

    wa_f = SC.f32(KC * 16)
    wa_buf = r3(SC.bf(KC * 16), KC); wa_key = 'wa'
    P.dma('sp', lambda q: q.dma_start(out=r3(wa_f, KC), in_=w_in[:, OFF['a']:OFF['a'] + 16].rearrange("(k p) e -> p k e", p=128)), writes=['wa_f'])
    P.op('dve', lambda e: e.tensor_copy(out=wa_buf.rearrange("p a b -> p (a b)"), in_=wa_f), reads=['wa_f'], writes=['wa'])
    banks, keys = mmF(wa_buf, wa_key, 0, 16, rx, ['xT'], KC, GROUPS)
    evac('act', GROUPS, banks, keys, 16, lambda c0, c1: a17[0:16, c0:c1], ['a17'])
    banks, keys = mmF(wa_buf, wa_key, 0, 16, rxp, ['xpT'], KC, PGROUPS)
    evac('act', PGROUPS, banks, keys, 16, lambda c0, c1: a17p[0:16, c0:c1], ['a17p'])
    A17K = ['a17', 'a17ones']; A17PK = ['a17p', 'a17ones']

    if stop == 'A1':
        return finish()
    wa2b = SC.f32(256)
    qT = r3(SC.bf(2 * NT), 2); kT = r3(SC.bf(2 * NT), 2); vT = r3(SC.bf(4 * NT), 4); gsT = r3(SC.bf(4 * NT), 4)
    kpT = r3(SC.bf(2 * NP), 2); vpT = r3(SC.bf(4 * NP), 4)
    S = r3(SC.f32(1024), 2); Sb = r3(SC.bf(1024), 2)
    e1 = SC.f32(256); sp_t = SC.f32(256); ek_t = SC.f32(256)
    eqT = r3(SC.f32(256), 2); ekT = r3(SC.f32(256), 2)
    qd = r3(SC.bf(256), 2); kdT = r3(SC.bf(256), 2)
    kd_t = SC.bf(256); v_t = SC.bf(512); scm = SC.bf(128)
    junk = SC.bf(512); on = SC.bf(512); st2 = SC.f32(4)
    aTs = r3(SC.f32(2 * NS), 2); km = [SC.bf(256) for _ in range(2)]
    QG = SC.bf(NS * 2 * NS).rearrange("p (s j c) -> p s j c", s=NS, j=2)
    SS = [S, r3(SC.f32(1024), 2)]
    SSb = [Sb, Sb]

    def gla_post(n, c0, h, ops_ap, okeys):
        ss = st2[0:n, 0:1]; lv = st2[0:n, 1:2]; rstd = st2[0:n, 2:3]
        P.op('act', lambda e: e.activation(out=junk[0:n, :], in_=ops_ap, func=AF.Square, accum_out=ss), reads=okeys, writes=['junk', 'st2a'])
        P.op('act', lambda e: e.activation(out=lv, in_=ss, func=AF.Ln, scale=1.0 / 512.0, bias=1e-6), reads=['st2a'], writes=['st2b'])
        P.op('act', lambda e: e.activation(out=rstd, in_=lv, func=AF.Exp, scale=-0.5), reads=['st2b'], writes=['st2c'])
        P.op('dve', lambda e: e.scalar_tensor_tensor(out=on[0:n, :], in0=ops_ap, scalar=rstd, in1=gnorm[0:n, :], op0=ALU.mult, op1=ALU.mult),
             reads=list(okeys) + ['st2c', 'small'], writes=['on'])
        tv = bankbf(7)[:, 0:512].rearrange("p (a b) -> p a b", a=4)

        def fn(pe):
            ins = None
            for vv in range(4):
                ins = pe.transpose(tv[:, vv, 0:n], on[0:n, vv * 128:(vv + 1) * 128], identb[0:n, 0:n])
            return ins
        P.op('pe', fn, reads=['on', 'cstb'], writes=[('ps', 7)])
        P.op('dve', lambda e: e.tensor_tensor(out=ogT[:, 4 * h:4 * h + 4, c0:c0 + n], in0=tv[:, :, 0:n], in1=gsT[:, :, c0:c0 + n], op=ALU.mult),
             reads=[('ps', 7), 'gsT'], writes=['ogT'])

    def gla_chunk(h, n, c0, kTs, vTs, a_src, akeys, state_only):
        wcols = wa2b[0:17, 0:256]
        P.op('pe', lambda pe: pe.matmul(bank(0)[0:n, 0:256], lhsT=a_src[0:17, c0:c0 + n], rhs=wcols, start=True, stop=True),
             reads=list(akeys) + ['wa2b'], writes=[('ps', 0)])
        P.op('act', lambda e: e.activation(out=e1[0:n, :], in_=bank(0)[0:n, 0:256], func=AF.Exp, scale=-1.0), reads=[('ps', 0)], writes=['e1'])
        P.op('act', lambda e: e.activation(out=sp_t[0:n, :], in_=e1[0:n, :], func=AF.Ln, bias=1.0), reads=['e1'], writes=['sp'])
        chk('g1')
        bct = psum[:, 512:1024].rearrange("p (a b) -> p a b", a=2)

        def fnb(pe):
            pe.matmul(bank(0)[0:n, 256:512], lhsT=trineg[0:n, 0:n], rhs=sp_t[0:n, :], start=True, stop=True)
            ins = None
            for jj in range(2):
                ins = pe.matmul(bct[:, jj, 0:n], lhsT=sp_t[0:n, jj * 128:(jj + 1) * 128], rhs=trineg[0:n, 0:n], start=True, stop=True)
            return ins
        P.op('pe', fnb, reads=['sp', 'cst', 'e1'], writes=[('ps', 0), ('ps', 1)])
        chk('g2')
        P.op('act', lambda e: e.activation(out=ek_t[0:n, :], in_=bank(0)[0:n, 256:512], func=AF.Exp, scale=-1.0), reads=[('ps', 0)], writes=['ek_t'])
        P.op('act', lambda e: e.activation(out=eqT[:, :, 0:n], in_=bct[:, :, 0:n], func=AF.Exp), reads=[('ps', 1)], writes=['eqT'])
        if not state_only:
            P.op('act', lambda e: e.activation(out=ekT[:, :, 0:n], in_=bct[:, :, 0:n], func=AF.Exp, scale=-1.0), reads=[('ps', 1)], writes=['ekT'])
        chk('g3')
        ktv = bankbf(2)[:, 0:256]
        vtv = bankbf(2)[:, 256:768]

        def fnt(pe):
            ins = None
            for jj in range(2):
                ins = pe.transpose(ktv[0:n, jj * 128:(jj + 1) * 128], kTs[:, jj, c0:c0 + n], identb)
            for vv in range(4):
                ins = pe.transpose(vtv[0:n, vv * 128:(vv + 1) * 128], vTs[:, vv, c0:c0 + n], identb)
            return ins
        P.op('pe', fnt, reads=['kvT', 'cstb'], writes=[('ps', 2)])
        chk('g4')
        P.op('dve', lambda e: e.tensor_tensor(out=kd_t[0:n, :], in0=ktv[0:n, :], in1=ek_t[0:n, :], op=ALU.mult), reads=[('ps', 2), 'ek_t'], writes=['kd_t'])
        chk('g4a')
        P.op('dve', lambda e: e.tensor_copy(out=v_t[0:n, :], in_=vtv[0:n, :]), reads=[('ps', 2)], writes=['v_t'])
        if not state_only:
            P.op('dve', lambda e: e.tensor_tensor(out=qd[:, :, 0:n], in0=qT[:, :, c0:c0 + n], in1=eqT[:, :, 0:n], op=ALU.mult), reads=['qT', 'eqT'], writes=['qd'])
            P.op('dve', lambda e: e.tensor_tensor(out=kdT[:, :, 0:n], in0=kTs[:, :, c0:c0 + n], in1=ekT[:, :, 0:n], op=ALU.mult), reads=['kvT', 'ekT'], writes=['kdT'])

            def fns(pe):
                ins = None
                for jj in range(2):
                    ins = pe.matmul(bank(3)[0:n, 0:n], lhsT=kdT[:, jj, 0:n], rhs=qd[:, jj, 0:n], start=(jj == 0), stop=(jj == 1))
                return ins
            P.op('pe', fns, reads=['qd', 'kdT'], writes=[('ps', 3)])
            P.op('dve', lambda e: e.tensor_tensor(out=scm[0:n, 0:n], in0=bank(3)[0:n, 0:n], in1=maskf[0:n, 0:n], op=ALU.mult), reads=[('ps', 3), 'cst'], writes=['scm'])

            def fno(pe):
                pe.matmul(bank(4)[0:n, :], lhsT=scm[0:n, 0:n], rhs=v_t[0:n, :], start=True, stop=False)
                ins = None
                for jj in range(2):
                    ins = pe.matmul(bank(4)[0:n, :], lhsT=qd[:, jj, 0:n], rhs=Sb[:, jj, :], start=False, stop=(jj == 1))
                return ins
            P.op('pe', fno, reads=['scm', 'v_t', 'qd', 'Sb'], writes=[('ps', 4)])
            gla_post(n, c0, h, bank(4)[0:n, :], [('ps', 4)])
        chk('g5')
        def fnu(pe):
            ins = None
            for jj in range(2):
                ins = pe.matmul(bank(5 + jj), lhsT=kd_t[0:n, jj * 128:(jj + 1) * 128], rhs=v_t[0:n, :], start=True, stop=True)
            return ins
        P.op('pe', fnu, reads=['kd_t', 'v_t'], writes=[('ps', 5), ('ps', 6)])
        for jj in range(2):
            el = eqT[:, jj, n - 1:n]
            P.op('dve', (lambda e, jj=jj: e.tensor_tensor(out=S[:, jj, :], in0=S[:, jj, :], in1=bank(5 + jj), op=ALU.add)),
                 reads=[('ps', 5 + jj), 'S'], writes=['S'])
            P.op('dve', (lambda e, jj=jj, el=el: e.tensor_scalar(out=S[:, jj, :], in0=S[:, jj, :], scalar1=el, scalar2=None, op0=ALU.mult)),
                 reads=['S', 'eqT'], writes=['S'])
        P.op('act', lambda e: e.activation(out=Sb.rearrange("p a b -> p (a b)"), in_=S.rearrange("p a b -> p (a b)"), func=AF.Copy), reads=['S'], writes=['Sb'])

    for h in range(4):
        P.dma('sp', (lambda q, h=h: q.dma_start(out=wa2b[0:17, :], in_=wa2b_in[:, h * 256:(h + 1) * 256])), writes=['wa2b'])
        wA, kA = load_w([(w_in[:, OFF['q'] + h * 256:OFF['q'] + (h + 1) * 256], 0), (w_in[:, OFF['k'] + h * 256:OFF['k'] + (h + 1) * 256], 256)], KC, 512)
        for ec in range(4):
            banks, keys = mmF(wA, kA, ec * 128, 128, rx, ['xT'], KC, GROUPS)
            if ec < 2:
                evac('act', GROUPS, banks, keys, 128, lambda c0, c1, ec=ec: qT[:, ec, c0:c1], ['qT'], scale=0.0625)
            else:
                evac('act', GROUPS, banks, keys, 128, lambda c0, c1, ec=ec: kT[:, ec - 2, c0:c1], ['kvT'])
        for ec in range(2, 4):
            banks, keys = mmF(wA, kA, ec * 128, 128, rxp, ['xpT'], KC, PGROUPS)
            evac('dve', PGROUPS, banks, keys, 128, lambda c0, c1, ec=ec: kpT[:, ec - 2, c0:c1], ['kvT'])
        wB, kB = load_w([(w_in[:, OFF['v'] + h * 512:OFF['v'] + (h + 1) * 512], 0)], KC, 512)
        for ec in range(4):
            banks, keys = mmF(wB, kB, ec * 128, 128, rx, ['xT'], KC, GROUPS)
            evac('act', GROUPS, banks, keys, 128, lambda c0, c1, ec=ec: vT[:, ec, c0:c1], ['kvT'])
            banks, keys = mmF(wB, kB, ec * 128, 128, rxp, ['xpT'], KC, PGROUPS)
            evac('dve', PGROUPS, banks, keys, 128, lambda c0, c1, ec=ec: vpT[:, ec, c0:c1], ['kvT'])
        wC, kC = load_w([(w_in[:, OFF['g'] + h * 512:OFF['g'] + (h + 1) * 512], 0)], KC, 512)
        for ec in range(4):
            banks, keys = mmF(wC, kC, ec * 128, 128, rx, ['xT'], KC, GROUPS)
            evac('act', GROUPS, banks, keys, 128, lambda c0, c1, ec=ec: gsT[:, ec, c0:c1], ['gsT'], func=AF.Silu)
        if stop == 'A4p':
            return finish()
        P.op('dve', lambda e: e.memset(S.rearrange("p a b -> p (a b)"), 0.0), writes=['S'])
        P.op('dve', lambda e: e.memset(Sb.rearrange("p a b -> p (a b)"), 0.0), writes=['Sb'])
        for i in range(8):
            gla_chunk(h, 128, i * 128, kpT, vpT, a17p, A17PK, True)
            if stop == 'A4pre':
                return finish()
        gla_chunk(h, 2, 18, kT, vT, a17, A17K, False)
        if stop == 'A4h':
            return finish()
        for i in range(8):
            gla_chunk(h, 128, M0 + i * 128, kT, vT, a17, A17K, False)
        if stop == 'A4m':
            return finish()
        P.dma('sp', (lambda q, h=h: q.dma_start(out=pgla_out[h].rearrange("(jj p) v -> p jj v", p=128), in_=S)), reads=['S'])
        pre_s = psum[:, 0:2 * NS].rearrange("p (a b) -> p a b", a=2)

        def fna(pe, h=h):
            ins = None
            for jj in range(2):
                ins = pe.matmul(pre_s[:, jj, :], lhsT=wa2b[0:17, jj * 128:(jj + 1) * 128], rhs=a17[0:17, 0:NS], start=True, stop=True)
            return ins
        P.op('pe', fna, reads=A17K + ['wa2b'], writes=[('ps', 0)])
        P.op('act', lambda e: e.activation(out=aTs, in_=pre_s, func=AF.Exp, scale=-1.0), reads=[('ps', 0)], writes=['aTs'])
        P.op('act', lambda e: e.activation(out=aTs, in_=aTs, func=AF.Ln, bias=1.0), reads=['aTs'], writes=['aTs'])
        P.op('act', lambda e: e.activation(out=aTs, in_=aTs, func=AF.Exp, scale=-1.0 / 16.0), reads=['aTs'], writes=['aTs'])
        chk('s1')
        ktv = bankbf(2)[:, 0:256]; vtv = bankbf(2)[:, 256:768]

        def fnts(pe):
            ins = None
            for jj in range(2):
                ins = pe.transpose(ktv[0:NS, jj * 128:(jj + 1) * 128], kT[:, jj, 0:NS], identb)
            for vv in range(4):
                ins = pe.transpose(vtv[0:NS, vv * 128:(vv + 1) * 128], vT[:, vv, 0:NS], identb)
            return ins
        P.op('pe', fnts, reads=['kvT', 'cstb'], writes=[('ps', 2)])
        P.op('dve', lambda e: e.tensor_copy(out=v_t[0:NS, :], in_=vtv[0:NS, :]), reads=[('ps', 2)], writes=['v_t'])
        P.op('dve', lambda e: e.tensor_copy(out=kd_t[0:NS, :], in_=ktv[0:NS, :]), reads=[('ps', 2)], writes=['kd_t'])
        P.op('dve', lambda e: e.memset(QG.rearrange("p s j c -> p (s j c)"), 0.0), writes=['QG'])
        for s in range(NS):
            P.op('dve', (lambda e, s=s: e.tensor_copy(out=QG[:, s, :, s:s + 1], in_=qT[:, :, s:s + 1])), reads=['qT', 'QG'], writes=['QG'])
        chk('s2')
        for s in range(NS):
            i = s % 2
            P.dma('sp', (lambda q, s=s, i=i, h=h: q.dma_start(out=SS[i], in_=sgla_in[s, h].rearrange("(jj p) v -> p jj v", p=128))), writes=[('SS', i), 'S'] if i == 0 else [('SS', i)])
            P.op('dve', (lambda e, s=s, i=i: e.tensor_scalar(out=km[i][0:NS, :], in0=kd_t[0:NS, :], scalar1=ident[0:NS, s:s + 1], scalar2=None, op0=ALU.mult)),
                 reads=['kd_t', 'cst'], writes=[('km', i)])

            def fnu(pe, s=s):
                ins = None
                for jj in range(2):
                    ins = pe.matmul(bank(5 + jj), lhsT=km[s % 2][0:NS, jj * 128:(jj + 1) * 128], rhs=v_t[0:NS, :], start=True, stop=True)
                return ins
            P.op('pe', fnu, reads=[('km', i), 'v_t'], writes=[('ps', 5), ('ps', 6)])
            for jj in range(2):
                P.op('dve', (lambda e, s=s, jj=jj, i=i: e.scalar_tensor_tensor(out=SS[i][:, jj, :], in0=SS[i][:, jj, :], scalar=aTs[:, jj, s:s + 1],
                                                                              in1=bank(5 + jj), op0=ALU.mult, op1=ALU.add)),
                     reads=[('SS', i), 'aTs', ('ps', 5 + jj)], writes=[('SS', i)])
            P.dma('sp', (lambda q, s=s, i=i, h=h: q.dma_start(out=sgla_out[s, h].rearrange("(jj p) v -> p jj v", p=128), in_=SS[i])), reads=[('SS', i)])
            P.op('act', (lambda e, i=i: e.activation(out=SSb[i].rearrange("p a b -> p (a b)"), in_=SS[i].rearrange("p a b -> p (a b)"), func=AF.Copy)),
                 reads=[('SS', i)], writes=['Sb'])

            def fnos(pe, s=s, i=i):
                ins = None
                for jj in range(2):
                    ins = pe.matmul(bank(4)[0:NS, :], lhsT=QG[:, s, jj, :], rhs=SSb[i][:, jj, :], start=(s == 0 and jj == 0), stop=(s == NS - 1 and jj == 1))
                return ins
            P.op('pe', fnos, reads=['QG', 'Sb'], writes=[('ps', 4)])
            if s == 0:
                chk('s3')
        chk('s4')
        gla_post(NS, 0, h, bank(4)[0:NS, :], [('ps', 4)])
    P.op('dve', lambda e: e.memset(ogT[:, :, 16:18], 0.0), reads=['ogT'], writes=['ogT'])
    P.barrier()

    if stop == 'A4':
        return finish()
    SC = Arena(SCR0, NW)
    zT = r3(SC.f32(8 * NT), 8)
    scT = SC.f32(8 * 2 * NS).rearrange("p (j r s) -> p j r s", j=8, r=2)
    ctmp = SC.f32(NT); ctmp2 = SC.f32(NS)
    scv = scT_in.rearrange("(j p) r s -> p j r s", p=128)
    for j0 in (0, 4):
        P.dma('sp', (lambda q, j0=j0: q.dma_start(out=scT[:, j0:j0 + 4], in_=scv[:, j0:j0 + 4])), writes=['scT'])
    P.dma('sp', lambda q: q.dma_start(out=sconv_out[:, 0, :], in_=scT_in[:, 1, :]))
    for which in ('cb', 'cc', 'ch'):
        for blk in range(2):
            wbuf, wkey = load_w([(w_in[:, OFF[which] + blk * 512:OFF[which] + (blk + 1) * 512], 0)], KC, 512)
            for ec in range(4):
                j = blk * 4 + ec
                banks, keys = mmF(wbuf, wkey, ec * 128, 128, rx, ['xT'], KC, GROUPS)
                if which == 'cb':
                    evac('act', GROUPS, banks, keys, 128, lambda c0, c1, j=j: buT[:, j, c0:c1], [('buT', j)])
                elif which == 'cc':
                    evac('act', GROUPS, banks, keys, 128, lambda c0, c1, j=j: zT[:, j, c0:c1], [('zT', j)])
                else:
                    for gi, (c0, c1) in enumerate(GROUPS):
                        P.op('dve', (lambda e, j=j, c0=c0, c1=c1, b=banks[gi]: e.tensor_tensor(
                            out=zT[:, j, c0:c1], in0=bank(b)[:, 0:c1 - c0], in1=zT[:, j, c0:c1], op=ALU.mult)),
                            reads=[keys[gi], ('zT', j)], writes=[('zT', j)])
    for j in range(8):
        w0 = wconv[:, j * 3 + 0:j * 3 + 1]; w1 = wconv[:, j * 3 + 1:j * 3 + 2]; w2 = wconv[:, j * 3 + 2:j * 3 + 3]
        L = NT - 18
        u = ctmp[:, 0:L]
        P.op('dve', (lambda e, j=j, w0=w0: e.tensor_scalar(out=u, in0=zT[:, j, 16:16 + L], scalar1=w0, scalar2=None, op0=ALU.mult)),
             reads=[('zT', j), 'small'], writes=['ctmp'])
        P.op('dve', (lambda e, j=j, w1=w1: e.scalar_tensor_tensor(out=u, in0=zT[:, j, 17:17 + L], scalar=w1, in1=u, op0=ALU.mult, op1=ALU.add)),
             reads=[('zT', j), 'ctmp'], writes=['ctmp'])
        P.op('dve', (lambda e, j=j, w2=w2: e.scalar_tensor_tensor(out=u, in0=zT[:, j, 18:18 + L], scalar=w2, in1=u, op0=ALU.mult, op1=ALU.add)),
             reads=[('zT', j), 'ctmp'], writes=['ctmp'])
        P.op('dve', (lambda e, j=j: e.tensor_tensor(out=buT[:, j, 18:NT], in0=buT[:, j, 18:NT], in1=u, op=ALU.mult)),
             reads=[('buT', j), 'ctmp'], writes=[('buT', j)])
        us = ctmp2[:, 0:NS]
        P.op('dve', (lambda e, j=j, w0=w0: e.tensor_scalar(out=us, in0=scT[:, j, 0, :], scalar1=w0, scalar2=None, op0=ALU.mult)),
             reads=['scT', 'small'], writes=['ctmp2'])
        P.op('dve', (lambda e, j=j, w1=w1: e.scalar_tensor_tensor(out=us, in0=scT[:, j, 1, :], scalar=w1, in1=us, op0=ALU.mult, op1=ALU.add)),
             reads=['scT', 'ctmp2'], writes=['ctmp2'])
        P.op('dve', (lambda e, j=j, w2=w2: e.scalar_tensor_tensor(out=us, in0=zT[:, j, 0:NS], scalar=w2, in1=us, op0=ALU.mult, op1=ALU.add)),
             reads=[('zT', j), 'ctmp2'], writes=['ctmp2'])
        P.op('dve', (lambda e, j=j: e.tensor_tensor(out=buT[:, j, 0:NS], in0=buT[:, j, 0:NS], in1=us, op=ALU.mult)),
             reads=[('buT', j), 'ctmp2'], writes=[('buT', j)])
        P.op('dve', (lambda e, j=j: e.memset(buT[:, j, 16:18], 0.0)), writes=[('buT', j)])
    zk = [('zT', j) for j in range(8)]
    P.dma('sp', lambda q: q.dma_start(out=pconv_out.rearrange("(j p) r -> p j r", p=128), in_=zT[:, :, NT - 2:NT]), reads=zk)
    P.dma('sp', lambda q: q.dma_start(out=sconv_out[:, 1, :].rearrange("(j p) s -> p j s", p=128), in_=zT[:, :, 0:NS]), reads=zk)
    P.barrier()

    if stop == 'A2':
        return finish()
    SC.reset()
    MEM_OFF = SC.off
    memT = r3(SC.bf(KC * 256), KC)
    mkT = r3(SC.bf(8 * 256), 8)
    mvb = r3(SC.bf(2 * 1024), 2)
    mqT = r3(xpT_or_merged[:, 0:8 * NT], 8)
    Pf = SC.f32(1024); Pn = SC.bf(1024); PT = SC.bf(1024)
    mtok = Pf[:, 0:512]
    st4 = SC.f32(16)
    memv = memT_in.rearrange("(k p) m -> p k m", p=128)
    P.dma('pool', lambda q: q.dma_start(out=memT, in_=memv), writes=['memT'])
    rmem = lambda kc, c0, c1: memT[:, kc, c0:c1]
    tb = [0]

    def nbank():
        b = tb[0] % 6; tb[0] += 1
        return b
    for (wsrc, dst, isk) in ((w_mem_k, pmk_out, True), (w_mem_v, pmv_out, False)):
        for blk in range(2):
            wbuf, wkey = load_w([(wsrc[:, blk * 512:(blk + 1) * 512], 0)], KC, 512)
            for mt in range(2):
                b = nbank()

                def fn(pe, b=b, mt=mt, wbuf=wbuf):
                    ins = None
                    for kc in range(KC):
                        ins = pe.matmul(bank(b), lhsT=memT[:, kc, mt * 128:(mt + 1) * 128], rhs=wbuf[:, kc, :], start=(kc == 0), stop=(kc == KC - 1))
                    return ins
                P.op('pe', fn, reads=[wkey, 'memT'], writes=[('ps', b)])
                P.op('act', (lambda e, b=b: e.activation(out=mtok, in_=bank(b), func=AF.Copy)), reads=[('ps', b)], writes=['Pf'])
                P.dma('sp', (lambda q, dst=dst, mt=mt, blk=blk: q.dma_start(out=dst[mt * 128:(mt + 1) * 128, blk * 512:(blk + 1) * 512], in_=mtok)),
                      reads=['Pf'])
                if not isk:
                    P.op('dve', (lambda e, mt=mt, blk=blk: e.tensor_copy(out=mvb[:, mt, blk * 512:(blk + 1) * 512], in_=mtok)),
                         reads=['Pf'], writes=['mvb'])
                chk('m0')
            chk('t%d%d' % (int(isk), blk))
            if isk:
                for ec in range(4):
                    j = blk * 4 + ec
                    banks, keys = mmF(wbuf, wkey, ec * 128, 128, rmem, ['memT'], KC, [(0, 256)])
                    evac('dve', [(0, 256)], banks, keys, 128, lambda c0, c1, j=j: mkT[:, j, c0:c1], ['mkT'])
                    chk('m0c')
                chk('f%d' % blk)
        chk('m0e' if isk else 'm0f')
    chk('m1')
    for blk in range(2):
        wbuf, wkey = load_w([(w_in[:, OFF['mq'] + blk * 512:OFF['mq'] + (blk + 1) * 512], 0)], KC, 512)
        for ec in range(4):
            j = blk * 4 + ec
            banks, keys = mmF(wbuf, wkey, ec * 128, 128, rx, ['xT'], KC, GROUPS)
            evac('act', GROUPS, banks, keys, 128, lambda c0, c1, j=j: mqT[:, j, c0:c1], ['mqT'], scale=0.0625)

    chk('m2')

    def softmax_rows(n, lg_ap3, rkeys):
        mx = st4[0:n, 0:4]; nmx = st4[0:n, 4:8]; ssum = st4[0:n, 8:12]; rs = st4[0:n, 12:16]
        Pf3 = r3(Pf, 4); Pn3 = r3(Pn, 4)
        chk('x0')
        P.op('dve', lambda e: e.tensor_reduce(out=mx, in_=lg_ap3, axis=AX.X, op=ALU.max), reads=rkeys, writes=['st4'])
        chk('x1')
        P.op('dve', lambda e: e.tensor_scalar(out=nmx, in0=mx, scalar1=-1.0, scalar2=None, op0=ALU.mult), reads=['st4'], writes=['st4'])
        for h in range(4):
            P.op('act', (lambda e, h=h: e.activation(out=Pf3[0:n, h, :], in_=lg_ap3[:, h, :], func=AF.Exp, bias=nmx[:, h:h + 1],
                                                     accum_out=ssum[:, h:h + 1])), reads=list(rkeys) + ['st4'], writes=['Pf', ('ssum', h)])
            chk('x3')
        P.op('dve', lambda e: e.reciprocal(out=rs, in_=ssum), reads=[('ssum', h) for h in range(4)] + ['st4'], writes=['st4'])
        chk('x4')
        for h in range(4):
            P.op('dve', (lambda e, h=h: e.tensor_scalar(out=Pn3[0:n, h, :], in0=Pf3[0:n, h, :], scalar1=rs[:, h:h + 1], scalar2=None, op0=ALU.mult)),
                 reads=['Pf', 'st4'], writes=['Pn'])

    def attn_pv(n, c0, vsrc_fn, vkeys):
        Pn3 = r3(Pn, 4)
        ptv = bankbf(6)[:, 0:8 * 128].rearrange("p (a b) -> p a b", a=8)

        def fn(pe):
            ins = None
            for h in range(4):
                for mc in range(2):
                    ins = pe.transpose(ptv[:, h * 2 + mc, 0:n], Pn3[0:n, h, mc * 128:(mc + 1) * 128], identb[0:n, 0:n])
            return ins
        P.op('pe', fn, reads=['Pn', 'cstb'], writes=[('ps', 6)])
        PT3 = r3(PT, 8)
        P.op('dve', lambda e: e.tensor_copy(out=PT3[:, :, 0:n], in_=ptv[:, :, 0:n]), reads=[('ps', 6)], writes=['PT'])
        ov = psum[:, 0:1024].rearrange("p (a b) -> p a b", a=8)

        def fn2(pe):
            ins = None
            for j in range(8):
                for mc in range(2):
                    ins = pe.matmul(ov[:, j, 0:n], lhsT=vsrc_fn(mc, j), rhs=PT3[:, (j // 2) * 2 + mc, 0:n], start=(mc == 0), stop=(mc == 1))
            return ins
        P.op('pe', fn2, reads=['PT'] + list(vkeys), writes=[('ps', 0), ('ps', 1)])
        P.op('dve', lambda e: e.tensor_copy(out=omT[:, :, c0:c0 + n], in_=ov[:, :, 0:n]), reads=[('ps', 0), ('ps', 1)], writes=['omT'])

    lg3 = psum[:, 2 * 512:4 * 512].rearrange("p (a b) -> p a b", a=4)
    for (c0, n) in [(H0, 4)] + TILES[1:]:
        def fnl(pe, c0=c0, n=n):
            ins = None
            for h in range(4):
                for jj in range(2):
                    ins = pe.matmul(lg3[0:n, h, :], lhsT=mqT[:, 2 * h + jj, c0:c0 + n], rhs=mkT[:, 2 * h + jj, :], start=(jj == 0), stop=(jj == 1))
            return ins
        P.op('pe', fnl, reads=['mqT', 'mkT'], writes=[('ps', 2), ('ps', 3)])
        softmax_rows(n, lg3[0:n], [('ps', 2), ('ps', 3)])
        chk('m3')
        attn_pv(n, c0, lambda mc, j: mvb[:, mc, j * 128:(j + 1) * 128], ['mvb'])
        chk('m4')
    chk('m5')
    P.barrier()
    QS = [SC.bf(8 * 64).rearrange("p (j c) -> p j c", j=8) for _ in range(2)]
    PTs = SC.bf(2 * 64)
    SK = Arena(MEM_OFF, MEM_OFF + KC * 256 // 2)
    KS = [r3(SK.bf(8 * 256), 8) for _ in range(2)]
    VS = [r3(KS[i_].rearrange("p a b -> p (a b)"), 2) for i_ in range(2)]
    lgs = bank(4)[0:64, 0:256]
    for s in range(NS):
        i = s % 2
        P.dma('pool', (lambda q, s=s, i=i: q.dma_start(out=KS[i], in_=kcT_in[s].rearrange("(j p) m -> p j m", p=128))), writes=[('KS', i)])
        P.op('dve', (lambda e, i=i: e.memset(QS[i].rearrange("p j c -> p (j c)"), 0.0)), writes=[('QS', i)])
        for j in range(8):
            P.op('dve', (lambda e, s=s, j=j, i=i: e.tensor_copy(out=QS[i][:, j, s * 4 + j // 2:s * 4 + j // 2 + 1], in_=mqT[:, j, s:s + 1])),
                 reads=['mqT', ('QS', i)], writes=[('QS', i)])

        def fnk(pe, s=s, i=i):
            ins = None
            for j in range(8):
                ins = pe.matmul(lgs, lhsT=QS[i][:, j, :], rhs=KS[i][:, j, :], start=(s == 0 and j == 0), stop=(s == NS - 1 and j == 7))
            return ins
        P.op('pe', fnk, reads=[('QS', i), ('KS', i)], writes=[('ps', 4)])
    chk('m6')
    smx = st4[0:64, 0:1]; snm = st4[0:64, 1:2]; ssm = st4[0:64, 2:3]; srs = st4[0:64, 3:4]
    P.op('dve', lambda e: e.tensor_reduce(out=smx, in_=lgs, axis=AX.X, op=ALU.max), reads=[('ps', 4)], writes=['st4'])
    P.op('dve', lambda e: e.tensor_scalar(out=snm, in0=smx, scalar1=-1.0, scalar2=None, op0=ALU.mult), reads=['st4'], writes=['st4'])
    P.op('act', lambda e: e.activation(out=Pf[0:64, 0:256], in_=lgs, func=AF.Exp, bias=snm, accum_out=ssm), reads=[('ps', 4), 'st4'], writes=['Pf', 'ssm'])
    P.op('dve', lambda e: e.reciprocal(out=srs, in_=ssm), reads=['ssm', 'st4'], writes=['st4'])
    P.op('dve', lambda e: e.tensor_scalar(out=Pn[0:64, 0:256], in0=Pf[0:64, 0:256], scalar1=srs, scalar2=None, op0=ALU.mult), reads=['Pf', 'st4'], writes=['Pn'])
    chk('m7')
    ptsv = bankbf(6)[:, 0:128].rearrange("p (a b) -> p a b", a=2)

    def fnt(pe):
        ins = None
        for mc in range(2):
            ins = pe.transpose(ptsv[:, mc, :], Pn[0:64, mc * 128:(mc + 1) * 128], identb[0:64, 0:64])
        return ins
    P.op('pe', fnt, reads=['Pn', 'cstb'], writes=[('ps', 6)])
    PTs3 = r3(PTs, 2)
    P.op('dve', lambda e: e.tensor_copy(out=PTs3, in_=ptsv), reads=[('ps', 6)], writes=['PTs'])
    ovs = bank(5)[:, 0:128].rearrange("p (a b) -> p a b", a=8)
    for s in range(NS):
        i = s % 2
        P.dma('pool', (lambda q, s=s, i=i: q.dma_start(out=VS[i], in_=vc_in[s].rearrange("(mc p) d -> p mc d", p=128))), writes=[('KS', i)])

        def fnv(pe, s=s, i=i):
            ins = None
            for j in range(8):
                for mc in range(2):
                    col = s * 4 + j // 2
                    ins = pe.matmul(ovs[:, j, s:s + 1], lhsT=VS[i][:, mc, j * 128:(j + 1) * 128], rhs=PTs3[:, mc, col:col + 1],
                                    start=(mc == 0), stop=(mc == 1))
            return ins
        P.op('pe', fnv, reads=['PTs', ('KS', i)], writes=[('ps', 5)])
    P.op('dve', lambda e: e.tensor_copy(out=omT[:, :, 0:NS], in_=ovs), reads=[('ps', 5)], writes=['omT'])
    P.op('dve', lambda e: e.memset(omT[:, :, 16:18], 0.0), writes=['omT'])
    P.barrier()

    if stop == 'A3':
        return finish()
    SC.reset()
    G = r3(SC.f32(4 * NT), 4); Mg = r3(SC.f32(4 * NT), 4); mt_ = SC.f32(NT)
    rog = lambda kc, c0, c1: ogT[:, kc, c0:c1]
    rbu = lambda kc, c0, c1: buT[:, kc, c0:c1]
    rom = lambda kc, c0, c1: omT[:, kc, c0:c1]
    for blk in range(4):
        cs = slice(blk * 512, (blk + 1) * 512)
        for bi, (zname, wsrc, nk, rfn, rkey) in enumerate((('za', w_gla_out, 16, rog, 'ogT'), ('zb', w_conv_out, 8, rbu, 'buT'), ('zm', w_mem_out, 8, rom, 'omT'))):
            wz, kz = load_w([(w_in[:, OFF[zname] + blk * 512:OFF[zname] + (blk + 1) * 512], 0)], KC, 512)
            for ec in range(4):
                banks, keys = mmF(wz, kz, ec * 128, 128, rx, ['xT'], KC, GROUPS)
                evac('act', GROUPS, banks, keys, 128, lambda c0, c1, ec=ec: G[:, ec, c0:c1], [('G', ec)], func=AF.Sigmoid)
            wy, ky = load_w([(wsrc[:, cs], 0)], nk, 512)
            for ec in range(4):
                banks, keys = mmF(wy, ky, ec * 128, 128, rfn, [rkey] if rkey != 'buT' else [('buT', j) for j in range(8)], nk, GROUPS)
                for gi, (c0, c1) in enumerate(GROUPS):
                    src = bank(banks[gi])[:, 0:c1 - c0]
                    if bi == 0:
                        P.op('dve', (lambda e, ec=ec, c0=c0, c1=c1, src=src: e.tensor_tensor(out=Mg[:, ec, c0:c1], in0=src, in1=G[:, ec, c0:c1], op=ALU.mult)),
                             reads=[keys[gi], ('G', ec)], writes=[('Mg', ec)])
                    else:
                        P.op('dve', (lambda e, ec=ec, c0=c0, c1=c1, src=src: e.tensor_tensor(out=mt_[:, c0:c1], in0=src, in1=G[:, ec, c0:c1], op=ALU.mult)),
                             reads=[keys[gi], ('G', ec)], writes=['mt_'])
                        dst = Mg[:, ec, c0:c1] if bi == 1 else mergedT[:, blk * 4 + ec, c0:c1]
                        P.op('dve', (lambda e, ec=ec, c0=c0, c1=c1, dst=dst: e.tensor_tensor(out=dst, in0=Mg[:, ec, c0:c1], in1=mt_[:, c0:c1], op=ALU.add)),
                             reads=['mt_', ('Mg', ec)], writes=[('Mg', ec), 'mergedT'])
    P.barrier()
    if stop == 'A5':
        return finish()
    A2 = Arena(OG_OFF, NW)
    r1 = A2.f32(9 * D).rearrange("p (t d) -> p t d", t=9)
    lng = A2.f32(D); lnb = A2.f32(D)
    xres = [A2.f32(512) for _ in range(2)]
    x1b = A2.bf(D)
    bst = A2.f32(4 * 6); mvv = A2.f32(4)
    x1T = xT
    P.dma('sp', lambda q, lng=lng: q.dma_start(out=lng, in_=ln_in[0]), writes=['lng'])
    P.dma('sp', lambda q, lnb=lnb: q.dma_start(out=lnb, in_=ln_in[1]), writes=['lnb'])

    def layer_norm_tile(ti, n, src, keyr, gkey, bkey, lng, lnb, bst, mvv):
        for c in range(4):
            P.op('dve', (lambda e, c=c: e.bn_stats(out=bst[0:n, c * 6:(c + 1) * 6], in_=src[0:n, c * 512:(c + 1) * 512])), reads=[keyr], writes=['bst'])
        P.op('dve', lambda e: e.bn_aggr(out=mvv[0:n, 0:2], in_=bst[0:n, :]), reads=['bst'], writes=['mvv'])
        P.op('act', lambda e: e.activation(out=mvv[0:n, 2:3], in_=mvv[0:n, 1:2], func=AF.Ln, bias=1e-5), reads=['mvv'], writes=['mvv2'])
        P.op('act', lambda e: e.activation(out=mvv[0:n, 3:4], in_=mvv[0:n, 2:3], func=AF.Exp, scale=-0.5), reads=['mvv2'], writes=['mvv3'])
        P.op('dve', lambda e: e.tensor_scalar(out=src[0:n, :], in0=src[0:n, :], scalar1=mvv[0:n, 0:1], scalar2=mvv[0:n, 3:4], op0=ALU.subtract, op1=ALU.mult),
             reads=[keyr, 'mvv', 'mvv3'], writes=[keyr])
        P.op('dve', lambda e: e.tensor_tensor(out=src[0:n, :], in0=src[0:n, :], in1=lng[0:n, :], op=ALU.mult), reads=[keyr, gkey], writes=[keyr])
        P.op('dve', lambda e: e.tensor_tensor(out=src[0:n, :], in0=src[0:n, :], in1=lnb[0:n, :], op=ALU.add), reads=[keyr, bkey], writes=[keyr])

    xi = [0]
    for blk in range(4):
        wo, ko = load_w([(w_o[:, blk * 512:(blk + 1) * 512], 0)], KC, 512)
        for ti, (c0, n) in enumerate(TILES):
            b = nbank()

            def fn(pe, b=b, c0=c0, n=n, wo=wo):
                ins = None
                for kc in range(KC):
                    ins = pe.matmul(bank(b)[0:n, :], lhsT=mergedT[:, kc, c0:c0 + n], rhs=wo[:, kc, :], start=(kc == 0), stop=(kc == KC - 1))
                return ins
            P.op('pe', fn, reads=[ko, 'mergedT'], writes=[('ps', b)])
            xr = xi[0] % 2; xi[0] += 1
            P.dma('sp', (lambda q, xr=xr, c0=c0, n=n, blk=blk: q.dma_start(out=xres[xr][0:n, :], in_=x_tok[c0:c0 + n, blk * 512:(blk + 1) * 512])), writes=[('xres', xr)])
            P.op('dve', (lambda e, b=b, n=n, ti=ti, blk=blk, xr=xr: e.scalar_tensor_tensor(out=r1[0:n, ti, blk * 512:(blk + 1) * 512], in0=xres[xr][0:n, :], scalar=ALPHA,
                                                                                         in1=bank(b)[0:n, :], op0=ALU.mult, op1=ALU.add)),
                 reads=[('ps', b), ('xres', xr)], writes=[('r1', ti)])
    for ti, (c0, n) in enumerate(TILES):
        layer_norm_tile(ti, n, r1[:, ti, :], ('r1', ti), 'lng', 'lnb', lng, lnb, bst, mvv)
        P.dma('sp', (lambda q, ti=ti, c0=c0, n=n: q.dma_start(out=x1_scr[c0:c0 + n, :], in_=r1[0:n, ti, :])), reads=[('r1', ti)], writes=[('x1scr', ti)])
        P.op('act', (lambda e, ti=ti, n=n: e.activation(out=x1b[0:n, :], in_=r1[0:n, ti, :], func=AF.Copy)), reads=[('r1', ti)], writes=['x1b'])
        tv = bankbf(6, 2)[:, 0:KC * 128].rearrange("p (a b) -> p a b", a=KC)

        def fn(pe, n=n):
            ins = None
            for kc in range(KC):
                ins = pe.transpose(tv[:, kc, 0:n], x1b[0:n, kc * 128:(kc + 1) * 128], identb[0:n, 0:n])
            return ins
        P.op('pe', fn, reads=['x1b', 'cstb'], writes=[('ps', 6), ('ps', 7)])
        P.op('dve', (lambda e, c0=c0, n=n: e.tensor_copy(out=x1T[:, :, c0:c0 + n], in_=tv[:, :, 0:n])), reads=[('ps', 6), ('ps', 7)], writes=['x1T'])
    P.barrier()

    if stop == 'LN1':
        return finish()
    A3 = Arena(WB_OFF, NW)
    WG = [A3.bf(KC * 512) for _ in range(2)]
    hT = r3(A3.bf(FC * NT), FC)
    HT_END = A3.off
    AM_ = Arena(1500 + KC * NT // 2, WB_OFF)
    hgx = AM_.f32(NT); hc = AM_.f32(NT); hs = AM_.f32(NT)
    sfT = AM_.f32(FC * 2 * NS).rearrange("p (j r s) -> p j r s", j=FC, r=2)
    pfc = r3(AM_.f32(FC * 2), FC); sfc = r3(AM_.f32(FC * NS), FC)
    sfv = sfT_in.rearrange("(j p) r s -> p j r s", p=128)
    for j0 in range(0, FC, 4):
        j1 = min(FC, j0 + 4)
        P.dma('sp', (lambda q, j0=j0, j1=j1: q.dma_start(out=sfT[:, j0:j1], in_=sfv[:, j0:j1])), writes=['sfT'])
    for r0 in range(0, DFF, 1376):
        P.dma('sp', (lambda q, r0=r0: q.dma_start(out=sffn_out[r0:r0 + 1376, 0, :], in_=sfT_in[r0:r0 + 1376, 1, :])))
    P.op('dve', lambda e: e.memset(hs[:, 16:20], 0.0), writes=['hs'])
    rx1 = lambda kc, c0, c1: x1T[:, kc, c0:c1]
    wg_i = [0]

    def load_wg(src):
        i = wg_i[0]; wg_i[0] ^= 1
        w = src.shape[1]
        buf = WG[i][:, 0:KC * w].rearrange("p (k e) -> p k e", k=KC)
        sv = src.rearrange("(k p) e -> p k e", p=128)
        for (k0, k1) in ((0, 8), (8, 16)):
            P.dma('pool', (lambda q, o=buf[:, k0:k1, :], s=sv[:, k0:k1, :]: q.dma_start(out=o, in_=s)), writes=[('WG', i)])
        return buf, ('WG', i)
    for fb in range(11):
        f0 = fb * 512; fw = min(512, DFF - f0)
        wg, kg = load_wg(w_gate[:, f0:f0 + fw])
        wu, ku = load_wg(w_up[:, f0:f0 + fw])
        for ec in range(fw // 128):
            j = fb * 4 + ec
            w0 = wfc[:, j * 4 + 0:j * 4 + 1]; w1 = wfc[:, j * 4 + 1:j * 4 + 2]; w2 = wfc[:, j * 4 + 2:j * 4 + 3]; bb = wfc[:, j * 4 + 3:j * 4 + 4]
            banks, keys = mmF(wg, kg, ec * 128, 128, rx1, ['x1T'], KC, GROUPS)
            evac('act', GROUPS, banks, keys, 128, lambda c0, c1: hgx[:, c0:c1], ['hgx'])
            P.op('dve', lambda e: e.tensor_scalar(out=hgx[:, 18:20], in0=hgx[:, 18:20], scalar1=flag, scalar2=None, op0=ALU.mult), reads=['hgx', 'small'], writes=['hgx'])
            P.op('dve', (lambda e, w0=w0, bb=bb: e.tensor_scalar(out=hc[:, M0:NT], in0=hgx[:, M0 - 2:NT - 2], scalar1=w0, scalar2=bb, op0=ALU.mult, op1=ALU.add)),
                 reads=['hgx', 'small'], writes=['hc'])
            P.op('dve', (lambda e, w1=w1: e.scalar_tensor_tensor(out=hc[:, M0:NT], in0=hgx[:, M0 - 1:NT - 1], scalar=w1, in1=hc[:, M0:NT], op0=ALU.mult, op1=ALU.add)),
                 reads=['hgx', 'hc'], writes=['hc'])
            P.op('dve', (lambda e, w2=w2: e.scalar_tensor_tensor(out=hc[:, M0:NT], in0=hgx[:, M0:NT], scalar=w2, in1=hc[:, M0:NT], op0=ALU.mult, op1=ALU.add)),
                 reads=['hgx', 'hc'], writes=['hc'])
            P.op('dve', (lambda e, w0=w0, bb=bb, j=j: e.tensor_scalar(out=hc[:, 0:NS], in0=sfT[:, j, 0, :], scalar1=w0, scalar2=bb, op0=ALU.mult, op1=ALU.add)),
                 reads=['sfT', 'small', 'hc'], writes=['hc'])
            P.op('dve', (lambda e, w1=w1, j=j: e.scalar_tensor_tensor(out=hc[:, 0:NS], in0=sfT[:, j, 1, :], scalar=w1, in1=hc[:, 0:NS], op0=ALU.mult, op1=ALU.add)),
                 reads=['sfT', 'hc'], writes=['hc'])
            P.op('dve', (lambda e, w2=w2: e.scalar_tensor_tensor(out=hc[:, 0:NS], in0=hgx[:, 0:NS], scalar=w2, in1=hc[:, 0:NS], op0=ALU.mult, op1=ALU.add)),
                 reads=['hgx', 'hc'], writes=['hc'])
            P.op('dve', (lambda e, j=j: e.tensor_copy(out=pfc[:, j, :], in_=hgx[:, NT - 2:NT])), reads=['hgx'], writes=['pfc'])
            P.op('dve', (lambda e, j=j: e.tensor_copy(out=sfc[:, j, :], in_=hgx[:, 0:NS])), reads=['hgx'], writes=['sfc'])
            P.op('act', lambda e: e.activation(out=hs[:, M0:NT], in_=hc[:, M0:NT], func=AF.Silu), reads=['hc'], writes=['hs'])
            P.op('act', lambda e: e.activation(out=hs[:, 0:NS], in_=hc[:, 0:NS], func=AF.Silu), reads=['hc'], writes=['hs'])
            banks, keys = mmF(wu, ku, ec * 128, 128, rx1, ['x1T'], KC, GROUPS)
            for gi, (c0, c1) in enumerate(GROUPS):
                P.op('dve', (lambda e, j=j, c0=c0, c1=c1, b=banks[gi]: e.tensor_tensor(out=hT[:, j, c0:c1], in0=bank(b)[:, 0:c1 - c0], in1=hs[:, c0:c1], op=ALU.mult)),
                     reads=[keys[gi], 'hs'], writes=['hT'])
    pfv = pffn_out.rearrange("(j p) r -> p j r", p=128)
    sfov = sffn_out[:, 1, :].rearrange("(j p) s -> p j s", p=128)
    for j0 in range(0, FC, 8):
        j1 = min(FC, j0 + 8)
        P.dma('sp', (lambda q, j0=j0, j1=j1: q.dma_start(out=pfv[:, j0:j1], in_=pfc[:, j0:j1])), reads=['pfc'])
        P.dma('sp', (lambda q, j0=j0, j1=j1: q.dma_start(out=sfov[:, j0:j1], in_=sfc[:, j0:j1])), reads=['sfc'])
    P.barrier()
    if stop == 'B1':
        return finish()
    AX_ = Arena(0, WB_OFF + 2 * (KC * 512 // 2))
    r2 = [AX_.f32(D) for _ in range(7)]
    WD0_OFF = AX_.off
    WD = [r3(AX_.bf(FC * 256), FC) for _ in range(2)]
    xres2 = [AX_.f32(256) for _ in range(2)]
    bst = AX_.f32(24); mvv = AX_.f32(4)
    AY_ = Arena(HT_END, NW)
    r2 += [AY_.f32(D) for _ in range(2)]
    AL_ = Arena(WD0_OFF, WD0_OFF + 2 * D)
    lng = AL_.f32(D); lnb = AL_.f32(D)
    wdv = w_down.rearrange("(k p) e -> p k e", p=128)
    for blk in range(8):
        i = blk % 2
        for (k0, k1) in ((0, 22), (22, FC)):
            P.dma('pool', (lambda q, i=i, k0=k0, k1=k1, blk=blk: q.dma_start(out=WD[i][:, k0:k1, :], in_=wdv[:, k0:k1, blk * 256:(blk + 1) * 256])), writes=[('WD', i)])
        for ti, (c0, n) in enumerate(TILES):
            b = nbank()

            def fn(pe, b=b, c0=c0, n=n, i=i):
                ins = None
                for kc in range(FC):
                    ins = pe.matmul(bank(b)[0:n, 0:256], lhsT=hT[:, kc, c0:c0 + n], rhs=WD[i][:, kc, :], start=(kc == 0), stop=(kc == FC - 1))
                return ins
            P.op('pe', fn, reads=[('WD', i), 'hT'], writes=[('ps', b)])
            xr = xi[0] % 2; xi[0] += 1
            P.dma('sp', (lambda q, xr=xr, c0=c0, n=n, blk=blk: q.dma_start(out=xres2[xr][0:n, :], in_=x1_scr[c0:c0 + n, blk * 256:(blk + 1) * 256])),
                  reads=[('x1scr', ti)], writes=[('xres2', xr)])
            P.op('dve', (lambda e, b=b, n=n, ti=ti, blk=blk, xr=xr: e.scalar_tensor_tensor(out=r2[ti][0:n, blk * 256:(blk + 1) * 256], in0=xres2[xr][0:n, :], scalar=ALPHA,
                                                                                         in1=bank(b)[0:n, 0:256], op0=ALU.mult, op1=ALU.add)),
                 reads=[('ps', b), ('xres2', xr)], writes=[('r2', ti)])
    P.barrier()
    P.dma('sp', lambda q, lng=lng: q.dma_start(out=lng, in_=ln_in[2]), writes=['lng2'])
    P.dma('sp', lambda q, lnb=lnb: q.dma_start(out=lnb, in_=ln_in[3]), writes=['lnb2'])
    for ti, (c0, n) in enumerate(TILES):
        layer_norm_tile(ti, n, r2[ti], ('r2', ti), 'lng2', 'lnb2', lng, lnb, bst, mvv)
        P.dma('sp', (lambda q, ti=ti, c0=c0, n=n: q.dma_start(out=y_out[c0:c0 + n, :], in_=r2[ti][0:n, :])), reads=[('r2', ti)])
    return finish()


_NC_CACHE = {}


def _consts():
    c = np.zeros((128, 512), np.float32)
    c[:, 0:128] = np.eye(128, dtype=np.float32)
    m = np.triu(np.ones((128, 128), np.float32))
    c[:, 128:256] = m
    c[:, 256:384] = m * (-1.0 / 16.0)
    return c


def kernel(x_prompt, x_sample, mem_prompt, cache_mem_k, cache_mem_v, state_gla, state_conv, state_ffn_conv,
           w_in, w_gla_a2, b_gla_a2, g_gla_norm, w_gla_out, w_conv, w_conv_out, w_mem_k, w_mem_v, w_mem_out,
           w_o, ln1_g, ln1_b, w_ffn_gate, w_ffn_up, w_ffn_conv, b_ffn_conv, w_ffn_down, ln2_g, ln2_b):
    f = lambda a: np.ascontiguousarray(np.asarray(a, dtype=np.float32))
    x_prompt = f(x_prompt); x_sample = f(x_sample); mem_prompt = f(mem_prompt)
    cache_mem_k = f(cache_mem_k); cache_mem_v = f(cache_mem_v); state_gla = f(state_gla)
    state_conv = f(state_conv); state_ffn_conv = f(state_ffn_conv)
    if 'nc' not in _NC_CACHE:
        _NC_CACHE['nc'] = build_program()
    nc = _NC_CACHE['nc']
    shared = {
        "consts": _consts(), "ones_row": np.ones((1, NT), np.float32),
        "w_in": f(w_in[0]), "wa2b": f(np.concatenate([np.asarray(w_gla_a2[0]), np.asarray(b_gla_a2[0])[None, :]], axis=0)),
        "gnorm_b": f(np.broadcast_to(np.asarray(g_gla_norm[0])[None, :], (128, 512))),
        "w_gla_out": f(w_gla_out[0]),
        "wconv_p": f(np.asarray(w_conv[0]).reshape(3, 8, 128).transpose(2, 1, 0).reshape(128, 24)),
        "w_conv_out": f(w_conv_out[0]), "w_mem_k": f(w_mem_k[0]), "w_mem_v": f(w_mem_v[0]), "w_mem_out": f(w_mem_out[0]),
        "w_o": f(w_o[0]),
        "ln_b": f(np.stack([np.broadcast_to(np.asarray(v[0])[None, :], (128, D)) for v in (ln1_g, ln1_b, ln2_g, ln2_b)])),
        "w_ffn_gate": f(w_ffn_gate[0]), "w_ffn_up": f(w_ffn_up[0]),
        "wfc_p": f(np.concatenate([np.asarray(w_ffn_conv[0]), np.asarray(b_ffn_conv[0])[None, :]], axis=0).reshape(4, FC, 128).transpose(2, 1, 0).reshape(128, FC * 4)),
        "w_ffn_down": f(w_ffn_down[0]),
    }
    in_maps = []
    for c in range(8):
        b, h = c // 2, c % 2
        T0 = 1024 * h
        toks = np.zeros((NT, D), np.float32)
        toks[0:NS] = x_sample[16 * c:16 * c + 16, 0]
        if h == 1:
            toks[H0:M0] = x_prompt[b, T0 - 4:T0]
        toks[M0:] = x_prompt[b, T0:T0 + 1024]
        xp = np.zeros((NP, D), np.float32)
        if h == 1:
            xp[2:] = x_prompt[b, 0:1022]
        m = dict(shared)
        m.update({
            "xT_in": f(toks.T), "xpT_in": f(xp.T), "x_tok": toks,
            "flag": np.full((128, 1), float(h), np.float32),
            "memT_in": f(mem_prompt[b].T),
            "kcT_in": f(cache_mem_k[0, 16 * c:16 * c + 16].reshape(16, 256, 1024).transpose(0, 2, 1)),
            "vc_in": f(cache_mem_v[0, 16 * c:16 * c + 16].reshape(16, 256, 1024)),
            "sgla_in": f(state_gla[0, 16 * c:16 * c + 16]),
            "scT_in": f(state_conv[0, 16 * c:16 * c + 16].transpose(2, 1, 0)),
            "sfT_in": f(state_ffn_conv[0, 16 * c:16 * c + 16].transpose(2, 1, 0)),
        })
        in_maps.append(m)
    res = run_bass_kernel_spmd(nc, in_maps, core_ids=list(range(8)))
    R = res.results
    yp = np.zeros((4, 2048, D), np.float32); ys = np.zeros((128, 1, D), np.float32)
    pmk = np.zeros((1, 4, 256, 4, 256), np.float32); pmv = np.zeros_like(pmk)
    pgla = np.zeros((1, 4, 4, 256, 512), np.float32); pconv = np.zeros((1, 4, 2, 1024), np.float32); pffn = np.zeros((1, 4, 2, DFF), np.float32)
    sgla = np.zeros((1, 128, 4, 256, 512), np.float32); sconv = np.zeros((1, 128, 2, 1024), np.float32); sffn = np.zeros((1, 128, 2, DFF), np.float32)
    for c in range(8):
        b, h = c // 2, c % 2
        r = R[c]
        yp[b, 1024 * h:1024 * h + 1024] = r["y_out"][M0:]
        ys[16 * c:16 * c + 16, 0] = r["y_out"][0:NS]
        sgla[0, 16 * c:16 * c + 16] = r["sgla_out"]
        sconv[0, 16 * c:16 * c + 16] = r["sconvT"].transpose(2, 1, 0)
        sffn[0, 16 * c:16 * c + 16] = r["sffnT"].transpose(2, 1, 0)
        if h == 0:
            pmk[0, b] = r["pmk"].reshape(256, 4, 256); pmv[0, b] = r["pmv"].reshape(256, 4, 256)
        else:
            pgla[0, b] = r["pgla"]; pconv[0, b] = r["pconvT"].T; pffn[0, b] = r["pffnT"].T
    return (yp, ys, pmk, pmv, pgla, pconv, pffn, sgla, sconv, sffn)
```

```python
import numpy as np
import concourse.bass as bass
import concourse.mybir as mybir
from concourse.bass_utils import run_bass_kernel_spmd
from contextlib import ExitStack

F32 = mybir.dt.float32
BF16 = mybir.dt.bfloat16
AF = mybir.ActivationFunctionType
ALU = mybir.AluOpType
AX = mybir.AxisListType

D = 2048; KC = 16; NT = 1044; NS = 16; H0 = 16; M0 = 20; NP = 1024
DFF = 5504; FC = 43
GROUPS = [(0, 348), (348, 696), (696, 1044)]
PGROUPS = [(0, 512), (512, 1024)]
ALPHA = float(2.0 ** 0.25)
OFF = dict(q=0, k=1024, v=2048, g=4096, a=6144, cb=6160, cc=7184, ch=8208, mq=9232, za=10256, zb=12304, zm=14352)
TILES = [(0, 20)] + [(M0 + 128 * i, 128) for i in range(8)]


class StopBuild(Exception):
    pass


class Prog:
    def __init__(self):
        self.ops = {e: [] for e in ('pe', 'act', 'dve', 'pool', 'sp')}
        self.cnt = {}
        self.res_w = {}
        self.res_r = {}
        self.known = {e: {} for e in self.ops}
        self.rr = {'sp': 0, 'pool': 0, 'act': 0}
        self.nd = {'sp': 8, 'pool': 4, 'act': 4}

    def _deps(self, reads, writes):
        d = {}

        def add(sv):
            if sv is None:
                return
            s, v = sv
            if d.get(s, -1) < v:
                d[s] = v
        for k in reads:
            add(self.res_w.get(k))
        for k in writes:
            add(self.res_w.get(k))
            for s, v in self.res_r.get(k, {}).items():
                add((s, v))
        return d

    def _record(self, stream, val, reads, writes):
        for k in reads:
            self.res_r.setdefault(k, {})[stream] = val
        for k in writes:
            self.res_w[k] = (stream, val)
            self.res_r[k] = {}

    def _waits(self, eng, d):
        waits = []
        for s, v in d.items():
            if s == eng and eng == 'pe':
                continue
            if self.known[eng].get(s, -1) >= v:
                continue
            self.known[eng][s] = v
            waits.append((s, v))
        return waits

    def op(self, eng, fn, reads=(), writes=()):
        d = self._deps(reads, writes)
        waits = self._waits(eng, d)
        val = self.cnt.get(eng, 0) + 1
        self.cnt[eng] = val
        self.ops[eng].append((waits, fn, (eng, 1)))
        self._record(eng, val, reads, writes)

    def dma(self, q, fn, reads=(), writes=()):
        k = self.rr[q]
        self.rr[q] = (k + 1) % self.nd[q]
        stream = 'dma_%s%d' % (q, k)
        d = self._deps(reads, writes)
        prev = self.cnt.get(stream, 0)
        if prev > 0:
            d[stream] = max(d.get(stream, 0), prev)
        waits = self._waits(q, d)
        val = prev + 16
        self.cnt[stream] = val
        self.ops[q].append((waits, fn, (stream, 16)))
        self._record(stream, val, reads, writes)

    def barrier(self, pool=True):
        snap = dict(self.cnt)
        for e in self.ops:
            if e == 'pool' and not pool:
                continue
            waits = self._waits(e, dict(snap))
            if waits:
                self.ops[e].append((waits, None, None))

    def emit(self, nc, es):
        sems = {s: es.enter_context(nc.semaphore(s)) for s in self.cnt}
        block = es.enter_context(nc.Block())

        def run(name, eh):
            for waits, fn, sig in self.ops[name]:
                for s, v in waits:
                    eh.wait_ge(sems[s], v)
                if fn is not None:
                    ins = fn(eh)
                    ins.then_inc(sems[sig[0]], sig[1])

        @block.tensor
        def _(e):
            run('pe', e)

        @block.scalar
        def _(e):
            run('act', e)

        @block.vector
        def _(e):
            run('dve', e)

        @block.gpsimd
        def _(e):
            run('pool', e)

        @block.sync
        def _(e):
            run('sp', e)


def build_program(stop=None):
    holder = {}
    try:
        return _build(stop, holder)
    except StopBuild:
        return holder['finish']()


def _build(stop, holder):
    nc = bass.Bass("TRN2", target_bir_lowering=False)
    P = Prog()
    es = ExitStack()

    def din(name, shape):
        return nc.dram_tensor(name, list(shape), F32, kind="ExternalInput").ap()

    def dout(name, shape):
        return nc.dram_tensor(name, list(shape), F32, kind="ExternalOutput").ap()

    def finish():
        P.barrier()
        with nc.allow_non_contiguous_dma(reason="small strided state rows"):
            P.emit(nc, es)
        es.close()
        nc._prog_counts = dict(P.cnt)
        return nc
    holder['finish'] = finish

    def chk(tag):
        if stop == tag:
            raise StopBuild()

    xT_in = din("xT_in", [D, NT]); xpT_in = din("xpT_in", [D, NP]); x_tok = din("x_tok", [NT, D])
    flag_in = din("flag", [128, 1]); consts_in = din("consts", [128, 512]); ones_in = din("ones_row", [1, NT])
    memT_in = din("memT_in", [D, 256]); kcT_in = din("kcT_in", [NS, 1024, 256]); vc_in = din("vc_in", [NS, 256, 1024])
    sgla_in = din("sgla_in", [NS, 4, 256, 512]); scT_in = din("scT_in", [1024, 2, NS]); sfT_in = din("sfT_in", [DFF, 2, NS])
    w_in = din("w_in", [D, 16400]); wa2b_in = din("wa2b", [17, 1024]); gnorm_in = din("gnorm_b", [128, 512])
    w_gla_out = din("w_gla_out", [D, D]); wconv_in = din("wconv_p", [128, 24]); w_conv_out = din("w_conv_out", [1024, D])
    w_mem_k = din("w_mem_k", [D, 1024]); w_mem_v = din("w_mem_v", [D, 1024]); w_mem_out = din("w_mem_out", [1024, D])
    w_o = din("w_o", [D, D]); ln_in = din("ln_b", [4, 128, D])
    w_gate = din("w_ffn_gate", [D, DFF]); w_up = din("w_ffn_up", [D, DFF]); wfc_in = din("wfc_p", [128, FC * 4])
    w_down = din("w_ffn_down", [DFF, D])

    y_out = dout("y_out", [NT, D]); pmk_out = dout("pmk", [256, 1024]); pmv_out = dout("pmv", [256, 1024])
    pgla_out = dout("pgla", [4, 256, 512]); pconv_out = dout("pconvT", [1024, 2]); pffn_out = dout("pffnT", [DFF, 2])
    sgla_out = dout("sgla_out", [NS, 4, 256, 512]); sconv_out = dout("sconvT", [1024, 2, NS]); sffn_out = dout("sffnT", [DFF, 2, NS])
    x1_scr = nc.dram_tensor("x1_scr", [NT, D], F32, kind="Internal").ap()

    NW = 53200
    big = es.enter_context(nc.sbuf_tensor("big", [128, NW], F32))
    psum = es.enter_context(nc.psum_tensor("ps", [128, 4096], F32))

    class Arena:
        def __init__(self, base, limit):
            self.base = base; self.off = base; self.limit = limit

        def f32(self, n):
            o = self.off; self.off += n
            assert self.off <= self.limit, (self.off, self.limit)
            return big[:, o:o + n]

        def bf(self, n):
            w = (n + 1) // 2
            o = self.off; self.off += w
            assert self.off <= self.limit, (self.off, self.limit)
            return big[:, o:o + w].bitcast(BF16)

        def reset(self):
            self.off = self.base

    def r3(ap, a):
        return ap.rearrange("p (a b) -> p a b", a=a)

    def bank(b, n=512):
        return psum[:, b * 512:b * 512 + n]

    def bankbf(b, nb=1):
        return psum[:, b * 512:(b + nb) * 512].bitcast(BF16)

    PA = Arena(0, 1500)
    cst = PA.f32(512)
    ident = cst[:, 0:128]; maskf = cst[:, 128:256]; trineg = cst[:, 256:384]
    identb = PA.bf(128); maskb = PA.bf(128)
    flag = PA.f32(1); wconv = PA.f32(24); wfc = PA.f32(FC * 4); gnorm = PA.f32(512)
    A1 = Arena(1500, NW)
    xT = r3(A1.bf(KC * NT), KC)
    xpT_or_merged = A1.bf(KC * NT)
    xpT = r3(xpT_or_merged[:, 0:KC * NP], KC)
    mergedT = r3(xpT_or_merged, KC)
    WB_OFF = A1.off
    WB = [A1.bf(KC * 512) for _ in range(2)]
    OG_OFF = A1.off
    ogT = r3(A1.bf(KC * NT), KC)
    BU_OFF = A1.off
    buT = r3(A1.bf(8 * NT), 8)
    omT = r3(A1.bf(8 * NT), 8)
    SCR0 = A1.off
    SC = Arena(BU_OFF, NW)
    a17 = SC.f32(NT); a17p = SC.f32(NP)

    wb_i = [0]

    def load_w(src_list, nk, ncols_total):
        i = wb_i[0]; wb_i[0] ^= 1
        buf = WB[i][:, 0:nk * ncols_total].rearrange("p (k e) -> p k e", k=nk)
        for (src, co) in src_list:
            w = src.shape[1]
            sv = src.rearrange("(k p) e -> p k e", p=128)
            half = (nk + 1) // 2
            for (k0, k1) in ((0, half), (half, nk)):
                P.dma('pool', (lambda q, o=buf[:, k0:k1, co:co + w], s=sv[:, k0:k1, :]: q.dma_start(out=o, in_=s)),
                      writes=[('W', i)])
        return buf, ('W', i)

    fset = [0]

    def mmF(wbuf, wkey, e0, M, rhs_fn, rkeys, nk, groups):
        s = fset[0]; fset[0] ^= 1
        banks = [3 * s + gi for gi in range(len(groups))]
        keys = [('ps', b) for b in banks]

        def fn(pe):
            ins = None
            for kc in range(nk):
                for gi, (c0, c1) in enumerate(groups):
                    ins = pe.matmul(bank(banks[gi])[0:M, 0:c1 - c0], lhsT=wbuf[:, kc, e0:e0 + M], rhs=rhs_fn(kc, c0, c1),
                                    start=(kc == 0), stop=(kc == nk - 1))
            return ins
        P.op('pe', fn, reads=[wkey] + list(rkeys), writes=keys)
        return banks, keys

    def evac(eng, groups, banks, keys, M, out_fn, wkeys, func=None, scale=None, rkeys=()):
        for gi, (c0, c1) in enumerate(groups):
            src = bank(banks[gi])[0:M, 0:c1 - c0]
            dst = out_fn(c0, c1)
            if eng == 'act':
                kw = {}
                if scale is not None:
                    kw['scale'] = scale
                P.op('act', (lambda e, d=dst, s=src, kw=kw: e.activation(out=d, in_=s, func=(func or AF.Copy), **kw)),
                     reads=[keys[gi]] + list(rkeys), writes=wkeys)
            else:
                P.op('dve', (lambda e, d=dst, s=src: e.tensor_copy(out=d, in_=s)), reads=[keys[gi]] + list(rkeys), writes=wkeys)

    P.dma('sp', lambda q: q.dma_start(out=cst, in_=consts_in[:, :]), writes=['cst'])
    P.dma('sp', lambda q: q.dma_start(out=flag, in_=flag_in[:, :]), writes=['small'])
    P.dma('sp', lambda q: q.dma_start(out=wconv, in_=wconv_in[:, :]), writes=['small'])
    P.dma('sp', lambda q: q.dma_start(out=wfc, in_=wfc_in[:, :]), writes=['small'])
    P.dma('sp', lambda q: q.dma_start(out=gnorm, in_=gnorm_in[:, :]), writes=['small'])
    P.dma('sp', lambda q: q.dma_start(out=a17[16:17, :], in_=ones_in[0:1, :]), writes=['a17ones'])
    P.dma('sp', lambda q: q.dma_start(out=a17p[16:17, :], in_=ones_in[0:1, 0:NP]), writes=['a17ones'])
    P.op('dve', lambda e: e.tensor_copy(out=identb, in_=ident), reads=['cst'], writes=['cstb'])
    P.op('dve', lambda e: e.tensor_copy(out=maskb, in_=maskf), reads=['cst'], writes=['cstb'])
    xv = xT_in.rearrange("(k p) t -> p k t", p=128)
    xpv = xpT_in.rearrange("(k p) t -> p k t", p=128)
    for k0 in range(0, KC, 4):
        P.dma('pool', (lambda q, k0=k0: q.dma_start(out=xT[:, k0:k0 + 4, :], in_=xv[:, k0:k0 + 4, :])), writes=['xT'])
    for k0 in range(0, KC, 4):
        P.dma('pool', (lambda q, k0=k0: q.dma_start(out=xpT[:, k0:k0 + 4, :], in_=xpv[:, k0:k0 + 4, :])), writes=['xpT'])

    if stop == 'A0':
        return finish()
    rx = lambda kc, c0, c1: xT[:, kc, c0:c1]
    rxp = lambda kc, c0, c1: xpT[:, kc, c0:c1]

    wa_f = SC.f32(KC * 16)
    wa_buf = r3(SC.bf(KC * 16), KC); wa_key = 'wa'
    P.dma('sp', lambda q: q.dma_start(out=r3(wa_f, KC), in_=w_in[:, OFF['a']:OFF['a'] + 16].rearrange("(k p) e -> p k e", p=128)), writes=['wa_f'])
    P.op('dve', lambda e: e.tensor_copy(out=wa_buf.rearrange("p a b -> p (a b)"), in_=wa_f), reads=['wa_f'], writes=['wa'])
    banks, keys = mmF(wa_buf, wa_key, 0, 16, rx, ['xT'], KC, GROUPS)
    evac('act', GROUPS, banks, keys, 16, lambda c0, c1: a17[0:16, c0:c1], ['a17'])
    banks, keys = mmF(wa_buf, wa_key, 0, 16, rxp, ['xpT'], KC, PGROUPS)
    evac('act', PGROUPS, banks, keys, 16, lambda c0, c1: a17p[0:16, c0:c1], ['a17p'])
    A17K = ['a17', 'a17ones']; A17PK = ['a17p', 'a17ones']

    if stop == 'A1':
        return finish()
    wa2b = SC.f32(256)
    qT = r3(SC.bf(2 * NT), 2); kT = r3(SC.bf(2 * NT), 2); vT = r3(SC.bf(4 * NT), 4); gsT = r3(SC.bf(4 * NT), 4)
    kpT = r3(SC.bf(2 * NP), 2); vpT = r3(SC.bf(4 * NP), 4)
    S = r3(SC.f32(1024), 2); Sb = r3(SC.bf(1024), 2)
    e1 = SC.f32(256); sp_t = SC.f32(256); ek_t = SC.f32(256)
    eqT = r3(SC.f32(256), 2); ekT = r3(SC.f32(256), 2)
    qd = r3(SC.bf(256), 2); kdT = r3(SC.bf(256), 2)
    kd_t = SC.bf(256); v_t = SC.bf(512); scm = SC.bf(128)
    junk = SC.bf(512); on = SC.bf(512); st2 = SC.f32(4)
    aTs = r3(SC.f32(2 * NS), 2); km = [SC.bf(256) for _ in range(2)]
    QG = SC.bf(NS * 2 * NS).rearrange("p (s j c) -> p s j c", s=NS, j=2)
    SS = [S, r3(SC.f32(1024), 2)]
    SSb = [Sb, Sb]

    def gla_post(n, c0, h, ops_ap, okeys):
        ss = st2[0:n, 0:1]; lv = st2[0:n, 1:2]; rstd = st2[0:n, 2:3]
        P.op('act', lambda e: e.activation(out=junk[0:n, :], in_=ops_ap, func=AF.Square, accum_out=ss), reads=okeys, writes=['junk', 'st2a'])
        P.op('act', lambda e: e.activation(out=lv, in_=ss, func=AF.Ln, scale=1.0 / 512.0, bias=1e-6), reads=['st2a'], writes=['st2b'])
        P.op('act', lambda e: e.activation(out=rstd, in_=lv, func=AF.Exp, scale=-0.5), reads=['st2b'], writes=['st2c'])
        P.op('dve', lambda e: e.scalar_tensor_tensor(out=on[0:n, :], in0=ops_ap, scalar=rstd, in1=gnorm[0:n, :], op0=ALU.mult, op1=ALU.mult),
             reads=list(okeys) + ['st2c', 'small'], writes=['on'])
        tv = bankbf(7)[:, 0:512].rearrange("p (a b) -> p a b", a=4)

        def fn(pe):
            ins = None
            for vv in range(4):
                ins = pe.transpose(tv[:, vv, 0:n], on[0:n, vv * 128:(vv + 1) * 128], identb[0:n, 0:n])
            return ins
        P.op('pe', fn, reads=['on', 'cstb'], writes=[('ps', 7)])
        P.op('dve', lambda e: e.tensor_tensor(out=ogT[:, 4 * h:4 * h + 4, c0:c0 + n], in0=tv[:, :, 0:n], in1=gsT[:, :, c0:c0 + n], op=ALU.mult),
             reads=[('ps', 7), 'gsT'], writes=['ogT'])

    def gla_chunk(h, n, c0, kTs, vTs, a_src, akeys, state_only, need_sb=True):
        wcols = wa2b[0:17, 0:256]
        P.op('pe', lambda pe: pe.matmul(bank(0)[0:n, 0:256], lhsT=a_src[0:17, c0:c0 + n], rhs=wcols, start=True, stop=True),
             reads=list(akeys) + ['wa2b'], writes=[('ps', 0)])
        P.op('act', lambda e: e.activation(out=e1[0:n, :], in_=bank(0)[0:n, 0:256], func=AF.Exp, scale=-1.0), reads=[('ps', 0)], writes=['e1'])
        P.op('act', lambda e: e.activation(out=sp_t[0:n, :], in_=e1[0:n, :], func=AF.Ln, bias=1.0), reads=['e1'], writes=['sp'])
        chk('g1')
        bct = psum[:, 512:1024].rearrange("p (a b) -> p a b", a=2)

        def fnb(pe):
            pe.matmul(bank(0)[0:n, 256:512], lhsT=trineg[0:n, 0:n], rhs=sp_t[0:n, :], start=True, stop=True)
            ins = None
            for jj in range(2):
                ins = pe.matmul(bct[:, jj, 0:n], lhsT=sp_t[0:n, jj * 128:(jj + 1) * 128], rhs=trineg[0:n, 0:n], start=True, stop=True)
            return ins
        P.op('pe', fnb, reads=['sp', 'cst', 'e1'], writes=[('ps', 0), ('ps', 1)])
        chk('g2')
        P.op('act', lambda e: e.activation(out=ek_t[0:n, :], in_=bank(0)[0:n, 256:512], func=AF.Exp, scale=-1.0), reads=[('ps', 0)], writes=['ek_t'])
        P.op('act', lambda e: e.activation(out=eqT[:, :, 0:n], in_=bct[:, :, 0:n], func=AF.Exp), reads=[('ps', 1)], writes=['eqT'])
        if not state_only:
            P.op('act', lambda e: e.activation(out=ekT[:, :, 0:n], in_=bct[:, :, 0:n], func=AF.Exp, scale=-1.0), reads=[('ps', 1)], writes=['ekT'])
        chk('g3')
        ktv = bankbf(2)[:, 0:256]
        vtv = bankbf(2)[:, 256:768]

        def fnt(pe):
            ins = None
            for jj in range(2):
                ins = pe.transpose(ktv[0:n, jj * 128:(jj + 1) * 128], kTs[:, jj, c0:c0 + n], identb)
            for vv in range(4):
                ins = pe.transpose(vtv[0:n, vv * 128:(vv + 1) * 128], vTs[:, vv, c0:c0 + n], identb)
            return ins
        P.op('pe', fnt, reads=['kvT', 'cstb'], writes=[('ps', 2)])
        chk('g4')
        P.op('dve', lambda e: e.tensor_tensor(out=kd_t[0:n, :], in0=ktv[0:n, :], in1=ek_t[0:n, :], op=ALU.mult), reads=[('ps', 2), 'ek_t'], writes=['kd_t'])
        chk('g4a')
        P.op('dve', lambda e: e.tensor_copy(out=v_t[0:n, :], in_=vtv[0:n, :]), reads=[('ps', 2)], writes=['v_t'])
        if not state_only:
            P.op('dve', lambda e: e.tensor_tensor(out=qd[:, :, 0:n], in0=qT[:, :, c0:c0 + n], in1=eqT[:, :, 0:n], op=ALU.mult), reads=['qT', 'eqT'], writes=['qd'])
            P.op('dve', lambda e: e.tensor_tensor(out=kdT[:, :, 0:n], in0=kTs[:, :, c0:c0 + n], in1=ekT[:, :, 0:n], op=ALU.mult), reads=['kvT', 'ekT'], writes=['kdT'])

            def fns(pe):
                ins = None
                for jj in range(2):
                    ins = pe.matmul(bank(3)[0:n, 0:n], lhsT=kdT[:, jj, 0:n], rhs=qd[:, jj, 0:n], start=(jj == 0), stop=(jj == 1))
                return ins
            P.op('pe', fns, reads=['qd', 'kdT'], writes=[('ps', 3)])
            P.op('dve', lambda e: e.tensor_tensor(out=scm[0:n, 0:n], in0=bank(3)[0:n, 0:n], in1=maskf[0:n, 0:n], op=ALU.mult), reads=[('ps', 3), 'cst'], writes=['scm'])

            def fno(pe):
                pe.matmul(bank(4)[0:n, :], lhsT=scm[0:n, 0:n], rhs=v_t[0:n, :], start=True, stop=False)
                ins = None
                for jj in range(2):
                    ins = pe.matmul(bank(4)[0:n, :], lhsT=qd[:, jj, 0:n], rhs=Sb[:, jj, :], start=False, stop=(jj == 1))
                return ins
            P.op('pe', fno, reads=['scm', 'v_t', 'qd', 'Sb'], writes=[('ps', 4)])
            gla_post(n, c0, h, bank(4)[0:n, :], [('ps', 4)])
        chk('g5')
        def fnu(pe):
            ins = None
            for jj in range(2):
                ins = pe.matmul(bank(5 + jj), lhsT=kd_t[0:n, jj * 128:(jj + 1) * 128], rhs=v_t[0:n, :], start=True, stop=True)
            return ins
        P.op('pe', fnu, reads=['kd_t', 'v_t'], writes=[('ps', 5), ('ps', 6)])
        for jj in range(2):
            el = eqT[:, jj, n - 1:n]
            P.op('dve', (lambda e, jj=jj: e.tensor_tensor(out=S[:, jj, :], in0=S[:, jj, :], in1=bank(5 + jj), op=ALU.add)),
                 reads=[('ps', 5 + jj), 'S'], writes=['S'])
            P.op('dve', (lambda e, jj=jj, el=el: e.tensor_scalar(out=S[:, jj, :], in0=S[:, jj, :], scalar1=el, scalar2=None, op0=ALU.mult)),
                 reads=['S', 'eqT'], writes=['S'])
        if need_sb:
            P.op('act', lambda e: e.activation(out=Sb.rearrange("p a b -> p (a b)"), in_=S.rearrange("p a b -> p (a b)"), func=AF.Copy), reads=['S'], writes=['Sb'])

    for h in range(4):
        P.dma('sp', (lambda q, h=h: q.dma_start(out=wa2b[0:17, :], in_=wa2b_in[:, h * 256:(h + 1) * 256])), writes=['wa2b'])
        wA, kA = load_w([(w_in[:, OFF['q'] + h * 256:OFF['q'] + (h + 1) * 256], 0), (w_in[:, OFF['k'] + h * 256:OFF['k'] + (h + 1) * 256], 256)], KC, 512)
        for ec in range(4):
            banks, keys = mmF(wA, kA, ec * 128, 128, rx, ['xT'], KC, GROUPS)
            if ec < 2:
                evac('act', GROUPS, banks, keys, 128, lambda c0, c1, ec=ec: qT[:, ec, c0:c1], ['qT'], scale=0.0625)
            else:
                evac('act', GROUPS, banks, keys, 128, lambda c0, c1, ec=ec: kT[:, ec - 2, c0:c1], ['kvT'])
        for ec in range(2, 4):
            banks, keys = mmF(wA, kA, ec * 128, 128, rxp, ['xpT'], KC, PGROUPS)
            evac('dve', PGROUPS, banks, keys, 128, lambda c0, c1, ec=ec: kpT[:, ec - 2, c0:c1], ['kvT'])
        wB, kB = load_w([(w_in[:, OFF['v'] + h * 512:OFF['v'] + (h + 1) * 512], 0)], KC, 512)
        for ec in range(4):
            banks, keys = mmF(wB, kB, ec * 128, 128, rx, ['xT'], KC, GROUPS)
            evac('act', GROUPS, banks, keys, 128, lambda c0, c1, ec=ec: vT[:, ec, c0:c1], ['kvT'])
            banks, keys = mmF(wB, kB, ec * 128, 128, rxp, ['xpT'], KC, PGROUPS)
            evac('dve', PGROUPS, banks, keys, 128, lambda c0, c1, ec=ec: vpT[:, ec, c0:c1], ['kvT'])
        wC, kC = load_w([(w_in[:, OFF['g'] + h * 512:OFF['g'] + (h + 1) * 512], 0)], KC, 512)
        for ec in range(4):
            banks, keys = mmF(wC, kC, ec * 128, 128, rx, ['xT'], KC, GROUPS)
            evac('act', GROUPS, banks, keys, 128, lambda c0, c1, ec=ec: gsT[:, ec, c0:c1], ['gsT'], func=AF.Silu)
        if stop == 'A4p':
            return finish()
        P.op('dve', lambda e: e.memset(S.rearrange("p a b -> p (a b)"), 0.0), writes=['S'])
        P.op('dve', lambda e: e.memset(Sb.rearrange("p a b -> p (a b)"), 0.0), writes=['Sb'])
        for i in range(8):
            gla_chunk(h, 128, i * 128, kpT, vpT, a17p, A17PK, True, need_sb=(i == 7))
            if stop == 'A4pre':
                return finish()
        gla_chunk(h, 2, 18, kT, vT, a17, A17K, False)
        if stop == 'A4h':
            return finish()
        for i in range(8):
            gla_chunk(h, 128, M0 + i * 128, kT, vT, a17, A17K, False, need_sb=(i < 7))
        if stop == 'A4m':
            return finish()
        P.dma('sp', (lambda q, h=h: q.dma_start(out=pgla_out[h].rearrange("(jj p) v -> p jj v", p=128), in_=S)), reads=['S'])
        pre_s = psum[:, 0:2 * NS].rearrange("p (a b) -> p a b", a=2)

        def fna(pe, h=h):
            ins = None
            for jj in range(2):
                ins = pe.matmul(pre_s[:, jj, :], lhsT=wa2b[0:17, jj * 128:(jj + 1) * 128], rhs=a17[0:17, 0:NS], start=True, stop=True)
            return ins
        P.op('pe', fna, reads=A17K + ['wa2b'], writes=[('ps', 0)])
        P.op('act', lambda e: e.activation(out=aTs, in_=pre_s, func=AF.Exp, scale=-1.0), reads=[('ps', 0)], writes=['aTs'])
        P.op('act', lambda e: e.activation(out=aTs, in_=aTs, func=AF.Ln, bias=1.0), reads=['aTs'], writes=['aTs'])
        P.op('act', lambda e: e.activation(out=aTs, in_=aTs, func=AF.Exp, scale=-1.0 / 16.0), reads=['aTs'], writes=['aTs'])
        chk('s1')
        ktv = bankbf(2)[:, 0:256]; vtv = bankbf(2)[:, 256:768]

        def fnts(pe):
            ins = None
            for jj in range(2):
                ins = pe.transpose(ktv[0:NS, jj * 128:(jj + 1) * 128], kT[:, jj, 0:NS], identb)
            for vv in range(4):
                ins = pe.transpose(vtv[0:NS, vv * 128:(vv + 1) * 128], vT[:, vv, 0:NS], identb)
            return ins
        P.op('pe', fnts, reads=['kvT', 'cstb'], writes=[('ps', 2)])
        P.op('dve', lambda e: e.tensor_copy(out=v_t[0:NS, :], in_=vtv[0:NS, :]), reads=[('ps', 2)], writes=['v_t'])
        P.op('dve', lambda e: e.tensor_copy(out=kd_t[0:NS, :], in_=ktv[0:NS, :]), reads=[('ps', 2)], writes=['kd_t'])
        P.op('dve', lambda e: e.memset(QG.rearrange("p s j c -> p (s j c)"), 0.0), writes=['QG'])
        for s in range(NS):
            P.op('dve', (lambda e, s=s: e.tensor_copy(out=QG[:, s, :, s:s + 1], in_=qT[:, :, s:s + 1])), reads=['qT', 'QG'], writes=['QG'])
        chk('s2')
        for s in range(NS):
            i = s % 2
            P.dma('sp', (lambda q, s=s, i=i, h=h: q.dma_start(out=SS[i], in_=sgla_in[s, h].rearrange("(jj p) v -> p jj v", p=128))), writes=[('SS', i), 'S'] if i == 0 else [('SS', i)])
            P.op('dve', (lambda e, s=s, i=i: e.tensor_scalar(out=km[i][0:NS, :], in0=kd_t[0:NS, :], scalar1=ident[0:NS, s:s + 1], scalar2=None, op0=ALU.mult)),
                 reads=['kd_t', 'cst'], writes=[('km', i)])

            def fnu(pe, s=s):
                ins = None
                for jj in range(2):
                    ins = pe.matmul(bank(5 + jj), lhsT=km[s % 2][0:NS, jj * 128:(jj + 1) * 128], rhs=v_t[0:NS, :], start=True, stop=True)
                return ins
            P.op('pe', fnu, reads=[('km', i), 'v_t'], writes=[('ps', 5), ('ps', 6)])
            for jj in range(2):
                P.op('dve', (lambda e, s=s, jj=jj, i=i: e.scalar_tensor_tensor(out=SS[i][:, jj, :], in0=SS[i][:, jj, :], scalar=aTs[:, jj, s:s + 1],
                                                                              in1=bank(5 + jj), op0=ALU.mult, op1=ALU.add)),
                     reads=[('SS', i), 'aTs', ('ps', 5 + jj)], writes=[('SS', i)])
            P.dma('sp', (lambda q, s=s, i=i, h=h: q.dma_start(out=sgla_out[s, h].rearrange("(jj p) v -> p jj v", p=128), in_=SS[i])), reads=[('SS', i)])
            P.op('act', (lambda e, i=i: e.activation(out=SSb[i].rearrange("p a b -> p (a b)"), in_=SS[i].rearrange("p a b -> p (a b)"), func=AF.Copy)),
                 reads=[('SS', i)], writes=['Sb'])

            def fnos(pe, s=s, i=i):
                ins = None
                for jj in range(2):
                    ins = pe.matmul(bank(4)[0:NS, :], lhsT=QG[:, s, jj, :], rhs=SSb[i][:, jj, :], start=(s == 0 and jj == 0), stop=(s == NS - 1 and jj == 1))
                return ins
            P.op('pe', fnos, reads=['QG', 'Sb'], writes=[('ps', 4)])
            if s == 0:
                chk('s3')
        chk('s4')
        gla_post(NS, 0, h, bank(4)[0:NS, :], [('ps', 4)])
    P.op('dve', lambda e: e.memset(ogT[:, :, 16:18], 0.0), reads=['ogT'], writes=['ogT'])
    P.barrier(pool=False)

    if stop == 'A4':
        return finish()
    SC = Arena(SCR0, NW)
    zT = r3(SC.f32(8 * NT), 8)
    scT = SC.f32(8 * 2 * NS).rearrange("p (j r s) -> p j r s", j=8, r=2)
    ctmp = SC.f32(NT); ctmp2 = SC.f32(NS)
    scv = scT_in.rearrange("(j p) r s -> p j r s", p=128)
    for j0 in (0, 4):
        P.dma('sp', (lambda q, j0=j0: q.dma_start(out=scT[:, j0:j0 + 4], in_=scv[:, j0:j0 + 4])), writes=['scT'])
    P.dma('sp', lambda q: q.dma_start(out=sconv_out[:, 0, :], in_=scT_in[:, 1, :]))
    for which in ('cb', 'cc', 'ch'):
        for blk in range(2):
            wbuf, wkey = load_w([(w_in[:, OFF[which] + blk * 512:OFF[which] + (blk + 1) * 512], 0)], KC, 512)
            for ec in range(4):
                j = blk * 4 + ec
                banks, keys = mmF(wbuf, wkey, ec * 128, 128, rx, ['xT'], KC, GROUPS)
                if which == 'cb':
                    evac('act', GROUPS, banks, keys, 128, lambda c0, c1, j=j: buT[:, j, c0:c1], [('buT', j)])
                elif which == 'cc':
                    evac('act', GROUPS, banks, keys, 128, lambda c0, c1, j=j: zT[:, j, c0:c1], [('zT', j)])
                else:
                    for gi, (c0, c1) in enumerate(GROUPS):
                        P.op('dve', (lambda e, j=j, c0=c0, c1=c1, b=banks[gi]: e.tensor_tensor(
                            out=zT[:, j, c0:c1], in0=bank(b)[:, 0:c1 - c0], in1=zT[:, j, c0:c1], op=ALU.mult)),
                            reads=[keys[gi], ('zT', j)], writes=[('zT', j)])
    for j in range(8):
        w0 = wconv[:, j * 3 + 0:j * 3 + 1]; w1 = wconv[:, j * 3 + 1:j * 3 + 2]; w2 = wconv[:, j * 3 + 2:j * 3 + 3]
        L = NT - 18
        u = ctmp[:, 0:L]
        P.op('dve', (lambda e, j=j, w0=w0: e.tensor_scalar(out=u, in0=zT[:, j, 16:16 + L], scalar1=w0, scalar2=None, op0=ALU.mult)),
             reads=[('zT', j), 'small'], writes=['ctmp'])
        P.op('dve', (lambda e, j=j, w1=w1: e.scalar_tensor_tensor(out=u, in0=zT[:, j, 17:17 + L], scalar=w1, in1=u, op0=ALU.mult, op1=ALU.add)),
             reads=[('zT', j), 'ctmp'], writes=['ctmp'])
        P.op('dve', (lambda e, j=j, w2=w2: e.scalar_tensor_tensor(out=u, in0=zT[:, j, 18:18 + L], scalar=w2, in1=u, op0=ALU.mult, op1=ALU.add)),
             reads=[('zT', j), 'ctmp'], writes=['ctmp'])
        P.op('dve', (lambda e, j=j: e.tensor_tensor(out=buT[:, j, 18:NT], in0=buT[:, j, 18:NT], in1=u, op=ALU.mult)),
             reads=[('buT', j), 'ctmp'], writes=[('buT', j)])
        us = ctmp2[:, 0:NS]
        P.op('dve', (lambda e, j=j, w0=w0: e.tensor_scalar(out=us, in0=scT[:, j, 0, :], scalar1=w0, scalar2=None, op0=ALU.mult)),
             reads=['scT', 'small'], writes=['ctmp2'])
        P.op('dve', (lambda e, j=j, w1=w1: e.scalar_tensor_tensor(out=us, in0=scT[:, j, 1, :], scalar=w1, in1=us, op0=ALU.mult, op1=ALU.add)),
             reads=['scT', 'ctmp2'], writes=['ctmp2'])
        P.op('dve', (lambda e, j=j, w2=w2: e.scalar_tensor_tensor(out=us, in0=zT[:, j, 0:NS], scalar=w2, in1=us, op0=ALU.mult, op1=ALU.add)),
             reads=[('zT', j), 'ctmp2'], writes=['ctmp2'])
        P.op('dve', (lambda e, j=j: e.tensor_tensor(out=buT[:, j, 0:NS], in0=buT[:, j, 0:NS], in1=us, op=ALU.mult)),
             reads=[('buT', j), 'ctmp2'], writes=[('buT', j)])
        P.op('dve', (lambda e, j=j: e.memset(buT[:, j, 16:18], 0.0)), writes=[('buT', j)])
    zk = [('zT', j) for j in range(8)]
    P.dma('sp', lambda q: q.dma_start(out=pconv_out.rearrange("(j p) r -> p j r", p=128), in_=zT[:, :, NT - 2:NT]), reads=zk)
    P.dma('sp', lambda q: q.dma_start(out=sconv_out[:, 1, :].rearrange("(j p) s -> p j s", p=128), in_=zT[:, :, 0:NS]), reads=zk)
    P.barrier()

    if stop == 'A2':
        return finish()
    SC.reset()
    MEM_OFF = SC.off
    memT = r3(SC.bf(KC * 256), KC)
    mkT = r3(SC.bf(8 * 256), 8)
    mvb = r3(SC.bf(2 * 1024), 2)
    mqT = r3(xpT_or_merged[:, 0:8 * NT], 8)
    Pf = SC.f32(1024); Pn = SC.bf(1024); PT = SC.bf(1024)
    mtok = Pf[:, 0:512]
    st4 = SC.f32(16)
    memv = memT_in.rearrange("(k p) m -> p k m", p=128)
    P.dma('pool', lambda q: q.dma_start(out=memT, in_=memv), writes=['memT'])
    rmem = lambda kc, c0, c1: memT[:, kc, c0:c1]
    tb = [0]

    def nbank():
        b = tb[0] % 6; tb[0] += 1
        return b
    for (wsrc, dst, isk) in ((w_mem_k, pmk_out, True), (w_mem_v, pmv_out, False)):
        for blk in range(2):
            wbuf, wkey = load_w([(wsrc[:, blk * 512:(blk + 1) * 512], 0)], KC, 512)
            for mt in range(2):
                b = nbank()

                def fn(pe, b=b, mt=mt, wbuf=wbuf):
                    ins = None
                    for kc in range(KC):
                        ins = pe.matmul(bank(b), lhsT=memT[:, kc, mt * 128:(mt + 1) * 128], rhs=wbuf[:, kc, :], start=(kc == 0), stop=(kc == KC - 1))
                    return ins
                P.op('pe', fn, reads=[wkey, 'memT'], writes=[('ps', b)])
                P.op('act', (lambda e, b=b: e.activation(out=mtok, in_=bank(b), func=AF.Copy)), reads=[('ps', b)], writes=['Pf'])
                P.dma('sp', (lambda q, dst=dst, mt=mt, blk=blk: q.dma_start(out=dst[mt * 128:(mt + 1) * 128, blk * 512:(blk + 1) * 512], in_=mtok)),
                      reads=['Pf'])
                if not isk:
                    P.op('dve', (lambda e, mt=mt, blk=blk: e.tensor_copy(out=mvb[:, mt, blk * 512:(blk + 1) * 512], in_=mtok)),
                         reads=['Pf'], writes=['mvb'])
                chk('m0')
            chk('t%d%d' % (int(isk), blk))
            if isk:
                for ec in range(4):
                    j = blk * 4 + ec
                    banks, keys = mmF(wbuf, wkey, ec * 128, 128, rmem, ['memT'], KC, [(0, 256)])
                    evac('dve', [(0, 256)], banks, keys, 128, lambda c0, c1, j=j: mkT[:, j, c0:c1], ['mkT'])
                    chk('m0c')
                chk('f%d' % blk)
        chk('m0e' if isk else 'm0f')
    chk('m1')
    for blk in range(2):
        wbuf, wkey = load_w([(w_in[:, OFF['mq'] + blk * 512:OFF['mq'] + (blk + 1) * 512], 0)], KC, 512)
        for ec in range(4):
            j = blk * 4 + ec
            banks, keys = mmF(wbuf, wkey, ec * 128, 128, rx, ['xT'], KC, GROUPS)
            evac('act', GROUPS, banks, keys, 128, lambda c0, c1, j=j: mqT[:, j, c0:c1], ['mqT'], scale=0.0625)

    chk('m2')

    def softmax_rows(n, lg_ap3, rkeys):
        mx = st4[0:n, 0:4]; nmx = st4[0:n, 4:8]; ssum = st4[0:n, 8:12]; rs = st4[0:n, 12:16]
        Pf3 = r3(Pf, 4); Pn3 = r3(Pn, 4)
        chk('x0')
        P.op('dve', lambda e: e.tensor_reduce(out=mx, in_=lg_ap3, axis=AX.X, op=ALU.max), reads=rkeys, writes=['st4'])
        chk('x1')
        P.op('dve', lambda e: e.tensor_scalar(out=nmx, in0=mx, scalar1=-1.0, scalar2=None, op0=ALU.mult), reads=['st4'], writes=['st4'])
        for h in range(4):
            P.op('act', (lambda e, h=h: e.activation(out=Pf3[0:n, h, :], in_=lg_ap3[:, h, :], func=AF.Exp, bias=nmx[:, h:h + 1],
                                                     accum_out=ssum[:, h:h + 1])), reads=list(rkeys) + ['st4'], writes=['Pf', ('ssum', h)])
            chk('x3')
        P.op('dve', lambda e: e.reciprocal(out=rs, in_=ssum), reads=[('ssum', h) for h in range(4)] + ['st4'], writes=['st4'])
        chk('x4')
        for h in range(4):
            P.op('dve', (lambda e, h=h: e.tensor_scalar(out=Pn3[0:n, h, :], in0=Pf3[0:n, h, :], scalar1=rs[:, h:h + 1], scalar2=None, op0=ALU.mult)),
                 reads=['Pf', 'st4'], writes=['Pn'])

    def attn_pv(n, c0, vsrc_fn, vkeys):
        Pn3 = r3(Pn, 4)
        ptv = bankbf(6)[:, 0:8 * 128].rearrange("p (a b) -> p a b", a=8)

        def fn(pe):
            ins = None
            for h in range(4):
                for mc in range(2):
                    ins = pe.transpose(ptv[:, h * 2 + mc, 0:n], Pn3[0:n, h, mc * 128:(mc + 1) * 128], identb[0:n, 0:n])
            return ins
        P.op('pe', fn, reads=['Pn', 'cstb'], writes=[('ps', 6)])
        PT3 = r3(PT, 8)
        P.op('dve', lambda e: e.tensor_copy(out=PT3[:, :, 0:n], in_=ptv[:, :, 0:n]), reads=[('ps', 6)], writes=['PT'])
        ov = psum[:, 0:1024].rearrange("p (a b) -> p a b", a=8)

        def fn2(pe):
            ins = None
            for j in range(8):
                for mc in range(2):
                    ins = pe.matmul(ov[:, j, 0:n], lhsT=vsrc_fn(mc, j), rhs=PT3[:, (j // 2) * 2 + mc, 0:n], start=(mc == 0), stop=(mc == 1))
            return ins
        P.op('pe', fn2, reads=['PT'] + list(vkeys), writes=[('ps', 0), ('ps', 1)])
        P.op('dve', lambda e: e.tensor_copy(out=omT[:, :, c0:c0 + n], in_=ov[:, :, 0:n]), reads=[('ps', 0), ('ps', 1)], writes=['omT'])

    lg3 = psum[:, 2 * 512:4 * 512].rearrange("p (a b) -> p a b", a=4)
    for (c0, n) in [(H0, 4)] + TILES[1:]:
        def fnl(pe, c0=c0, n=n):
            ins = None
            for h in range(4):
                for jj in range(2):
                    ins = pe.matmul(lg3[0:n, h, :], lhsT=mqT[:, 2 * h + jj, c0:c0 + n], rhs=mkT[:, 2 * h + jj, :], start=(jj == 0), stop=(jj == 1))
            return ins
        P.op('pe', fnl, reads=['mqT', 'mkT'], writes=[('ps', 2), ('ps', 3)])
        softmax_rows(n, lg3[0:n], [('ps', 2), ('ps', 3)])
        chk('m3')
        attn_pv(n, c0, lambda mc, j: mvb[:, mc, j * 128:(j + 1) * 128], ['mvb'])
        chk('m4')
    chk('m5')
    P.barrier()
    QS = [SC.bf(8 * 64).rearrange("p (j c) -> p j c", j=8) for _ in range(2)]
    PTs = SC.bf(2 * 64)
    SK = Arena(MEM_OFF, MEM_OFF + KC * 256 // 2)
    KS = [r3(SK.bf(8 * 256), 8) for _ in range(2)]
    VS = [r3(KS[i_].rearrange("p a b -> p (a b)"), 2) for i_ in range(2)]
    lgs = bank(4)[0:64, 0:256]
    for s in range(NS):
        i = s % 2
        P.dma('pool', (lambda q, s=s, i=i: q.dma_start(out=KS[i], in_=kcT_in[s].rearrange("(j p) m -> p j m", p=128))), writes=[('KS', i)])
        P.op('dve', (lambda e, i=i: e.memset(QS[i].rearrange("p j c -> p (j c)"), 0.0)), writes=[('QS', i)])
        for j in range(8):
            P.op('dve', (lambda e, s=s, j=j, i=i: e.tensor_copy(out=QS[i][:, j, s * 4 + j // 2:s * 4 + j // 2 + 1], in_=mqT[:, j, s:s + 1])),
                 reads=['mqT', ('QS', i)], writes=[('QS', i)])

        def fnk(pe, s=s, i=i):
            ins = None
            for j in range(8):
                ins = pe.matmul(lgs, lhsT=QS[i][:, j, :], rhs=KS[i][:, j, :], start=(s == 0 and j == 0), stop=(s == NS - 1 and j == 7))
            return ins
        P.op('pe', fnk, reads=[('QS', i), ('KS', i)], writes=[('ps', 4)])
    chk('m6')
    smx = st4[0:64, 0:1]; snm = st4[0:64, 1:2]; ssm = st4[0:64, 2:3]; srs = st4[0:64, 3:4]
    P.op('dve', lambda e: e.tensor_reduce(out=smx, in_=lgs, axis=AX.X, op=ALU.max), reads=[('ps', 4)], writes=['st4'])
    P.op('dve', lambda e: e.tensor_scalar(out=snm, in0=smx, scalar1=-1.0, scalar2=None, op0=ALU.mult), reads=['st4'], writes=['st4'])
    P.op('act', lambda e: e.activation(out=Pf[0:64, 0:256], in_=lgs, func=AF.Exp, bias=snm, accum_out=ssm), reads=[('ps', 4), 'st4'], writes=['Pf', 'ssm'])
    P.op('dve', lambda e: e.reciprocal(out=srs, in_=ssm), reads=['ssm', 'st4'], writes=['st4'])
    P.op('dve', lambda e: e.tensor_scalar(out=Pn[0:64, 0:256], in0=Pf[0:64, 0:256], scalar1=srs, scalar2=None, op0=ALU.mult), reads=['Pf', 'st4'], writes=['Pn'])
    chk('m7')
    ptsv = bankbf(6)[:, 0:128].rearrange("p (a b) -> p a b", a=2)

    def fnt(pe):
        ins = None
        for mc in range(2):
            ins = pe.transpose(ptsv[:, mc, :], Pn[0:64, mc * 128:(mc + 1) * 128], identb[0:64, 0:64])
        return ins
    P.op('pe', fnt, reads=['Pn', 'cstb'], writes=[('ps', 6)])
    PTs3 = r3(PTs, 2)
    P.op('dve', lambda e: e.tensor_copy(out=PTs3, in_=ptsv), reads=[('ps', 6)], writes=['PTs'])
    ovs = bank(5)[:, 0:128].rearrange("p (a b) -> p a b", a=8)
    for s in range(NS):
        i = s % 2
        P.dma('pool', (lambda q, s=s, i=i: q.dma_start(out=VS[i], in_=vc_in[s].rearrange("(mc p) d -> p mc d", p=128))), writes=[('KS', i)])

        def fnv(pe, s=s, i=i):
            ins = None
            for j in range(8):
                for mc in range(2):
                    col = s * 4 + j // 2
                    ins = pe.matmul(ovs[:, j, s:s + 1], lhsT=VS[i][:, mc, j * 128:(j + 1) * 128], rhs=PTs3[:, mc, col:col + 1],
                                    start=(mc == 0), stop=(mc == 1))
            return ins
        P.op('pe', fnv, reads=['PTs', ('KS', i)], writes=[('ps', 5)])
    P.op('dve', lambda e: e.tensor_copy(out=omT[:, :, 0:NS], in_=ovs), reads=[('ps', 5)], writes=['omT'])
    P.op('dve', lambda e: e.memset(omT[:, :, 16:18], 0.0), writes=['omT'])
    P.barrier(pool=False)

    if stop == 'A3':
        return finish()
    SC.reset()
    G = r3(SC.f32(4 * NT), 4); Mg = r3(SC.f32(4 * NT), 4); mt_ = SC.f32(NT)
    rog = lambda kc, c0, c1: ogT[:, kc, c0:c1]
    rbu = lambda kc, c0, c1: buT[:, kc, c0:c1]
    rom = lambda kc, c0, c1: omT[:, kc, c0:c1]
    for blk in range(4):
        cs = slice(blk * 512, (blk + 1) * 512)
        for bi, (zname, wsrc, nk, rfn, rkey) in enumerate((('za', w_gla_out, 16, rog, 'ogT'), ('zb', w_conv_out, 8, rbu, 'buT'), ('zm', w_mem_out, 8, rom, 'omT'))):
            wz, kz = load_w([(w_in[:, OFF[zname] + blk * 512:OFF[zname] + (blk + 1) * 512], 0)], KC, 512)
            for ec in range(4):
                banks, keys = mmF(wz, kz, ec * 128, 128, rx, ['xT'], KC, GROUPS)
                evac('act', GROUPS, banks, keys, 128, lambda c0, c1, ec=ec: G[:, ec, c0:c1], [('G', ec)], func=AF.Sigmoid)
            wy, ky = load_w([(wsrc[:, cs], 0)], nk, 512)
            for ec in range(4):
                banks, keys = mmF(wy, ky, ec * 128, 128, rfn, [rkey] if rkey != 'buT' else [('buT', j) for j in range(8)], nk, GROUPS)
                for gi, (c0, c1) in enumerate(GROUPS):
                    src = bank(banks[gi])[:, 0:c1 - c0]
                    if bi == 0:
                        P.op('dve', (lambda e, ec=ec, c0=c0, c1=c1, src=src: e.tensor_tensor(out=Mg[:, ec, c0:c1], in0=src, in1=G[:, ec, c0:c1], op=ALU.mult)),
                             reads=[keys[gi], ('G', ec)], writes=[('Mg', ec)])
                    else:
                        P.op('dve', (lambda e, ec=ec, c0=c0, c1=c1, src=src: e.tensor_tensor(out=mt_[:, c0:c1], in0=src, in1=G[:, ec, c0:c1], op=ALU.mult)),
                             reads=[keys[gi], ('G', ec)], writes=['mt_'])
                        dst = Mg[:, ec, c0:c1] if bi == 1 else mergedT[:, blk * 4 + ec, c0:c1]
                        P.op('dve', (lambda e, ec=ec, c0=c0, c1=c1, dst=dst: e.tensor_tensor(out=dst, in0=Mg[:, ec, c0:c1], in1=mt_[:, c0:c1], op=ALU.add)),
                             reads=['mt_', ('Mg', ec)], writes=[('Mg', ec), 'mergedT'])
    P.barrier(pool=False)
    if stop == 'A5':
        return finish()
    A2 = Arena(OG_OFF, NW)
    r1 = A2.f32(9 * D).rearrange("p (t d) -> p t d", t=9)
    lng = A2.f32(D); lnb = A2.f32(D)
    xres = [A2.f32(512) for _ in range(2)]
    x1b = A2.bf(D)
    bst = A2.f32(4 * 6); mvv = A2.f32(4)
    x1T = xT
    P.dma('sp', lambda q, lng=lng: q.dma_start(out=lng, in_=ln_in[0]), writes=['lng'])
    P.dma('sp', lambda q, lnb=lnb: q.dma_start(out=lnb, in_=ln_in[1]), writes=['lnb'])

    def layer_norm_tile(ti, n, src, keyr, gkey, bkey, lng, lnb, bst, mvv):
        for c in range(4):
            P.op('dve', (lambda e, c=c: e.bn_stats(out=bst[0:n, c * 6:(c + 1) * 6], in_=src[0:n, c * 512:(c + 1) * 512])), reads=[keyr], writes=['bst'])
        P.op('dve', lambda e: e.bn_aggr(out=mvv[0:n, 0:2], in_=bst[0:n, :]), reads=['bst'], writes=['mvv'])
        P.op('act', lambda e: e.activation(out=mvv[0:n, 2:3], in_=mvv[0:n, 1:2], func=AF.Ln, bias=1e-5), reads=['mvv'], writes=['mvv2'])
        P.op('act', lambda e: e.activation(out=mvv[0:n, 3:4], in_=mvv[0:n, 2:3], func=AF.Exp, scale=-0.5), reads=['mvv2'], writes=['mvv3'])
        P.op('dve', lambda e: e.tensor_scalar(out=src[0:n, :], in0=src[0:n, :], scalar1=mvv[0:n, 0:1], scalar2=mvv[0:n, 3:4], op0=ALU.subtract, op1=ALU.mult),
             reads=[keyr, 'mvv', 'mvv3'], writes=[keyr])
        P.op('dve', lambda e: e.tensor_tensor(out=src[0:n, :], in0=src[0:n, :], in1=lng[0:n, :], op=ALU.mult), reads=[keyr, gkey], writes=[keyr])
        P.op('dve', lambda e: e.tensor_tensor(out=src[0:n, :], in0=src[0:n, :], in1=lnb[0:n, :], op=ALU.add), reads=[keyr, bkey], writes=[keyr])

    xi = [0]
    for blk in range(4):
        wo, ko = load_w([(w_o[:, blk * 512:(blk + 1) * 512], 0)], KC, 512)
        for ti, (c0, n) in enumerate(TILES):
            b = nbank()

            def fn(pe, b=b, c0=c0, n=n, wo=wo):
                ins = None
                for kc in range(KC):
                    ins = pe.matmul(bank(b)[0:n, :], lhsT=mergedT[:, kc, c0:c0 + n], rhs=wo[:, kc, :], start=(kc == 0), stop=(kc == KC - 1))
                return ins
            P.op('pe', fn, reads=[ko, 'mergedT'], writes=[('ps', b)])
            xr = xi[0] % 2; xi[0] += 1
            P.dma('sp', (lambda q, xr=xr, c0=c0, n=n, blk=blk: q.dma_start(out=xres[xr][0:n, :], in_=x_tok[c0:c0 + n, blk * 512:(blk + 1) * 512])), writes=[('xres', xr)])
            P.op('dve', (lambda e, b=b, n=n, ti=ti, blk=blk, xr=xr: e.scalar_tensor_tensor(out=r1[0:n, ti, blk * 512:(blk + 1) * 512], in0=xres[xr][0:n, :], scalar=ALPHA,
                                                                                         in1=bank(b)[0:n, :], op0=ALU.mult, op1=ALU.add)),
                 reads=[('ps', b), ('xres', xr)], writes=[('r1', ti)])
    for ti, (c0, n) in enumerate(TILES):
        layer_norm_tile(ti, n, r1[:, ti, :], ('r1', ti), 'lng', 'lnb', lng, lnb, bst, mvv)
        P.dma('sp', (lambda q, ti=ti, c0=c0, n=n: q.dma_start(out=x1_scr[c0:c0 + n, :], in_=r1[0:n, ti, :])), reads=[('r1', ti)], writes=[('x1scr', ti)])
        P.op('act', (lambda e, ti=ti, n=n: e.activation(out=x1b[0:n, :], in_=r1[0:n, ti, :], func=AF.Copy)), reads=[('r1', ti)], writes=['x1b'])
        tv = bankbf(6, 2)[:, 0:KC * 128].rearrange("p (a b) -> p a b", a=KC)

        def fn(pe, n=n):
            ins = None
            for kc in range(KC):
                ins = pe.transpose(tv[:, kc, 0:n], x1b[0:n, kc * 128:(kc + 1) * 128], identb[0:n, 0:n])
            return ins
        P.op('pe', fn, reads=['x1b', 'cstb'], writes=[('ps', 6), ('ps', 7)])
        P.op('dve', (lambda e, c0=c0, n=n: e.tensor_copy(out=x1T[:, :, c0:c0 + n], in_=tv[:, :, 0:n])), reads=[('ps', 6), ('ps', 7)], writes=['x1T'])
    P.barrier(pool=False)

    if stop == 'LN1':
        return finish()
    A3 = Arena(WB_OFF, NW)
    WG = [A3.bf(KC * 512) for _ in range(2)]
    hT = r3(A3.bf(FC * NT), FC)
    HT_END = A3.off
    AM_ = Arena(1500 + KC * NT // 2, WB_OFF)
    hgx = AM_.f32(NT); hc = AM_.f32(NT); hs = AM_.f32(NT)
    sfT = AM_.f32(FC * 2 * NS).rearrange("p (j r s) -> p j r s", j=FC, r=2)
    pfc = r3(AM_.f32(FC * 2), FC); sfc = r3(AM_.f32(FC * NS), FC)
    sfv = sfT_in.rearrange("(j p) r s -> p j r s", p=128)
    for j0 in range(0, FC, 4):
        j1 = min(FC, j0 + 4)
        P.dma('sp', (lambda q, j0=j0, j1=j1: q.dma_start(out=sfT[:, j0:j1], in_=sfv[:, j0:j1])), writes=['sfT'])
    for r0 in range(0, DFF, 1376):
        P.dma('sp', (lambda q, r0=r0: q.dma_start(out=sffn_out[r0:r0 + 1376, 0, :], in_=sfT_in[r0:r0 + 1376, 1, :])))
    P.op('dve', lambda e: e.memset(hs[:, 16:20], 0.0), writes=['hs'])
    rx1 = lambda kc, c0, c1: x1T[:, kc, c0:c1]
    wg_i = [0]

    def load_wg(src):
        i = wg_i[0]; wg_i[0] ^= 1
        w = src.shape[1]
        buf = WG[i][:, 0:KC * w].rearrange("p (k e) -> p k e", k=KC)
        sv = src.rearrange("(k p) e -> p k e", p=128)
        for (k0, k1) in ((0, 8), (8, 16)):
            P.dma('pool', (lambda q, o=buf[:, k0:k1, :], s=sv[:, k0:k1, :]: q.dma_start(out=o, in_=s)), writes=[('W', i)])
        return buf, ('W', i)
    for fb in range(11):
        f0 = fb * 512; fw = min(512, DFF - f0)
        wg, kg = load_wg(w_gate[:, f0:f0 + fw])
        wu, ku = load_wg(w_up[:, f0:f0 + fw])
        for ec in range(fw // 128):
            j = fb * 4 + ec
            w0 = wfc[:, j * 4 + 0:j * 4 + 1]; w1 = wfc[:, j * 4 + 1:j * 4 + 2]; w2 = wfc[:, j * 4 + 2:j * 4 + 3]; bb = wfc[:, j * 4 + 3:j * 4 + 4]
            banks, keys = mmF(wg, kg, ec * 128, 128, rx1, ['x1T'], KC, GROUPS)
            evac('act', GROUPS, banks, keys, 128, lambda c0, c1: hgx[:, c0:c1], ['hgx'])
            P.op('dve', lambda e: e.tensor_scalar(out=hgx[:, 18:20], in0=hgx[:, 18:20], scalar1=flag, scalar2=None, op0=ALU.mult), reads=['hgx', 'small'], writes=['hgx'])
            P.op('dve', (lambda e, w0=w0, bb=bb: e.tensor_scalar(out=hc[:, M0:NT], in0=hgx[:, M0 - 2:NT - 2], scalar1=w0, scalar2=bb, op0=ALU.mult, op1=ALU.add)),
                 reads=['hgx', 'small'], writes=['hc'])
            P.op('dve', (lambda e, w1=w1: e.scalar_tensor_tensor(out=hc[:, M0:NT], in0=hgx[:, M0 - 1:NT - 1], scalar=w1, in1=hc[:, M0:NT], op0=ALU.mult, op1=ALU.add)),
                 reads=['hgx', 'hc'], writes=['hc'])
            P.op('dve', (lambda e, w2=w2: e.scalar_tensor_tensor(out=hc[:, M0:NT], in0=hgx[:, M0:NT], scalar=w2, in1=hc[:, M0:NT], op0=ALU.mult, op1=ALU.add)),
                 reads=['hgx', 'hc'], writes=['hc'])
            P.op('dve', (lambda e, w0=w0, bb=bb, j=j: e.tensor_scalar(out=hc[:, 0:NS], in0=sfT[:, j, 0, :], scalar1=w0, scalar2=bb, op0=ALU.mult, op1=ALU.add)),
                 reads=['sfT', 'small', 'hc'], writes=['hc'])
            P.op('dve', (lambda e, w1=w1, j=j: e.scalar_tensor_tensor(out=hc[:, 0:NS], in0=sfT[:, j, 1, :], scalar=w1, in1=hc[:, 0:NS], op0=ALU.mult, op1=ALU.add)),
                 reads=['sfT', 'hc'], writes=['hc'])
            P.op('dve', (lambda e, w2=w2: e.scalar_tensor_tensor(out=hc[:, 0:NS], in0=hgx[:, 0:NS], scalar=w2, in1=hc[:, 0:NS], op0=ALU.mult, op1=ALU.add)),
                 reads=['hgx', 'hc'], writes=['hc'])
            P.op('dve', (lambda e, j=j: e.tensor_copy(out=pfc[:, j, :], in_=hgx[:, NT - 2:NT])), reads=['hgx'], writes=['pfc'])
            P.op('dve', (lambda e, j=j: e.tensor_copy(out=sfc[:, j, :], in_=hgx[:, 0:NS])), reads=['hgx'], writes=['sfc'])
            P.op('act', lambda e: e.activation(out=hs[:, M0:NT], in_=hc[:, M0:NT], func=AF.Silu), reads=['hc'], writes=['hs'])
            P.op('act', lambda e: e.activation(out=hs[:, 0:NS], in_=hc[:, 0:NS], func=AF.Silu), reads=['hc'], writes=['hs'])
            banks, keys = mmF(wu, ku, ec * 128, 128, rx1, ['x1T'], KC, GROUPS)
            for gi, (c0, c1) in enumerate(GROUPS):
                P.op('dve', (lambda e, j=j, c0=c0, c1=c1, b=banks[gi]: e.tensor_tensor(out=hT[:, j, c0:c1], in0=bank(b)[:, 0:c1 - c0], in1=hs[:, c0:c1], op=ALU.mult)),
                     reads=[keys[gi], 'hs'], writes=['hT'])
    pfv = pffn_out.rearrange("(j p) r -> p j r", p=128)
    sfov = sffn_out[:, 1, :].rearrange("(j p) s -> p j s", p=128)
    for j0 in range(0, FC, 8):
        j1 = min(FC, j0 + 8)
        P.dma('sp', (lambda q, j0=j0, j1=j1: q.dma_start(out=pfv[:, j0:j1], in_=pfc[:, j0:j1])), reads=['pfc'])
        P.dma('sp', (lambda q, j0=j0, j1=j1: q.dma_start(out=sfov[:, j0:j1], in_=sfc[:, j0:j1])), reads=['sfc'])
    P.barrier()
    if stop == 'B1':
        return finish()
    AX_ = Arena(0, WB_OFF + 2 * (KC * 512 // 2))
    r2 = [AX_.f32(D) for _ in range(7)]
    WD0_OFF = AX_.off
    WD = [r3(AX_.bf(FC * 256), FC) for _ in range(2)]
    xres2 = [AX_.f32(256) for _ in range(2)]
    bst = AX_.f32(24); mvv = AX_.f32(4)
    AY_ = Arena(HT_END, NW)
    r2 += [AY_.f32(D) for _ in range(2)]
    AL_ = Arena(WD0_OFF, WD0_OFF + 2 * D)
    lng = AL_.f32(D); lnb = AL_.f32(D)
    wdv = w_down.rearrange("(k p) e -> p k e", p=128)
    for blk in range(8):
        i = blk % 2
        for (k0, k1) in ((0, 22), (22, FC)):
            P.dma('pool', (lambda q, i=i, k0=k0, k1=k1, blk=blk: q.dma_start(out=WD[i][:, k0:k1, :], in_=wdv[:, k0:k1, blk * 256:(blk + 1) * 256])), writes=[('WD', i)])
        for ti, (c0, n) in enumerate(TILES):
            b = nbank()

            def fn(pe, b=b, c0=c0, n=n, i=i):
                ins = None
                for kc in range(FC):
                    ins = pe.matmul(bank(b)[0:n, 0:256], lhsT=hT[:, kc, c0:c0 + n], rhs=WD[i][:, kc, :], start=(kc == 0), stop=(kc == FC - 1))
                return ins
            P.op('pe', fn, reads=[('WD', i), 'hT'], writes=[('ps', b)])
            xr = xi[0] % 2; xi[0] += 1
            P.dma('sp', (lambda q, xr=xr, c0=c0, n=n, blk=blk: q.dma_start(out=xres2[xr][0:n, :], in_=x1_scr[c0:c0 + n, blk * 256:(blk + 1) * 256])),
                  reads=[('x1scr', ti)], writes=[('xres2', xr)])
            P.op('dve', (lambda e, b=b, n=n, ti=ti, blk=blk, xr=xr: e.scalar_tensor_tensor(out=r2[ti][0:n, blk * 256:(blk + 1) * 256], in0=xres2[xr][0:n, :], scalar=ALPHA,
                                                                                         in1=bank(b)[0:n, 0:256], op0=ALU.mult, op1=ALU.add)),
                 reads=[('ps', b), ('xres2', xr)], writes=[('r2', ti)])
    P.barrier()
    P.dma('sp', lambda q, lng=lng: q.dma_start(out=lng, in_=ln_in[2]), writes=['lng2'])
    P.dma('sp', lambda q, lnb=lnb: q.dma_start(out=lnb, in_=ln_in[3]), writes=['lnb2'])
    for ti, (c0, n) in enumerate(TILES):
        layer_norm_tile(ti, n, r2[ti], ('r2', ti), 'lng2', 'lnb2', lng, lnb, bst, mvv)
        P.dma('sp', (lambda q, ti=ti, c0=c0, n=n: q.dma_start(out=y_out[c0:c0 + n, :], in_=r2[ti][0:n, :])), reads=[('r2', ti)])
    return finish()


_NC_CACHE = {}


def _consts():
    c = np.zeros((128, 512), np.float32)
    c[:, 0:128] = np.eye(128, dtype=np.float32)
    m = np.triu(np.ones((128, 128), np.float32))
    c[:, 128:256] = m
    c[:, 256:384] = m * (-1.0 / 16.0)
    return c


def kernel(x_prompt, x_sample, mem_prompt, cache_mem_k, cache_mem_v, state_gla, state_conv, state_ffn_conv,
           w_in, w_gla_a2, b_gla_a2, g_gla_norm, w_gla_out, w_conv, w_conv_out, w_mem_k, w_mem_v, w_mem_out,
           w_o, ln1_g, ln1_b, w_ffn_gate, w_ffn_up, w_ffn_conv, b_ffn_conv, w_ffn_down, ln2_g, ln2_b):
    f = lambda a: np.ascontiguousarray(np.asarray(a, dtype=np.float32))
    x_prompt = f(x_prompt); x_sample = f(x_sample); mem_prompt = f(mem_prompt)
    cache_mem_k = f(cache_mem_k); cache_mem_v = f(cache_mem_v); state_gla = f(state_gla)
    state_conv = f(state_conv); state_ffn_conv = f(state_ffn_conv)
    if 'nc' not in _NC_CACHE:
        _NC_CACHE['nc'] = build_program()
    nc = _NC_CACHE['nc']
    shared = {
        "consts": _consts(), "ones_row": np.ones((1, NT), np.float32),
        "w_in": f(w_in[0]), "wa2b": f(np.concatenate([np.asarray(w_gla_a2[0]), np.asarray(b_gla_a2[0])[None, :]], axis=0)),
        "gnorm_b": f(np.broadcast_to(np.asarray(g_gla_norm[0])[None, :], (128, 512))),
        "w_gla_out": f(w_gla_out[0]),
        "wconv_p": f(np.asarray(w_conv[0]).reshape(3, 8, 128).transpose(2, 1, 0).reshape(128, 24)),
        "w_conv_out": f(w_conv_out[0]), "w_mem_k": f(w_mem_k[0]), "w_mem_v": f(w_mem_v[0]), "w_mem_out": f(w_mem_out[0]),
        "w_o": f(w_o[0]),
        "ln_b": f(np.stack([np.broadcast_to(np.asarray(v[0])[None, :], (128, D)) for v in (ln1_g, ln1_b, ln2_g, ln2_b)])),
        "w_ffn_gate": f(w_ffn_gate[0]), "w_ffn_up": f(w_ffn_up[0]),
        "wfc_p": f(np.concatenate([np.asarray(w_ffn_conv[0]), np.asarray(b_ffn_conv[0])[None, :]], axis=0).reshape(4, FC, 128).transpose(2, 1, 0).reshape(128, FC * 4)),
        "w_ffn_down": f(w_ffn_down[0]),
    }
    in_maps = []
    for c in range(8):
        b, h = c // 2, c % 2
        T0 = 1024 * h
        toks = np.zeros((NT, D), np.float32)
        toks[0:NS] = x_sample[16 * c:16 * c + 16, 0]
        if h == 1:
            toks[H0:M0] = x_prompt[b, T0 - 4:T0]
        toks[M0:] = x_prompt[b, T0:T0 + 1024]
        xp = np.zeros((NP, D), np.float32)
        if h == 1:
            xp[2:] = x_prompt[b, 0:1022]
        m = dict(shared)
        m.update({
            "xT_in": f(toks.T), "xpT_in": f(xp.T), "x_tok": toks,
            "flag": np.full((128, 1), float(h), np.float32),
            "memT_in": f(mem_prompt[b].T),
            "kcT_in": f(cache_mem_k[0, 16 * c:16 * c + 16].reshape(16, 256, 1024).transpose(0, 2, 1)),
            "vc_in": f(cache_mem_v[0, 16 * c:16 * c + 16].reshape(16, 256, 1024)),
            "sgla_in": f(state_gla[0, 16 * c:16 * c + 16]),
            "scT_in": f(state_conv[0, 16 * c:16 * c + 16].transpose(2, 1, 0)),
            "sfT_in": f(state_ffn_conv[0, 16 * c:16 * c + 16].transpose(2, 1, 0)),
        })
        in_maps.append(m)
    res = run_bass_kernel_spmd(nc, in_maps, core_ids=list(range(8)))
    R = res.results
    yp = np.zeros((4, 2048, D), np.float32); ys = np.zeros((128, 1, D), np.float32)
    pmk = np.zeros((1, 4, 256, 4, 256), np.float32); pmv = np.zeros_like(pmk)
    pgla = np.zeros((1, 4, 4, 256, 512), np.float32); pconv = np.zeros((1, 4, 2, 1024), np.float32); pffn = np.zeros((1, 4, 2, DFF), np.float32)
    sgla = np.zeros((1, 128, 4, 256, 512), np.float32); sconv = np.zeros((1, 128, 2, 1024), np.float32); sffn = np.zeros((1, 128, 2, DFF), np.float32)
    for c in range(8):
        b, h = c // 2, c % 2
        r = R[c]
        yp[b, 1024 * h:1024 * h + 1024] = r["y_out"][M0:]
        ys[16 * c:16 * c + 16, 0] = r["y_out"][0:NS]
        sgla[0, 16 * c:16 * c + 16] = r["sgla_out"]
        sconv[0, 16 * c:16 * c + 16] = r["sconvT"].transpose(2, 1, 0)
        sffn[0, 16 * c:16 * c + 16] = r["sffnT"].transpose(2, 1, 0)
        if h == 0:
            pmk[0, b] = r["pmk"].reshape(256, 4, 256); pmv[0, b] = r["pmv"].reshape(256, 4, 256)
        else:
            pgla[0, b] = r["pgla"]; pconv[0, b] = r["pconvT"].T; pffn[0, b] = r["pffnT"].T
    return (yp, ys, pmk, pmv, pgla, pconv, pffn, sgla, sconv, sffn)
```

```python
import numpy as np
import concourse.bass as bass
import concourse.mybir as mybir
from concourse.bass_utils import run_bass_kernel_spmd
from contextlib import ExitStack

F32 = mybir.dt.float32
BF16 = mybir.dt.bfloat16
AF = mybir.ActivationFunctionType
ALU = mybir.AluOpType
AX = mybir.AxisListType

D = 2048; KC = 16; NT = 1044; NS = 16; H0 = 16; M0 = 20; NP = 1024
DFF = 5504; FC = 43
GROUPS = [(0, 348), (348, 696), (696, 1044)]
PGROUPS = [(0, 512), (512, 1024)]
ALPHA = float(2.0 ** 0.25)
OFF = dict(q=0, k=1024, v=2048, g=4096, a=6144, cb=6160, cc=7184, ch=8208, mq=9232, za=10256, zb=12304, zm=14352)
TILES = [(0, 20)] + [(M0 + 128 * i, 128) for i in range(8)]


class StopBuild(Exception):
    pass


class Prog:
    def __init__(self):
        self.ops = {e: [] for e in ('pe', 'act', 'dve', 'pool', 'sp')}
        self.cnt = {}
        self.res_w = {}
        self.res_r = {}
        self.known = {e: {} for e in self.ops}
        self.rr = {'sp': 0, 'pool': 0, 'act': 0}
        self.nd = {'sp': 8, 'pool': 4, 'act': 4}

    def _deps(self, reads, writes):
        d = {}

        def add(sv):
            if sv is None:
                return
            s, v = sv
            if d.get(s, -1) < v:
                d[s] = v
        for k in reads:
            add(self.res_w.get(k))
        for k in writes:
            add(self.res_w.get(k))
            for s, v in self.res_r.get(k, {}).items():
                add((s, v))
        return d

    def _record(self, stream, val, reads, writes):
        for k in reads:
            self.res_r.setdefault(k, {})[stream] = val
        for k in writes:
            self.res_w[k] = (stream, val)
            self.res_r[k] = {}

    def _waits(self, eng, d):
        waits = []
        for s, v in d.items():
            if s == eng and eng == 'pe':
                continue
            if self.known[eng].get(s, -1) >= v:
                continue
            self.known[eng][s] = v
            waits.append((s, v))
        return waits

    def op(self, eng, fn, reads=(), writes=()):
        d = self._deps(reads, writes)
        waits = self._waits(eng, d)
        val = self.cnt.get(eng, 0) + 1
        self.cnt[eng] = val
        self.ops[eng].append((waits, fn, (eng, 1)))
        self._record(eng, val, reads, writes)

    def dma(self, q, fn, reads=(), writes=()):
        k = self.rr[q]
        self.rr[q] = (k + 1) % self.nd[q]
        stream = 'dma_%s%d' % (q, k)
        d = self._deps(reads, writes)
        prev = self.cnt.get(stream, 0)
        if prev > 0:
            d[stream] = max(d.get(stream, 0), prev)
        waits = self._waits(q, d)
        val = prev + 16
        self.cnt[stream] = val
        self.ops[q].append((waits, fn, (stream, 16)))
        self._record(stream, val, reads, writes)

    def barrier(self, pool=True):
        snap = dict(self.cnt)
        for e in self.ops:
            if e == 'pool' and not pool:
                continue
            waits = self._waits(e, dict(snap))
            if waits:
                self.ops[e].append((waits, None, None))

    def emit(self, nc, es):
        sems = {s: es.enter_context(nc.semaphore(s)) for s in self.cnt}
        block = es.enter_context(nc.Block())

        def run(name, eh):
            for waits, fn, sig in self.ops[name]:
                for s, v in waits:
                    eh.wait_ge(sems[s], v)
                if fn is not None:
                    ins = fn(eh)
                    ins.then_inc(sems[sig[0]], sig[1])

        @block.tensor
        def _(e):
            run('pe', e)

        @block.scalar
        def _(e):
            run('act', e)

        @block.vector
        def _(e):
            run('dve', e)

        @block.gpsimd
        def _(e):
            run('pool', e)

        @block.sync
        def _(e):
            run('sp', e)


def build_program(stop=None):
    holder = {}
    try:
        return _build(stop, holder)
    except StopBuild:
        return holder['finish']()


def _build(stop, holder):
    nc = bass.Bass("TRN2", target_bir_lowering=False)
    P = Prog()
    es = ExitStack()

    def din(name, shape):
        return nc.dram_tensor(name, list(shape), F32, kind="ExternalInput").ap()

    def dout(name, shape):
        return nc.dram_tensor(name, list(shape), F32, kind="ExternalOutput").ap()

    def finish():
        P.barrier()
        with nc.allow_non_contiguous_dma(reason="small strided state rows"):
            P.emit(nc, es)
        es.close()
        nc._prog_counts = dict(P.cnt)
        return nc
    holder['finish'] = finish

    def chk(tag):
        if stop == tag:
            raise StopBuild()

    xT_in = din("xT_in", [D, NT]); xpT_in = din("xpT_in", [D, NP]); x_tok = din("x_tok", [NT, D])
    flag_in = din("flag", [128, 1]); consts_in = din("consts", [128, 512]); ones_in = din("ones_row", [1, NT])
    memT_in = din("memT_in", [D, 256]); kcT_in = din("kcT_in", [NS, 1024, 256]); vc_in = din("vc_in", [NS, 256, 1024])
    sgla_in = din("sgla_in", [NS, 4, 256, 512]); scT_in = din("scT_in", [1024, 2, NS]); sfT_in = din("sfT_in", [DFF, 2, NS])
    w_in = din("w_in", [D, 16400]); wa2b_in = din("wa2b", [17, 1024]); gnorm_in = din("gnorm_b", [128, 512])
    w_gla_out = din("w_gla_out", [D, D]); wconv_in = din("wconv_p", [128, 24]); w_conv_out = din("w_conv_out", [1024, D])
    w_mem_k = din("w_mem_k", [D, 1024]); w_mem_v = din("w_mem_v", [D, 1024]); w_mem_out = din("w_mem_out", [1024, D])
    w_o = din("w_o", [D, D]); ln_in = din("ln_b", [4, 128, D])
    w_gate = din("w_ffn_gate", [D, DFF]); w_up = din("w_ffn_up", [D, DFF]); wfc_in = din("wfc_p", [128, FC * 4])
    w_down = din("w_ffn_down", [DFF, D])

    y_out = dout("y_out", [NT, D]); pmk_out = dout("pmk", [256, 1024]); pmv_out = dout("pmv", [256, 1024])
    pgla_out = dout("pgla", [4, 256, 512]); pconv_out = dout("pconvT", [1024, 2]); pffn_out = dout("pffnT", [DFF, 2])
    sgla_out = dout("sgla_out", [NS, 4, 256, 512]); sconv_out = dout("sconvT", [1024, 2, NS]); sffn_out = dout("sffnT", [DFF, 2, NS])
    x1_scr = nc.dram_tensor("x1_scr", [NT, D], F32, kind="Internal").ap()

    NW = 53200
    big = es.enter_context(nc.sbuf_tensor("big", [128, NW], F32))
    psum = es.enter_context(nc.psum_tensor("ps", [128, 4096], F32))

    class Arena:
        def __init__(self, base, limit):
            self.base = base; self.off = base; self.limit = limit

        def f32(self, n):
            o = self.off; self.off += n
            assert self.off <= self.limit, (self.off, self.limit)
            return big[:, o:o + n]

        def bf(self, n):
            w = (n + 1) // 2
            o = self.off; self.off += w
            assert self.off <= self.limit, (self.off, self.limit)
            return big[:, o:o + w].bitcast(BF16)

        def reset(self):
            self.off = self.base

    def r3(ap, a):
        return ap.rearrange("p (a b) -> p a b", a=a)

    def bank(b, n=512):
        return psum[:, b * 512:b * 512 + n]

    def bankbf(b, nb=1):
        return psum[:, b * 512:(b + nb) * 512].bitcast(BF16)

    PA = Arena(0, 1500)
    cst = PA.f32(512)
    ident = cst[:, 0:128]; maskf = cst[:, 128:256]; trineg = cst[:, 256:384]
    identb = PA.bf(128); maskb = PA.bf(128)
    flag = PA.f32(1); wconv = PA.f32(24); wfc = PA.f32(FC * 4); gnorm = PA.f32(512)
    A1 = Arena(1500, NW)
    xT = r3(A1.bf(KC * NT), KC)
    xpT_or_merged = A1.bf(KC * NT)
    xpT = r3(xpT_or_merged[:, 0:KC * NP], KC)
    mergedT = r3(xpT_or_merged, KC)
    WB_OFF = A1.off
    WB = [A1.bf(KC * 512) for _ in range(2)]
    OG_OFF = A1.off
    ogT = r3(A1.bf(KC * NT), KC)
    BU_OFF = A1.off
    buT = r3(A1.bf(8 * NT), 8)
    omT = r3(A1.bf(8 * NT), 8)
    SCR0 = A1.off
    SC = Arena(BU_OFF, NW)
    a17 = SC.f32(NT); a17p = SC.f32(NP)

    wb_i = [0]

    def load_w(src_list, nk, ncols_total):
        i = wb_i[0]; wb_i[0] ^= 1
        buf = WB[i][:, 0:nk * ncols_total].rearrange("p (k e) -> p k e", k=nk)
        for (src, co) in src_list:
            w = src.shape[1]
            sv = src.rearrange("(k p) e -> p k e", p=128)
            half = (nk + 1) // 2
            for (k0, k1) in ((0, half), (half, nk)):
                P.dma('pool', (lambda q, o=buf[:, k0:k1, co:co + w], s=sv[:, k0:k1, :]: q.dma_start(out=o, in_=s)),
                      writes=[('W', i)])
        return buf, ('W', i)

    fset = [0]

    def mmF(wbuf, wkey, e0, M, rhs_fn, rkeys, nk, groups):
        s = fset[0]; fset[0] ^= 1
        banks = [3 * s + gi for gi in range(len(groups))]
        keys = [('ps', b) for b in banks]

        def fn(pe):
            ins = None
            for kc in range(nk):
                for gi, (c0, c1) in enumerate(groups):
                    ins = pe.matmul(bank(banks[gi])[0:M, 0:c1 - c0], lhsT=wbuf[:, kc, e0:e0 + M], rhs=rhs_fn(kc, c0, c1),
                                    start=(kc == 0), stop=(kc == nk - 1))
            return ins
        P.op('pe', fn, reads=[wkey] + list(rkeys), writes=keys)
        return banks, keys

    def evac(eng, groups, banks, keys, M, out_fn, wkeys, func=None, scale=None, rkeys=()):
        for gi, (c0, c1) in enumerate(groups):
            src = bank(banks[gi])[0:M, 0:c1 - c0]
            dst = out_fn(c0, c1)
            if eng == 'act':
                kw = {}
                if scale is not None:
                    kw['scale'] = scale
                P.op('act', (lambda e, d=dst, s=src, kw=kw: e.activation(out=d, in_=s, func=(func or AF.Copy), **kw)),
                     reads=[keys[gi]] + list(rkeys), writes=wkeys)
            else:
                P.op('dve', (lambda e, d=dst, s=src: e.tensor_copy(out=d, in_=s)), reads=[keys[gi]] + list(rkeys), writes=wkeys)

    P.dma('sp', lambda q: q.dma_start(out=cst, in_=consts_in[:, :]), writes=['cst'])
    P.dma('sp', lambda q: q.dma_start(out=flag, in_=flag_in[:, :]), writes=['small'])
    P.dma('sp', lambda q: q.dma_start(out=wconv, in_=wconv_in[:, :]), writes=['small'])
    P.dma('sp', lambda q: q.dma_start(out=wfc, in_=wfc_in[:, :]), writes=['small'])
    P.dma('sp', lambda q: q.dma_start(out=gnorm, in_=gnorm_in[:, :]), writes=['small'])
    P.dma('sp', lambda q: q.dma_start(out=a17[16:17, :], in_=ones_in[0:1, :]), writes=['a17ones'])
    P.dma('sp', lambda q: q.dma_start(out=a17p[16:17, :], in_=ones_in[0:1, 0:NP]), writes=['a17ones'])
    P.op('dve', lambda e: e.tensor_copy(out=identb, in_=ident), reads=['cst'], writes=['cstb'])
    P.op('dve', lambda e: e.tensor_copy(out=maskb, in_=maskf), reads=['cst'], writes=['cstb'])
    xv = xT_in.rearrange("(k p) t -> p k t", p=128)
    xpv = xpT_in.rearrange("(k p) t -> p k t", p=128)
    for k0 in range(0, KC, 4):
        P.dma('pool', (lambda q, k0=k0: q.dma_start(out=xT[:, k0:k0 + 4, :], in_=xv[:, k0:k0 + 4, :])), writes=['xT'])
    for k0 in range(0, KC, 4):
        P.dma('pool', (lambda q, k0=k0: q.dma_start(out=xpT[:, k0:k0 + 4, :], in_=xpv[:, k0:k0 + 4, :])), writes=['xpT'])

    if stop == 'A0':
        return finish()
    rx = lambda kc, c0, c1: xT[:, kc, c0:c1]
    rxp = lambda kc, c0, c1: xpT[:, kc, c0:c1]

    wa_f = SC.f32(KC * 16)
    wa_buf = r3(SC.bf(KC * 16), KC); wa_key = 'wa'
    P.dma('sp', lambda q: q.dma_start(out=r3(wa_f, KC), in_=w_in[:, OFF['a']:OFF['a'] + 16].rearrange("(k p) e -> p k e", p=128)), writes=['wa_f'])
    P.op('dve', lambda e: e.tensor_copy(out=wa_buf.rearrange("p a b -> p (a b)"), in_=wa_f), reads=['wa_f'], writes=['wa'])
    banks, keys = mmF(wa_buf, wa_key, 0, 16, rx, ['xT'], KC, GROUPS)
    evac('act', GROUPS, banks, keys, 16, lambda c0, c1: a17[0:16, c0:c1], ['a17'])
    banks, keys = mmF(wa_buf, wa_key, 0, 16, rxp, ['xpT'], KC, PGROUPS)
    evac('act', PGROUPS, banks, keys, 16, lambda c0, c1: a17p[0:16, c0:c1], ['a17p'])
    A17K = ['a17', 'a17ones']; A17PK = ['a17p', 'a17ones']

    if stop == 'A1':
        return finish()
    wa2b = SC.f32(256)
    qT = r3(SC.bf(2 * NT), 2); kT = r3(SC.bf(2 * NT), 2); vT = r3(SC.bf(4 * NT), 4); gsT = r3(SC.bf(4 * NT), 4)
    kpT = r3(SC.bf(2 * NP), 2); vpT = r3(SC.bf(4 * NP), 4)
    S = r3(SC.f32(1024), 2); Sb = r3(SC.bf(1024), 2)
    e1 = SC.f32(256); sp_t = SC.f32(256); ek_t = SC.f32(256)
    eqT = r3(SC.f32(256), 2); ekT = r3(SC.f32(256), 2)
    qd = r3(SC.bf(256), 2); kdT = r3(SC.bf(256), 2)
    kd_t = SC.bf(256); v_t = SC.bf(512); scm = SC.bf(128)
    junk = SC.bf(512); on = SC.bf(512); st2 = SC.f32(4)
    aTs = r3(SC.f32(2 * NS), 2); km = [SC.bf(256) for _ in range(2)]
    QG = SC.bf(NS * 2 * NS).rearrange("p (s j c) -> p s j c", s=NS, j=2)
    SS = [S, r3(SC.f32(1024), 2)]
    SSb = [Sb, Sb]
    SSK = ['S', ('SS', 1)]

    def gla_post(n, c0, h, ops_ap, okeys):
        ss = st2[0:n, 0:1]; lv = st2[0:n, 1:2]; rstd = st2[0:n, 2:3]
        P.op('act', lambda e: e.activation(out=junk[0:n, :], in_=ops_ap, func=AF.Square, accum_out=ss), reads=okeys, writes=['junk', 'st2a'])
        P.op('act', lambda e: e.activation(out=lv, in_=ss, func=AF.Ln, scale=1.0 / 512.0, bias=1e-6), reads=['st2a'], writes=['st2b'])
        P.op('act', lambda e: e.activation(out=rstd, in_=lv, func=AF.Exp, scale=-0.5), reads=['st2b'], writes=['st2c'])
        P.op('dve', lambda e: e.scalar_tensor_tensor(out=on[0:n, :], in0=ops_ap, scalar=rstd, in1=gnorm[0:n, :], op0=ALU.mult, op1=ALU.mult),
             reads=list(okeys) + ['st2c', 'small'], writes=['on'])
        tv = bankbf(7)[:, 0:512].rearrange("p (a b) -> p a b", a=4)

        def fn(pe):
            ins = None
            for vv in range(4):
                ins = pe.transpose(tv[:, vv, 0:n], on[0:n, vv * 128:(vv + 1) * 128], identb[0:n, 0:n])
            return ins
        P.op('pe', fn, reads=['on', 'cstb'], writes=[('ps', 7)])
        P.op('dve', lambda e: e.tensor_tensor(out=ogT[:, 4 * h:4 * h + 4, c0:c0 + n], in0=tv[:, :, 0:n], in1=gsT[:, :, c0:c0 + n], op=ALU.mult),
             reads=[('ps', 7), 'gsT'], writes=['ogT'])

    def gla_chunk(h, n, c0, kTs, vTs, a_src, akeys, state_only, need_sb=True):
        wcols = wa2b[0:17, 0:256]
        P.op('pe', lambda pe: pe.matmul(bank(0)[0:n, 0:256], lhsT=a_src[0:17, c0:c0 + n], rhs=wcols, start=True, stop=True),
             reads=list(akeys) + ['wa2b'], writes=[('ps', 0)])
        P.op('act', lambda e: e.activation(out=e1[0:n, :], in_=bank(0)[0:n, 0:256], func=AF.Exp, scale=-1.0), reads=[('ps', 0)], writes=['e1'])
        P.op('act', lambda e: e.activation(out=sp_t[0:n, :], in_=e1[0:n, :], func=AF.Ln, bias=1.0), reads=['e1'], writes=['sp'])
        chk('g1')
        bct = psum[:, 512:1024].rearrange("p (a b) -> p a b", a=2)

        def fnb(pe):
            pe.matmul(bank(0)[0:n, 256:512], lhsT=trineg[0:n, 0:n], rhs=sp_t[0:n, :], start=True, stop=True)
            ins = None
            for jj in range(2):
                ins = pe.matmul(bct[:, jj, 0:n], lhsT=sp_t[0:n, jj * 128:(jj + 1) * 128], rhs=trineg[0:n, 0:n], start=True, stop=True)
            return ins
        P.op('pe', fnb, reads=['sp', 'cst', 'e1'], writes=[('ps', 0), ('ps', 1)])
        chk('g2')
        P.op('act', lambda e: e.activation(out=ek_t[0:n, :], in_=bank(0)[0:n, 256:512], func=AF.Exp, scale=-1.0), reads=[('ps', 0)], writes=['ek_t'])
        P.op('act', lambda e: e.activation(out=eqT[:, :, 0:n], in_=bct[:, :, 0:n], func=AF.Exp), reads=[('ps', 1)], writes=['eqT'])
        if not state_only:
            P.op('act', lambda e: e.activation(out=ekT[:, :, 0:n], in_=bct[:, :, 0:n], func=AF.Exp, scale=-1.0), reads=[('ps', 1)], writes=['ekT'])
        chk('g3')
        ktv = bankbf(2)[:, 0:256]
        vtv = bankbf(2)[:, 256:768]

        def fnt(pe):
            ins = None
            for jj in range(2):
                ins = pe.transpose(ktv[0:n, jj * 128:(jj + 1) * 128], kTs[:, jj, c0:c0 + n], identb)
            for vv in range(4):
                ins = pe.transpose(vtv[0:n, vv * 128:(vv + 1) * 128], vTs[:, vv, c0:c0 + n], identb)
            return ins
        P.op('pe', fnt, reads=['kvT', 'cstb'], writes=[('ps', 2)])
        chk('g4')
        P.op('dve', lambda e: e.tensor_tensor(out=kd_t[0:n, :], in0=ktv[0:n, :], in1=ek_t[0:n, :], op=ALU.mult), reads=[('ps', 2), 'ek_t'], writes=['kd_t'])
        chk('g4a')
        P.op('dve', lambda e: e.tensor_copy(out=v_t[0:n, :], in_=vtv[0:n, :]), reads=[('ps', 2)], writes=['v_t'])
        if not state_only:
            P.op('dve', lambda e: e.tensor_tensor(out=qd[:, :, 0:n], in0=qT[:, :, c0:c0 + n], in1=eqT[:, :, 0:n], op=ALU.mult), reads=['qT', 'eqT'], writes=['qd'])
            P.op('dve', lambda e: e.tensor_tensor(out=kdT[:, :, 0:n], in0=kTs[:, :, c0:c0 + n], in1=ekT[:, :, 0:n], op=ALU.mult), reads=['kvT', 'ekT'], writes=['kdT'])

            def fns(pe):
                ins = None
                for jj in range(2):
                    ins = pe.matmul(bank(3)[0:n, 0:n], lhsT=kdT[:, jj, 0:n], rhs=qd[:, jj, 0:n], start=(jj == 0), stop=(jj == 1))
                return ins
            P.op('pe', fns, reads=['qd', 'kdT'], writes=[('ps', 3)])
            P.op('dve', lambda e: e.tensor_tensor(out=scm[0:n, 0:n], in0=bank(3)[0:n, 0:n], in1=maskf[0:n, 0:n], op=ALU.mult), reads=[('ps', 3), 'cst'], writes=['scm'])

            def fno(pe):
                pe.matmul(bank(4)[0:n, :], lhsT=scm[0:n, 0:n], rhs=v_t[0:n, :], start=True, stop=False)
                ins = None
                for jj in range(2):
                    ins = pe.matmul(bank(4)[0:n, :], lhsT=qd[:, jj, 0:n], rhs=Sb[:, jj, :], start=False, stop=(jj == 1))
                return ins
            P.op('pe', fno, reads=['scm', 'v_t', 'qd', 'Sb'], writes=[('ps', 4)])
            gla_post(n, c0, h, bank(4)[0:n, :], [('ps', 4)])
        chk('g5')
        def fnu(pe):
            ins = None
            for jj in range(2):
                ins = pe.matmul(bank(5 + jj), lhsT=kd_t[0:n, jj * 128:(jj + 1) * 128], rhs=v_t[0:n, :], start=True, stop=True)
            return ins
        P.op('pe', fnu, reads=['kd_t', 'v_t'], writes=[('ps', 5), ('ps', 6)])
        for jj in range(2):
            el = eqT[:, jj, n - 1:n]
            P.op('dve', (lambda e, jj=jj: e.tensor_tensor(out=S[:, jj, :], in0=S[:, jj, :], in1=bank(5 + jj), op=ALU.add)),
                 reads=[('ps', 5 + jj), 'S'], writes=['S'])
            P.op('dve', (lambda e, jj=jj, el=el: e.tensor_scalar(out=S[:, jj, :], in0=S[:, jj, :], scalar1=el, scalar2=None, op0=ALU.mult)),
                 reads=['S', 'eqT'], writes=['S'])
        if need_sb:
            P.op('act', lambda e: e.activation(out=Sb.rearrange("p a b -> p (a b)"), in_=S.rearrange("p a b -> p (a b)"), func=AF.Copy), reads=['S'], writes=['Sb'])

    for h in range(4):
        P.dma('sp', (lambda q, h=h: q.dma_start(out=wa2b[0:17, :], in_=wa2b_in[:, h * 256:(h + 1) * 256])), writes=['wa2b'])
        wA, kA = load_w([(w_in[:, OFF['q'] + h * 256:OFF['q'] + (h + 1) * 256], 0), (w_in[:, OFF['k'] + h * 256:OFF['k'] + (h + 1) * 256], 256)], KC, 512)
        for ec in range(4):
            banks, keys = mmF(wA, kA, ec * 128, 128, rx, ['xT'], KC, GROUPS)
            if ec < 2:
                evac('act', GROUPS, banks, keys, 128, lambda c0, c1, ec=ec: qT[:, ec, c0:c1], ['qT'], scale=0.0625)
            else:
                evac('act', GROUPS, banks, keys, 128, lambda c0, c1, ec=ec: kT[:, ec - 2, c0:c1], ['kvT'])
        for ec in range(2, 4):
            banks, keys = mmF(wA, kA, ec * 128, 128, rxp, ['xpT'], KC, PGROUPS)
            evac('dve', PGROUPS, banks, keys, 128, lambda c0, c1, ec=ec: kpT[:, ec - 2, c0:c1], ['kvT'])
        wB, kB = load_w([(w_in[:, OFF['v'] + h * 512:OFF['v'] + (h + 1) * 512], 0)], KC, 512)
        for ec in range(4):
            banks, keys = mmF(wB, kB, ec * 128, 128, rx, ['xT'], KC, GROUPS)
            evac('act', GROUPS, banks, keys, 128, lambda c0, c1, ec=ec: vT[:, ec, c0:c1], ['kvT'])
            banks, keys = mmF(wB, kB, ec * 128, 128, rxp, ['xpT'], KC, PGROUPS)
            evac('dve', PGROUPS, banks, keys, 128, lambda c0, c1, ec=ec: vpT[:, ec, c0:c1], ['kvT'])
        wC, kC = load_w([(w_in[:, OFF['g'] + h * 512:OFF['g'] + (h + 1) * 512], 0)], KC, 512)
        for ec in range(4):
            banks, keys = mmF(wC, kC, ec * 128, 128, rx, ['xT'], KC, GROUPS)
            evac('act', GROUPS, banks, keys, 128, lambda c0, c1, ec=ec: gsT[:, ec, c0:c1], ['gsT'], func=AF.Silu)
        if stop == 'A4p':
            return finish()
        P.op('dve', lambda e: e.memset(S.rearrange("p a b -> p (a b)"), 0.0), writes=['S'])
        P.op('dve', lambda e: e.memset(Sb.rearrange("p a b -> p (a b)"), 0.0), writes=['Sb'])
        for i in range(8):
            gla_chunk(h, 128, i * 128, kpT, vpT, a17p, A17PK, True, need_sb=(i == 7))
            if stop == 'A4pre':
                return finish()
        gla_chunk(h, 2, 18, kT, vT, a17, A17K, False)
        if stop == 'A4h':
            return finish()
        for i in range(8):
            gla_chunk(h, 128, M0 + i * 128, kT, vT, a17, A17K, False, need_sb=(i < 7))
        if stop == 'A4m':
            return finish()
        P.dma('sp', (lambda q, h=h: q.dma_start(out=pgla_out[h].rearrange("(jj p) v -> p jj v", p=128), in_=S)), reads=['S'])
        pre_s = psum[:, 0:2 * NS].rearrange("p (a b) -> p a b", a=2)

        def fna(pe, h=h):
            ins = None
            for jj in range(2):
                ins = pe.matmul(pre_s[:, jj, :], lhsT=wa2b[0:17, jj * 128:(jj + 1) * 128], rhs=a17[0:17, 0:NS], start=True, stop=True)
            return ins
        P.op('pe', fna, reads=A17K + ['wa2b'], writes=[('ps', 0)])
        P.op('act', lambda e: e.activation(out=aTs, in_=pre_s, func=AF.Exp, scale=-1.0), reads=[('ps', 0)], writes=['aTs'])
        P.op('act', lambda e: e.activation(out=aTs, in_=aTs, func=AF.Ln, bias=1.0), reads=['aTs'], writes=['aTs'])
        P.op('act', lambda e: e.activation(out=aTs, in_=aTs, func=AF.Exp, scale=-1.0 / 16.0), reads=['aTs'], writes=['aTs'])
        chk('s1')
        ktv = bankbf(2)[:, 0:256]; vtv = bankbf(2)[:, 256:768]

        def fnts(pe):
            ins = None
            for jj in range(2):
                ins = pe.transpose(ktv[0:NS, jj * 128:(jj + 1) * 128], kT[:, jj, 0:NS], identb)
            for vv in range(4):
                ins = pe.transpose(vtv[0:NS, vv * 128:(vv + 1) * 128], vT[:, vv, 0:NS], identb)
            return ins
        P.op('pe', fnts, reads=['kvT', 'cstb'], writes=[('ps', 2)])
        P.op('dve', lambda e: e.tensor_copy(out=v_t[0:NS, :], in_=vtv[0:NS, :]), reads=[('ps', 2)], writes=['v_t'])
        P.op('dve', lambda e: e.tensor_copy(out=kd_t[0:NS, :], in_=ktv[0:NS, :]), reads=[('ps', 2)], writes=['kd_t'])
        P.op('dve', lambda e: e.memset(QG.rearrange("p s j c -> p (s j c)"), 0.0), writes=['QG'])
        for s in range(NS):
            P.op('dve', (lambda e, s=s: e.tensor_copy(out=QG[:, s, :, s:s + 1], in_=qT[:, :, s:s + 1])), reads=['qT', 'QG'], writes=['QG'])
        chk('s2')
        for s in range(NS):
            i = s % 2
            P.dma('sp', (lambda q, s=s, i=i, h=h: q.dma_start(out=SS[i], in_=sgla_in[s, h].rearrange("(jj p) v -> p jj v", p=128))), writes=[SSK[i]])
            P.op('dve', (lambda e, s=s, i=i: e.tensor_scalar(out=km[i][0:NS, :], in0=kd_t[0:NS, :], scalar1=ident[0:NS, s:s + 1], scalar2=None, op0=ALU.mult)),
                 reads=['kd_t', 'cst'], writes=[('km', i)])

            def fnu(pe, s=s):
                ins = None
                for jj in range(2):
                    ins = pe.matmul(bank(5 + jj), lhsT=km[s % 2][0:NS, jj * 128:(jj + 1) * 128], rhs=v_t[0:NS, :], start=True, stop=True)
                return ins
            P.op('pe', fnu, reads=[('km', i), 'v_t'], writes=[('ps', 5), ('ps', 6)])
            for jj in range(2):
                P.op('dve', (lambda e, s=s, jj=jj, i=i: e.scalar_tensor_tensor(out=SS[i][:, jj, :], in0=SS[i][:, jj, :], scalar=aTs[:, jj, s:s + 1],
                                                                              in1=bank(5 + jj), op0=ALU.mult, op1=ALU.add)),
                     reads=[SSK[i], 'aTs', ('ps', 5 + jj)], writes=[SSK[i]])
            P.dma('sp', (lambda q, s=s, i=i, h=h: q.dma_start(out=sgla_out[s, h].rearrange("(jj p) v -> p jj v", p=128), in_=SS[i])), reads=[SSK[i]])
            P.op('act', (lambda e, i=i: e.activation(out=SSb[i].rearrange("p a b -> p (a b)"), in_=SS[i].rearrange("p a b -> p (a b)"), func=AF.Copy)),
                 reads=[SSK[i]], writes=['Sb'])

            def fnos(pe, s=s, i=i):
                ins = None
                for jj in range(2):
                    ins = pe.matmul(bank(4)[0:NS, :], lhsT=QG[:, s, jj, :], rhs=SSb[i][:, jj, :], start=(s == 0 and jj == 0), stop=(s == NS - 1 and jj == 1))
                return ins
            P.op('pe', fnos, reads=['QG', 'Sb'], writes=[('ps', 4)])
            if s == 0:
                chk('s3')
        chk('s4')
        gla_post(NS, 0, h, bank(4)[0:NS, :], [('ps', 4)])
    P.op('dve', lambda e: e.memset(ogT[:, :, 16:18], 0.0), reads=['ogT'], writes=['ogT'])
    P.barrier(pool=False)

    if stop == 'A4':
        return finish()
    SC = Arena(SCR0, NW)
    zT = r3(SC.f32(8 * NT), 8)
    scT = SC.f32(8 * 2 * NS).rearrange("p (j r s) -> p j r s", j=8, r=2)
    ctmp = SC.f32(NT); ctmp2 = SC.f32(NS)
    scv = scT_in.rearrange("(j p) r s -> p j r s", p=128)
    for j0 in (0, 4):
        P.dma('sp', (lambda q, j0=j0: q.dma_start(out=scT[:, j0:j0 + 4], in_=scv[:, j0:j0 + 4])), writes=['scT'])
    P.dma('sp', lambda q: q.dma_start(out=sconv_out[:, 0, :], in_=scT_in[:, 1, :]))
    for which in ('cb', 'cc', 'ch'):
        for blk in range(2):
            wbuf, wkey = load_w([(w_in[:, OFF[which] + blk * 512:OFF[which] + (blk + 1) * 512], 0)], KC, 512)
            for ec in range(4):
                j = blk * 4 + ec
                banks, keys = mmF(wbuf, wkey, ec * 128, 128, rx, ['xT'], KC, GROUPS)
                if which == 'cb':
                    evac('act', GROUPS, banks, keys, 128, lambda c0, c1, j=j: buT[:, j, c0:c1], [('buT', j)])
                elif which == 'cc':
                    evac('act', GROUPS, banks, keys, 128, lambda c0, c1, j=j: zT[:, j, c0:c1], [('zT', j)])
                else:
                    for gi, (c0, c1) in enumerate(GROUPS):
                        P.op('dve', (lambda e, j=j, c0=c0, c1=c1, b=banks[gi]: e.tensor_tensor(
                            out=zT[:, j, c0:c1], in0=bank(b)[:, 0:c1 - c0], in1=zT[:, j, c0:c1], op=ALU.mult)),
                            reads=[keys[gi], ('zT', j)], writes=[('zT', j)])
    for j in range(8):
        w0 = wconv[:, j * 3 + 0:j * 3 + 1]; w1 = wconv[:, j * 3 + 1:j * 3 + 2]; w2 = wconv[:, j * 3 + 2:j * 3 + 3]
        L = NT - 18
        u = ctmp[:, 0:L]
        P.op('dve', (lambda e, j=j, w0=w0: e.tensor_scalar(out=u, in0=zT[:, j, 16:16 + L], scalar1=w0, scalar2=None, op0=ALU.mult)),
             reads=[('zT', j), 'small'], writes=['ctmp'])
        P.op('dve', (lambda e, j=j, w1=w1: e.scalar_tensor_tensor(out=u, in0=zT[:, j, 17:17 + L], scalar=w1, in1=u, op0=ALU.mult, op1=ALU.add)),
             reads=[('zT', j), 'ctmp'], writes=['ctmp'])
        P.op('dve', (lambda e, j=j, w2=w2: e.scalar_tensor_tensor(out=u, in0=zT[:, j, 18:18 + L], scalar=w2, in1=u, op0=ALU.mult, op1=ALU.add)),
             reads=[('zT', j), 'ctmp'], writes=['ctmp'])
        P.op('dve', (lambda e, j=j: e.tensor_tensor(out=buT[:, j, 18:NT], in0=buT[:, j, 18:NT], in1=u, op=ALU.mult)),
             reads=[('buT', j), 'ctmp'], writes=[('buT', j)])
        us = ctmp2[:, 0:NS]
        P.op('dve', (lambda e, j=j, w0=w0: e.tensor_scalar(out=us, in0=scT[:, j, 0, :], scalar1=w0, scalar2=None, op0=ALU.mult)),
             reads=['scT', 'small'], writes=['ctmp2'])
        P.op('dve', (lambda e, j=j, w1=w1: e.scalar_tensor_tensor(out=us, in0=scT[:, j, 1, :], scalar=w1, in1=us, op0=ALU.mult, op1=ALU.add)),
             reads=['scT', 'ctmp2'], writes=['ctmp2'])
        P.op('dve', (lambda e, j=j, w2=w2: e.scalar_tensor_tensor(out=us, in0=zT[:, j, 0:NS], scalar=w2, in1=us, op0=ALU.mult, op1=ALU.add)),
             reads=[('zT', j), 'ctmp2'], writes=['ctmp2'])
        P.op('dve', (lambda e, j=j: e.tensor_tensor(out=buT[:, j, 0:NS], in0=buT[:, j, 0:NS], in1=us, op=ALU.mult)),
             reads=[('buT', j), 'ctmp2'], writes=[('buT', j)])
        P.op('dve', (lambda e, j=j: e.memset(buT[:, j, 16:18], 0.0)), writes=[('buT', j)])
    zk = [('zT', j) for j in range(8)]
    P.dma('sp', lambda q: q.dma_start(out=pconv_out.rearrange("(j p) r -> p j r", p=128), in_=zT[:, :, NT - 2:NT]), reads=zk)
    P.dma('sp', lambda q: q.dma_start(out=sconv_out[:, 1, :].rearrange("(j p) s -> p j s", p=128), in_=zT[:, :, 0:NS]), reads=zk)
    P.barrier()

    if stop == 'A2':
        return finish()
    SC.reset()
    MEM_OFF = SC.off
    memT = r3(SC.bf(KC * 256), KC)
    mkT = r3(SC.bf(8 * 256), 8)
    mvb = r3(SC.bf(2 * 1024), 2)
    mqT = r3(xpT_or_merged[:, 0:8 * NT], 8)
    Pf = SC.f32(1024); Pn = SC.bf(1024); PT = SC.bf(1024)
    mtok = Pf[:, 0:512]
    st4 = SC.f32(16)
    memv = memT_in.rearrange("(k p) m -> p k m", p=128)
    P.dma('pool', lambda q: q.dma_start(out=memT, in_=memv), writes=['memT'])
    rmem = lambda kc, c0, c1: memT[:, kc, c0:c1]
    tb = [0]

    def nbank():
        b = tb[0] % 6; tb[0] += 1
        return b
    for (wsrc, dst, isk) in ((w_mem_k, pmk_out, True), (w_mem_v, pmv_out, False)):
        for blk in range(2):
            wbuf, wkey = load_w([(wsrc[:, blk * 512:(blk + 1) * 512], 0)], KC, 512)
            for mt in range(2):
                b = nbank()

                def fn(pe, b=b, mt=mt, wbuf=wbuf):
                    ins = None
                    for kc in range(KC):
                        ins = pe.matmul(bank(b), lhsT=memT[:, kc, mt * 128:(mt + 1) * 128], rhs=wbuf[:, kc, :], start=(kc == 0), stop=(kc == KC - 1))
                    return ins
                P.op('pe', fn, reads=[wkey, 'memT'], writes=[('ps', b)])
                P.op('act', (lambda e, b=b: e.activation(out=mtok, in_=bank(b), func=AF.Copy)), reads=[('ps', b)], writes=['Pf'])
                P.dma('sp', (lambda q, dst=dst, mt=mt, blk=blk: q.dma_start(out=dst[mt * 128:(mt + 1) * 128, blk * 512:(blk + 1) * 512], in_=mtok)),
                      reads=['Pf'])
                if not isk:
                    P.op('dve', (lambda e, mt=mt, blk=blk: e.tensor_copy(out=mvb[:, mt, blk * 512:(blk + 1) * 512], in_=mtok)),
                         reads=['Pf'], writes=['mvb'])
                chk('m0')
            chk('t%d%d' % (int(isk), blk))
            if isk:
                for ec in range(4):
                    j = blk * 4 + ec
                    banks, keys = mmF(wbuf, wkey, ec * 128, 128, rmem, ['memT'], KC, [(0, 256)])
                    evac('dve', [(0, 256)], banks, keys, 128, lambda c0, c1, j=j: mkT[:, j, c0:c1], ['mkT'])
                    chk('m0c')
                chk('f%d' % blk)
        chk('m0e' if isk else 'm0f')
    chk('m1')
    for blk in range(2):
        wbuf, wkey = load_w([(w_in[:, OFF['mq'] + blk * 512:OFF['mq'] + (blk + 1) * 512], 0)], KC, 512)
        for ec in range(4):
            j = blk * 4 + ec
            banks, keys = mmF(wbuf, wkey, ec * 128, 128, rx, ['xT'], KC, GROUPS)
            evac('act', GROUPS, banks, keys, 128, lambda c0, c1, j=j: mqT[:, j, c0:c1], ['mqT'], scale=0.0625)

    chk('m2')

    def softmax_rows(n, lg_ap3, rkeys):
        mx = st4[0:n, 0:4]; nmx = st4[0:n, 4:8]; ssum = st4[0:n, 8:12]; rs = st4[0:n, 12:16]
        Pf3 = r3(Pf, 4); Pn3 = r3(Pn, 4)
        chk('x0')
        P.op('dve', lambda e: e.tensor_reduce(out=mx, in_=lg_ap3, axis=AX.X, op=ALU.max), reads=rkeys, writes=['st4'])
        chk('x1')
        P.op('dve', lambda e: e.tensor_scalar(out=nmx, in0=mx, scalar1=-1.0, scalar2=None, op0=ALU.mult), reads=['st4'], writes=['st4'])
        for h in range(4):
            P.op('act', (lambda e, h=h: e.activation(out=Pf3[0:n, h, :], in_=lg_ap3[:, h, :], func=AF.Exp, bias=nmx[:, h:h + 1],
                                                     accum_out=ssum[:, h:h + 1])), reads=list(rkeys) + ['st4'], writes=['Pf', ('ssum', h)])
            chk('x3')
        P.op('dve', lambda e: e.reciprocal(out=rs, in_=ssum), reads=[('ssum', h) for h in range(4)] + ['st4'], writes=['st4'])
        chk('x4')
        for h in range(4):
            P.op('dve', (lambda e, h=h: e.tensor_scalar(out=Pn3[0:n, h, :], in0=Pf3[0:n, h, :], scalar1=rs[:, h:h + 1], scalar2=None, op0=ALU.mult)),
                 reads=['Pf', 'st4'], writes=['Pn'])

    def attn_pv(n, c0, vsrc_fn, vkeys):
        Pn3 = r3(Pn, 4)
        ptv = bankbf(6)[:, 0:8 * 128].rearrange("p (a b) -> p a b", a=8)

        def fn(pe):
            ins = None
            for h in range(4):
                for mc in range(2):
                    ins = pe.transpose(ptv[:, h * 2 + mc, 0:n], Pn3[0:n, h, mc * 128:(mc + 1) * 128], identb[0:n, 0:n])
            return ins
        P.op('pe', fn, reads=['Pn', 'cstb'], writes=[('ps', 6)])
        PT3 = r3(PT, 8)
        P.op('dve', lambda e: e.tensor_copy(out=PT3[:, :, 0:n], in_=ptv[:, :, 0:n]), reads=[('ps', 6)], writes=['PT'])
        ov = psum[:, 0:1024].rearrange("p (a b) -> p a b", a=8)

        def fn2(pe):
            ins = None
            for j in range(8):
                for mc in range(2):
                    ins = pe.matmul(ov[:, j, 0:n], lhsT=vsrc_fn(mc, j), rhs=PT3[:, (j // 2) * 2 + mc, 0:n], start=(mc == 0), stop=(mc == 1))
            return ins
        P.op('pe', fn2, reads=['PT'] + list(vkeys), writes=[('ps', 0), ('ps', 1)])
        P.op('dve', lambda e: e.tensor_copy(out=omT[:, :, c0:c0 + n], in_=ov[:, :, 0:n]), reads=[('ps', 0), ('ps', 1)], writes=['omT'])

    lg3 = psum[:, 2 * 512:4 * 512].rearrange("p (a b) -> p a b", a=4)
    for (c0, n) in [(H0, 4)] + TILES[1:]:
        def fnl(pe, c0=c0, n=n):
            ins = None
            for h in range(4):
                for jj in range(2):
                    ins = pe.matmul(lg3[0:n, h, :], lhsT=mqT[:, 2 * h + jj, c0:c0 + n], rhs=mkT[:, 2 * h + jj, :], start=(jj == 0), stop=(jj == 1))
            return ins
        P.op('pe', fnl, reads=['mqT', 'mkT'], writes=[('ps', 2), ('ps', 3)])
        softmax_rows(n, lg3[0:n], [('ps', 2), ('ps', 3)])
        chk('m3')
        attn_pv(n, c0, lambda mc, j: mvb[:, mc, j * 128:(j + 1) * 128], ['mvb'])
        chk('m4')
    chk('m5')
    P.barrier()
    QS = [SC.bf(8 * 64).rearrange("p (j c) -> p j c", j=8) for _ in range(2)]
    PTs = SC.bf(2 * 64)
    SK = Arena(MEM_OFF, MEM_OFF + KC * 256 // 2)
    KS = [r3(SK.bf(8 * 256), 8) for _ in range(2)]
    VS = [r3(KS[i_].rearrange("p a b -> p (a b)"), 2) for i_ in range(2)]
    lgs = bank(4)[0:64, 0:256]
    for s in range(NS):
        i = s % 2
        P.dma('pool', (lambda q, s=s, i=i: q.dma_start(out=KS[i], in_=kcT_in[s].rearrange("(j p) m -> p j m", p=128))), writes=[('KS', i)])
        P.op('dve', (lambda e, i=i: e.memset(QS[i].rearrange("p j c -> p (j c)"), 0.0)), writes=[('QS', i)])
        for j in range(8):
            P.op('dve', (lambda e, s=s, j=j, i=i: e.tensor_copy(out=QS[i][:, j, s * 4 + j // 2:s * 4 + j // 2 + 1], in_=mqT[:, j, s:s + 1])),
                 reads=['mqT', ('QS', i)], writes=[('QS', i)])

        def fnk(pe, s=s, i=i):
            ins = None
            for j in range(8):
                ins = pe.matmul(lgs, lhsT=QS[i][:, j, :], rhs=KS[i][:, j, :], start=(s == 0 and j == 0), stop=(s == NS - 1 and j == 7))
            return ins
        P.op('pe', fnk, reads=[('QS', i), ('KS', i)], writes=[('ps', 4)])
    chk('m6')
    smx = st4[0:64, 0:1]; snm = st4[0:64, 1:2]; ssm = st4[0:64, 2:3]; srs = st4[0:64, 3:4]
    P.op('dve', lambda e: e.tensor_reduce(out=smx, in_=lgs, axis=AX.X, op=ALU.max), reads=[('ps', 4)], writes=['st4'])
    P.op('dve', lambda e: e.tensor_scalar(out=snm, in0=smx, scalar1=-1.0, scalar2=None, op0=ALU.mult), reads=['st4'], writes=['st4'])
    P.op('act', lambda e: e.activation(out=Pf[0:64, 0:256], in_=lgs, func=AF.Exp, bias=snm, accum_out=ssm), reads=[('ps', 4), 'st4'], writes=['Pf', 'ssm'])
    P.op('dve', lambda e: e.reciprocal(out=srs, in_=ssm), reads=['ssm', 'st4'], writes=['st4'])
    P.op('dve', lambda e: e.tensor_scalar(out=Pn[0:64, 0:256], in0=Pf[0:64, 0:256], scalar1=srs, scalar2=None, op0=ALU.mult), reads=['Pf', 'st4'], writes=['Pn'])
    chk('m7')
    ptsv = bankbf(6)[:, 0:128].rearrange("p (a b) -> p a b", a=2)

    def fnt(pe):
        ins = None
        for mc in range(2):
            ins = pe.transpose(ptsv[:, mc, :], Pn[0:64, mc * 128:(mc + 1) * 128], identb[0:64, 0:64])
        return ins
    P.op('pe', fnt, reads=['Pn', 'cstb'], writes=[('ps', 6)])
    PTs3 = r3(PTs, 2)
    P.op('dve', lambda e: e.tensor_copy(out=PTs3, in_=ptsv), reads=[('ps', 6)], writes=['PTs'])
    ovs = bank(5)[:, 0:128].rearrange("p (a b) -> p a b", a=8)
    for s in range(NS):
        i = s % 2
        P.dma('pool', (lambda q, s=s, i=i: q.dma_start(out=VS[i], in_=vc_in[s].rearrange("(mc p) d -> p mc d", p=128))), writes=[('KS', i)])

        def fnv(pe, s=s, i=i):
            ins = None
            for j in range(8):
                for mc in range(2):
                    col = s * 4 + j // 2
                    ins = pe.matmul(ovs[:, j, s:s + 1], lhsT=VS[i][:, mc, j * 128:(j + 1) * 128], rhs=PTs3[:, mc, col:col + 1],
                                    start=(mc == 0), stop=(mc == 1))
            return ins
        P.op('pe', fnv, reads=['PTs', ('KS', i)], writes=[('ps', 5)])
    P.op('dve', lambda e: e.tensor_copy(out=omT[:, :, 0:NS], in_=ovs), reads=[('ps', 5)], writes=['omT'])
    P.op('dve', lambda e: e.memset(omT[:, :, 16:18], 0.0), writes=['omT'])
    P.barrier(pool=False)

    if stop == 'A3':
        return finish()
    SC.reset()
    G = r3(SC.f32(4 * NT), 4); Mg = r3(SC.f32(4 * NT), 4); mt_ = SC.f32(NT)
    rog = lambda kc, c0, c1: ogT[:, kc, c0:c1]
    rbu = lambda kc, c0, c1: buT[:, kc, c0:c1]
    rom = lambda kc, c0, c1: omT[:, kc, c0:c1]
    for blk in range(4):
        cs = slice(blk * 512, (blk + 1) * 512)
        for bi, (zname, wsrc, nk, rfn, rkey) in enumerate((('za', w_gla_out, 16, rog, 'ogT'), ('zb', w_conv_out, 8, rbu, 'buT'), ('zm', w_mem_out, 8, rom, 'omT'))):
            wz, kz = load_w([(w_in[:, OFF[zname] + blk * 512:OFF[zname] + (blk + 1) * 512], 0)], KC, 512)
            for ec in range(4):
                banks, keys = mmF(wz, kz, ec * 128, 128, rx, ['xT'], KC, GROUPS)
                evac('act', GROUPS, banks, keys, 128, lambda c0, c1, ec=ec: G[:, ec, c0:c1], [('G', ec)], func=AF.Sigmoid)
            wy, ky = load_w([(wsrc[:, cs], 0)], nk, 512)
            for ec in range(4):
                banks, keys = mmF(wy, ky, ec * 128, 128, rfn, [rkey] if rkey != 'buT' else [('buT', j) for j in range(8)], nk, GROUPS)
                for gi, (c0, c1) in enumerate(GROUPS):
                    src = bank(banks[gi])[:, 0:c1 - c0]
                    if bi == 0:
                        P.op('dve', (lambda e, ec=ec, c0=c0, c1=c1, src=src: e.tensor_tensor(out=Mg[:, ec, c0:c1], in0=src, in1=G[:, ec, c0:c1], op=ALU.mult)),
                             reads=[keys[gi], ('G', ec)], writes=[('Mg', ec)])
                    else:
                        P.op('dve', (lambda e, ec=ec, c0=c0, c1=c1, src=src: e.tensor_tensor(out=mt_[:, c0:c1], in0=src, in1=G[:, ec, c0:c1], op=ALU.mult)),
                             reads=[keys[gi], ('G', ec)], writes=['mt_'])
                        dst = Mg[:, ec, c0:c1] if bi == 1 else mergedT[:, blk * 4 + ec, c0:c1]
                        P.op('dve', (lambda e, ec=ec, c0=c0, c1=c1, dst=dst: e.tensor_tensor(out=dst, in0=Mg[:, ec, c0:c1], in1=mt_[:, c0:c1], op=ALU.add)),
                             reads=['mt_', ('Mg', ec)], writes=[('Mg', ec), 'mergedT'])
    P.barrier(pool=False)
    if stop == 'A5':
        return finish()
    A2 = Arena(OG_OFF, NW)
    r1 = A2.f32(9 * D).rearrange("p (t d) -> p t d", t=9)
    lng = A2.f32(D); lnb = A2.f32(D)
    xres = [A2.f32(512) for _ in range(2)]
    x1b = A2.bf(D)
    bst = A2.f32(4 * 6); mvv = A2.f32(4)
    x1T = xT
    P.dma('sp', lambda q, lng=lng: q.dma_start(out=lng, in_=ln_in[0]), writes=['lng'])
    P.dma('sp', lambda q, lnb=lnb: q.dma_start(out=lnb, in_=ln_in[1]), writes=['lnb'])

    def layer_norm_tile(ti, n, src, keyr, gkey, bkey, lng, lnb, bst, mvv):
        for c in range(4):
            P.op('dve', (lambda e, c=c: e.bn_stats(out=bst[0:n, c * 6:(c + 1) * 6], in_=src[0:n, c * 512:(c + 1) * 512])), reads=[keyr], writes=['bst'])
        P.op('dve', lambda e: e.bn_aggr(out=mvv[0:n, 0:2], in_=bst[0:n, :]), reads=['bst'], writes=['mvv'])
        P.op('act', lambda e: e.activation(out=mvv[0:n, 2:3], in_=mvv[0:n, 1:2], func=AF.Ln, bias=1e-5), reads=['mvv'], writes=['mvv2'])
        P.op('act', lambda e: e.activation(out=mvv[0:n, 3:4], in_=mvv[0:n, 2:3], func=AF.Exp, scale=-0.5), reads=['mvv2'], writes=['mvv3'])
        P.op('dve', lambda e: e.tensor_scalar(out=src[0:n, :], in0=src[0:n, :], scalar1=mvv[0:n, 0:1], scalar2=mvv[0:n, 3:4], op0=ALU.subtract, op1=ALU.mult),
             reads=[keyr, 'mvv', 'mvv3'], writes=[keyr])
        P.op('dve', lambda e: e.tensor_tensor(out=src[0:n, :], in0=src[0:n, :], in1=lng[0:n, :], op=ALU.mult), reads=[keyr, gkey], writes=[keyr])
        P.op('dve', lambda e: e.tensor_tensor(out=src[0:n, :], in0=src[0:n, :], in1=lnb[0:n, :], op=ALU.add), reads=[keyr, bkey], writes=[keyr])

    xi = [0]
    for blk in range(4):
        wo, ko = load_w([(w_o[:, blk * 512:(blk + 1) * 512], 0)], KC, 512)
        for ti, (c0, n) in enumerate(TILES):
            b = nbank()

            def fn(pe, b=b, c0=c0, n=n, wo=wo):
                ins = None
                for kc in range(KC):
                    ins = pe.matmul(bank(b)[0:n, :], lhsT=mergedT[:, kc, c0:c0 + n], rhs=wo[:, kc, :], start=(kc == 0), stop=(kc == KC - 1))
                return ins
            P.op('pe', fn, reads=[ko, 'mergedT'], writes=[('ps', b)])
            xr = xi[0] % 2; xi[0] += 1
            P.dma('sp', (lambda q, xr=xr, c0=c0, n=n, blk=blk: q.dma_start(out=xres[xr][0:n, :], in_=x_tok[c0:c0 + n, blk * 512:(blk + 1) * 512])), writes=[('xres', xr)])
            P.op('dve', (lambda e, b=b, n=n, ti=ti, blk=blk, xr=xr: e.scalar_tensor_tensor(out=r1[0:n, ti, blk * 512:(blk + 1) * 512], in0=xres[xr][0:n, :], scalar=ALPHA,
                                                                                         in1=bank(b)[0:n, :], op0=ALU.mult, op1=ALU.add)),
                 reads=[('ps', b), ('xres', xr)], writes=[('r1', ti)])
    for ti, (c0, n) in enumerate(TILES):
        layer_norm_tile(ti, n, r1[:, ti, :], ('r1', ti), 'lng', 'lnb', lng, lnb, bst, mvv)
        P.dma('sp', (lambda q, ti=ti, c0=c0, n=n: q.dma_start(out=x1_scr[c0:c0 + n, :], in_=r1[0:n, ti, :])), reads=[('r1', ti)], writes=[('x1scr', ti)])
        P.op('act', (lambda e, ti=ti, n=n: e.activation(out=x1b[0:n, :], in_=r1[0:n, ti, :], func=AF.Copy)), reads=[('r1', ti)], writes=['x1b'])
        tv = bankbf(6, 2)[:, 0:KC * 128].rearrange("p (a b) -> p a b", a=KC)

        def fn(pe, n=n):
            ins = None
            for kc in range(KC):
                ins = pe.transpose(tv[:, kc, 0:n], x1b[0:n, kc * 128:(kc + 1) * 128], identb[0:n, 0:n])
            return ins
        P.op('pe', fn, reads=['x1b', 'cstb'], writes=[('ps', 6), ('ps', 7)])
        P.op('dve', (lambda e, c0=c0, n=n: e.tensor_copy(out=x1T[:, :, c0:c0 + n], in_=tv[:, :, 0:n])), reads=[('ps', 6), ('ps', 7)], writes=['x1T'])
    P.barrier(pool=False)

    if stop == 'LN1':
        return finish()
    A3 = Arena(WB_OFF, NW)
    WG = [A3.bf(KC * 512) for _ in range(2)]
    hT = r3(A3.bf(FC * NT), FC)
    HT_END = A3.off
    AM_ = Arena(1500 + KC * NT // 2, WB_OFF)
    hgx = AM_.f32(NT); hc = AM_.f32(NT); hs = AM_.f32(NT)
    sfT = AM_.f32(FC * 2 * NS).rearrange("p (j r s) -> p j r s", j=FC, r=2)
    pfc = r3(AM_.f32(FC * 2), FC); sfc = r3(AM_.f32(FC * NS), FC)
    sfv = sfT_in.rearrange("(j p) r s -> p j r s", p=128)
    for j0 in range(0, FC, 4):
        j1 = min(FC, j0 + 4)
        P.dma('sp', (lambda q, j0=j0, j1=j1: q.dma_start(out=sfT[:, j0:j1], in_=sfv[:, j0:j1])), writes=['sfT'])
    for r0 in range(0, DFF, 1376):
        P.dma('sp', (lambda q, r0=r0: q.dma_start(out=sffn_out[r0:r0 + 1376, 0, :], in_=sfT_in[r0:r0 + 1376, 1, :])))
    P.op('dve', lambda e: e.memset(hs[:, 16:20], 0.0), writes=['hs'])
    rx1 = lambda kc, c0, c1: x1T[:, kc, c0:c1]
    wg_i = [0]

    def load_wg(src):
        i = wg_i[0]; wg_i[0] ^= 1
        w = src.shape[1]
        buf = WG[i][:, 0:KC * w].rearrange("p (k e) -> p k e", k=KC)
        sv = src.rearrange("(k p) e -> p k e", p=128)
        for (k0, k1) in ((0, 8), (8, 16)):
            P.dma('pool', (lambda q, o=buf[:, k0:k1, :], s=sv[:, k0:k1, :]: q.dma_start(out=o, in_=s)), writes=[('W', i)])
        return buf, ('W', i)
    for fb in range(11):
        f0 = fb * 512; fw = min(512, DFF - f0)
        wg, kg = load_wg(w_gate[:, f0:f0 + fw])
        wu, ku = load_wg(w_up[:, f0:f0 + fw])
        for ec in range(fw // 128):
            j = fb * 4 + ec
            w0 = wfc[:, j * 4 + 0:j * 4 + 1]; w1 = wfc[:, j * 4 + 1:j * 4 + 2]; w2 = wfc[:, j * 4 + 2:j * 4 + 3]; bb = wfc[:, j * 4 + 3:j * 4 + 4]
            banks, keys = mmF(wg, kg, ec * 128, 128, rx1, ['x1T'], KC, GROUPS)
            evac('act', GROUPS, banks, keys, 128, lambda c0, c1: hgx[:, c0:c1], ['hgx'])
            P.op('dve', lambda e: e.tensor_scalar(out=hgx[:, 18:20], in0=hgx[:, 18:20], scalar1=flag, scalar2=None, op0=ALU.mult), reads=['hgx', 'small'], writes=['hgx'])
            P.op('dve', (lambda e, w0=w0, bb=bb: e.tensor_scalar(out=hc[:, M0:NT], in0=hgx[:, M0 - 2:NT - 2], scalar1=w0, scalar2=bb, op0=ALU.mult, op1=ALU.add)),
                 reads=['hgx', 'small'], writes=['hc'])
            P.op('dve', (lambda e, w1=w1: e.scalar_tensor_tensor(out=hc[:, M0:NT], in0=hgx[:, M0 - 1:NT - 1], scalar=w1, in1=hc[:, M0:NT], op0=ALU.mult, op1=ALU.add)),
                 reads=['hgx', 'hc'], writes=['hc'])
            P.op('dve', (lambda e, w2=w2: e.scalar_tensor_tensor(out=hc[:, M0:NT], in0=hgx[:, M0:NT], scalar=w2, in1=hc[:, M0:NT], op0=ALU.mult, op1=ALU.add)),
                 reads=['hgx', 'hc'], writes=['hc'])
            P.op('dve', (lambda e, w0=w0, bb=bb, j=j: e.tensor_scalar(out=hc[:, 0:NS], in0=sfT[:, j, 0, :], scalar1=w0, scalar2=bb, op0=ALU.mult, op1=ALU.add)),
                 reads=['sfT', 'small', 'hc'], writes=['hc'])
            P.op('dve', (lambda e, w1=w1, j=j: e.scalar_tensor_tensor(out=hc[:, 0:NS], in0=sfT[:, j, 1, :], scalar=w1, in1=hc[:, 0:NS], op0=ALU.mult, op1=ALU.add)),
                 reads=['sfT', 'hc'], writes=['hc'])
            P.op('dve', (lambda e, w2=w2: e.scalar_tensor_tensor(out=hc[:, 0:NS], in0=hgx[:, 0:NS], scalar=w2, in1=hc[:, 0:NS], op0=ALU.mult, op1=ALU.add)),
                 reads=['hgx', 'hc'], writes=['hc'])
            P.op('dve', (lambda e, j=j: e.tensor_copy(out=pfc[:, j, :], in_=hgx[:, NT - 2:NT])), reads=['hgx'], writes=['pfc'])
            P.op('dve', (lambda e, j=j: e.tensor_copy(out=sfc[:, j, :], in_=hgx[:, 0:NS])), reads=['hgx'], writes=['sfc'])
            P.op('act', lambda e: e.activation(out=hs[:, M0:NT], in_=hc[:, M0:NT], func=AF.Silu), reads=['hc'], writes=['hs'])
            P.op('act', lambda e: e.activation(out=hs[:, 0:NS], in_=hc[:, 0:NS], func=AF.Silu), reads=['hc'], writes=['hs'])
            banks, keys = mmF(wu, ku, ec * 128, 128, rx1, ['x1T'], KC, GROUPS)
            for gi, (c0, c1) in enumerate(GROUPS):
                P.op('dve', (lambda e, j=j, c0=c0, c1=c1, b=banks[gi]: e.tensor_tensor(out=hT[:, j, c0:c1], in0=bank(b)[:, 0:c1 - c0], in1=hs[:, c0:c1], op=ALU.mult)),
                     reads=[keys[gi], 'hs'], writes=['hT'])
    pfv = pffn_out.rearrange("(j p) r -> p j r", p=128)
    sfov = sffn_out[:, 1, :].rearrange("(j p) s -> p j s", p=128)
    for j0 in range(0, FC, 8):
        j1 = min(FC, j0 + 8)
        P.dma('sp', (lambda q, j0=j0, j1=j1: q.dma_start(out=pfv[:, j0:j1], in_=pfc[:, j0:j1])), reads=['pfc'])
        P.dma('sp', (lambda q, j0=j0, j1=j1: q.dma_start(out=sfov[:, j0:j1], in_=sfc[:, j0:j1])), reads=['sfc'])
    P.barrier()
    if stop == 'B1':
        return finish()
    AX_ = Arena(0, WB_OFF + 2 * (KC * 512 // 2))
    r2 = [AX_.f32(D) for _ in range(7)]
    WD0_OFF = AX_.off
    WD = [r3(AX_.bf(FC * 256), FC) for _ in range(2)]
    xres2 = [AX_.f32(256) for _ in range(2)]
    bst = AX_.f32(24); mvv = AX_.f32(4)
    AY_ = Arena(HT_END, NW)
    r2 += [AY_.f32(D) for _ in range(2)]
    AL_ = Arena(WD0_OFF, WD0_OFF + 2 * D)
    lng = AL_.f32(D); lnb = AL_.f32(D)
    wdv = w_down.rearrange("(k p) e -> p k e", p=128)
    for blk in range(8):
        i = blk % 2
        for (k0, k1) in ((0, 22), (22, FC)):
            P.dma('pool', (lambda q, i=i, k0=k0, k1=k1, blk=blk: q.dma_start(out=WD[i][:, k0:k1, :], in_=wdv[:, k0:k1, blk * 256:(blk + 1) * 256])), writes=[('WD', i)])
        for ti, (c0, n) in enumerate(TILES):
            b = nbank()

            def fn(pe, b=b, c0=c0, n=n, i=i):
                ins = None
                for kc in range(FC):
                    ins = pe.matmul(bank(b)[0:n, 0:256], lhsT=hT[:, kc, c0:c0 + n], rhs=WD[i][:, kc, :], start=(kc == 0), stop=(kc == FC - 1))
                return ins
            P.op('pe', fn, reads=[('WD', i), 'hT'], writes=[('ps', b)])
            xr = xi[0] % 2; xi[0] += 1
            P.dma('sp', (lambda q, xr=xr, c0=c0, n=n, blk=blk: q.dma_start(out=xres2[xr][0:n, :], in_=x1_scr[c0:c0 + n, blk * 256:(blk + 1) * 256])),
                  reads=[('x1scr', ti)], writes=[('xres2', xr)])
            P.op('dve', (lambda e, b=b, n=n, ti=ti, blk=blk, xr=xr: e.scalar_tensor_tensor(out=r2[ti][0:n, blk * 256:(blk + 1) * 256], in0=xres2[xr][0:n, :], scalar=ALPHA,
                                                                                         in1=bank(b)[0:n, 0:256], op0=ALU.mult, op1=ALU.add)),
                 reads=[('ps', b), ('xres2', xr)], writes=[('r2', ti)])
    P.barrier()
    P.dma('sp', lambda q, lng=lng: q.dma_start(out=lng, in_=ln_in[2]), writes=['lng2'])
    P.dma('sp', lambda q, lnb=lnb: q.dma_start(out=lnb, in_=ln_in[3]), writes=['lnb2'])
    for ti, (c0, n) in enumerate(TILES):
        layer_norm_tile(ti, n, r2[ti], ('r2', ti), 'lng2', 'lnb2', lng, lnb, bst, mvv)
        P.dma('sp', (lambda q, ti=ti, c0=c0, n=n: q.dma_start(out=y_out[c0:c0 + n, :], in_=r2[ti][0:n, :])), reads=[('r2', ti)])
    return finish()


_NC_CACHE = {}


def _consts():
    c = np.zeros((128, 512), np.float32)
    c[:, 0:128] = np.eye(128, dtype=np.float32)
    m = np.triu(np.ones((128, 128), np.float32))
    c[:, 128:256] = m
    c[:, 256:384] = m * (-1.0 / 16.0)
    return c


def kernel(x_prompt, x_sample, mem_prompt, cache_mem_k, cache_mem_v, state_gla, state_conv, state_ffn_conv,
           w_in, w_gla_a2, b_gla_a2, g_gla_norm, w_gla_out, w_conv, w_conv_out, w_mem_k, w_mem_v, w_mem_out,
           w_o, ln1_g, ln1_b, w_ffn_gate, w_ffn_up, w_ffn_conv, b_ffn_conv, w_ffn_down, ln2_g, ln2_b):
    f = lambda a: np.ascontiguousarray(np.asarray(a, dtype=np.float32))
    x_prompt = f(x_prompt); x_sample = f(x_sample); mem_prompt = f(mem_prompt)
    cache_mem_k = f(cache_mem_k); cache_mem_v = f(cache_mem_v); state_gla = f(state_gla)
    state_conv = f(state_conv); state_ffn_conv = f(state_ffn_conv)
    if 'nc' not in _NC_CACHE:
        _NC_CACHE['nc'] = build_program()
    nc = _NC_CACHE['nc']
    shared = {
        "consts": _consts(), "ones_row": np.ones((1, NT), np.float32),
        "w_in": f(w_in[0]), "wa2b": f(np.concatenate([np.asarray(w_gla_a2[0]), np.asarray(b_gla_a2[0])[None, :]], axis=0)),
        "gnorm_b": f(np.broadcast_to(np.asarray(g_gla_norm[0])[None, :], (128, 512))),
        "w_gla_out": f(w_gla_out[0]),
        "wconv_p": f(np.asarray(w_conv[0]).reshape(3, 8, 128).transpose(2, 1, 0).reshape(128, 24)),
        "w_conv_out": f(w_conv_out[0]), "w_mem_k": f(w_mem_k[0]), "w_mem_v": f(w_mem_v[0]), "w_mem_out": f(w_mem_out[0]),
        "w_o": f(w_o[0]),
        "ln_b": f(np.stack([np.broadcast_to(np.asarray(v[0])[None, :], (128, D)) for v in (ln1_g, ln1_b, ln2_g, ln2_b)])),
        "w_ffn_gate": f(w_ffn_gate[0]), "w_ffn_up": f(w_ffn_up[0]),
        "wfc_p": f(np.concatenate([np.asarray(w_ffn_conv[0]), np.asarray(b_ffn_conv[0])[None, :]], axis=0).reshape(4, FC, 128).transpose(2, 1, 0).reshape(128, FC * 4)),
        "w_ffn_down": f(w_ffn_down[0]),
    }
    in_maps = []
    for c in range(8):
        b, h = c // 2, c % 2
        T0 = 1024 * h
        toks = np.zeros((NT, D), np.float32)
        toks[0:NS] = x_sample[16 * c:16 * c + 16, 0]
        if h == 1:
            toks[H0:M0] = x_prompt[b, T0 - 4:T0]
        toks[M0:] = x_prompt[b, T0:T0 + 1024]
        xp = np.zeros((NP, D), np.float32)
        if h == 1:
            xp[2:] = x_prompt[b, 0:1022]
        m = dict(shared)
        m.update({
            "xT_in": f(toks.T), "xpT_in": f(xp.T), "x_tok": toks,
            "flag": np.full((128, 1), float(h), np.float32),
            "memT_in": f(mem_prompt[b].T),
            "kcT_in": f(cache_mem_k[0, 16 * c:16 * c + 16].reshape(16, 256, 1024).transpose(0, 2, 1)),
            "vc_in": f(cache_mem_v[0, 16 * c:16 * c + 16].reshape(16, 256, 1024)),
            "sgla_in": f(state_gla[0, 16 * c:16 * c + 16]),
            "scT_in": f(state_conv[0, 16 * c:16 * c + 16].transpose(2, 1, 0)),
            "sfT_in": f(state_ffn_conv[0, 16 * c:16 * c + 16].transpose(2, 1, 0)),
        })
        in_maps.append(m)
    res = run_bass_kernel_spmd(nc, in_maps, core_ids=list(range(8)))
    R = res.results
    yp = np.zeros((4, 2048, D), np.float32); ys = np.zeros((128, 1, D), np.float32)
    pmk = np.zeros((1, 4, 256, 4, 256), np.float32); pmv = np.zeros_like(pmk)
    pgla = np.zeros((1, 4, 4, 256, 512), np.float32); pconv = np.zeros((1, 4, 2, 1024), np.float32); pffn = np.zeros((1, 4, 2, DFF), np.float32)
    sgla = np.zeros((1, 128, 4, 256, 512), np.float32); sconv = np.zeros((1, 128, 2, 1024), np.float32); sffn = np.zeros((1, 128, 2, DFF), np.float32)
    for c in range(8):
        b, h = c // 2, c % 2
        r = R[c]
        yp[b, 1024 * h:1024 * h + 1024] = r["y_out"][M0:]
        ys[16 * c:16 * c + 16, 0] = r["y_out"][0:NS]
        sgla[0, 16 * c:16 * c + 16] = r["sgla_out"]
        sconv[0, 16 * c:16 * c + 16] = r["sconvT"].transpose(2, 1, 0)
        sffn[0, 16 * c:16 * c + 16] = r["sffnT"].transpose(2, 1, 0)
        if h == 0:
            pmk[0, b] = r["pmk"].reshape(256, 4, 256); pmv[0, b] = r["pmv"].reshape(256, 4, 256)
        else:
            pgla[0, b] = r["pgla"]; pconv[0, b] = r["pconvT"].T; pffn[0, b] = r["pffnT"].T
    return (yp, ys, pmk, pmv, pgla, pconv, pffn, sgla, sconv, sffn)
```
